# Optimizing a Trainium2 kernel written in Bass

```python
import math
import jax, jax.numpy as jnp
from jax import lax
import numpy as np

D_MODEL = 2048
BATCH = 4
SEQ = 4096
DEPTH = 2

N_META = 16
LRU_WIDTH = D_MODEL // 2
LRU_BLOCKS = 8
LRU_BLOCK_W = LRU_WIDTH // LRU_BLOCKS
CONV_W = 4
LRU_C = 8.0
DA_HEADS = 8
DA_HEAD_DIM = 64
DA_V_DIM = 2 * DA_HEAD_DIM
DA_WIDTH = DA_HEADS * DA_V_DIM
QK_WIDTH = DA_HEADS * 2 * DA_HEAD_DIM
Q_BLOCK = 128
S5_WIDTH = D_MODEL // 2
S5_GROUP = 16
S5_GROUPS = S5_WIDTH // S5_GROUP
S5_STATE = 64
S5_DT_MIN = 1e-3
S5_DT_MAX = 1e-1
REL_BUCKETS = 32
REL_MAX_DIST = 128
N_BRANCH = 3
IN_SIZES = (LRU_WIDTH, LRU_WIDTH, QK_WIDTH, QK_WIDTH, DA_WIDTH, S5_WIDTH, N_BRANCH * D_MODEL)
N_IN = sum(IN_SIZES)
IN_SPLITS = tuple(int(v) for v in np.cumsum(IN_SIZES)[:-1])
D_FF = ((-(-(8 * D_MODEL) // 3) + 255) // 256) * 256

kernel_name = "hybrid_rglru_diffattn_s5_block"


def rms_norm(x, w, eps=1e-6):
    xf = x.astype(jnp.float32)
    xf = xf * lax.rsqrt(jnp.mean(xf * xf, axis=-1, keepdims=True) + eps)
    return (xf * w.astype(jnp.float32)).astype(x.dtype)


def _lin_combine(e1, e2):
    a1, b1 = e1
    a2, b2 = e2
    return a1 * a2, a2 * b1 + b2


def _complex_lin_combine(e1, e2):
    ar1, ai1, br1, bi1 = e1
    ar2, ai2, br2, bi2 = e2
    ar = ar2 * ar1 - ai2 * ai1
    ai = ar2 * ai1 + ai2 * ar1
    br = ar2 * br1 - ai2 * bi1 + br2
    bi = ar2 * bi1 + ai2 * br1 + bi2
    return ar, ai, br, bi


def causal_conv(x, w, b):
    y = lax.conv_general_dilated(
        x, w[:, None, :].astype(x.dtype), window_strides=(1,), padding=[(CONV_W - 1, 0)],
        dimension_numbers=("NWC", "WIO", "NWC"), feature_group_count=x.shape[-1])
    return y + b


def rg_lru_branch(gate_in, x_in, conv_w, conv_b, w_a, b_a, w_x, b_x, lam):
    B, T, _ = x_in.shape
    xc = causal_conv(x_in, conv_w, conv_b)
    xb = xc.reshape(B, T, LRU_BLOCKS, LRU_BLOCK_W)
    r = jax.nn.sigmoid(jnp.einsum("bthi,hij->bthj", xb, w_a).reshape(B, T, LRU_WIDTH) + b_a)
    i = jax.nn.sigmoid(jnp.einsum("bthi,hij->bthj", xb, w_x).reshape(B, T, LRU_WIDTH) + b_x)
    log_a = LRU_C * r.astype(jnp.float32) * jax.nn.log_sigmoid(lam.astype(jnp.float32))
    a = jnp.exp(log_a)
    b = jnp.sqrt(-jnp.expm1(2.0 * log_a)) * (i * xc).astype(jnp.float32)
    _, h = lax.associative_scan(_lin_combine, (a, b), axis=1)
    return h.astype(x_in.dtype) * jax.nn.gelu(gate_in)


def t5_bucket(q_pos, k_pos):
    n = jnp.maximum(q_pos[:, None] - k_pos[None, :], 0)
    max_exact = REL_BUCKETS // 2
    nf = jnp.maximum(n, 1).astype(jnp.float32)
    large = max_exact + (jnp.log(nf / max_exact) / math.log(REL_MAX_DIST / max_exact)
                         * (REL_BUCKETS - max_exact)).astype(jnp.int32)
    large = jnp.minimum(large, REL_BUCKETS - 1)
    return jnp.where(n < max_exact, n, large)


def diff_attention(q, k, v, rel_bias, lam, sub_w, lam_init):
    B, T = q.shape[0], q.shape[1]
    k_pos = jnp.arange(T)
    scale = DA_HEAD_DIM ** -0.5

    def block(q_blk, q_pos):
        s = jnp.einsum("bqhcd,bkhcd->bhcqk", q_blk, k).astype(jnp.float32) * scale
        bias = rel_bias[t5_bucket(q_pos, k_pos)].astype(jnp.float32)
        s = s + jnp.transpose(bias, (2, 0, 1))[None, :, None]
        s = jnp.where(k_pos[None, :] <= q_pos[:, None], s, -jnp.inf)
        p = jax.nn.softmax(s, axis=-1)
        w = p[:, :, 0] - lam * p[:, :, 1]
        return jnp.einsum("bhqk,bkhd->bqhd", w.astype(v.dtype), v)

    out_meta = block(q[:, :N_META], jnp.arange(N_META))
    nb = (T - N_META) // Q_BLOCK
    qb = q[:, N_META:].reshape(B, nb, Q_BLOCK, DA_HEADS, 2, DA_HEAD_DIM).transpose(1, 0, 2, 3, 4, 5)
    pb = (N_META + jnp.arange(nb * Q_BLOCK)).reshape(nb, Q_BLOCK)
    out_real = lax.map(lambda a: block(a[0], a[1]), (qb, pb))
    out_real = out_real.transpose(1, 0, 2, 3, 4).reshape(B, nb * Q_BLOCK, DA_HEADS, DA_V_DIM)
    o = jnp.concatenate([out_meta, out_real], axis=1)
    o = rms_norm(o, sub_w, eps=1e-5) * (1.0 - lam_init)
    return o.reshape(B, T, DA_WIDTH)


def s5_branch(u, lam_re, lam_im, b_re, b_im, c_re, c_im, d, log_step, w_glu, b_glu):
    B, T, _ = u.shape
    f32 = jnp.float32
    uf = u.astype(f32).reshape(B, T, S5_GROUPS, S5_GROUP)
    lr, li = lam_re.astype(f32), lam_im.astype(f32)
    step = jnp.exp(log_step.astype(f32))[:, None]
    mag = jnp.exp(lr * step)
    ab_re, ab_im = mag * jnp.cos(li * step), mag * jnp.sin(li * step)
    den = lr * lr + li * li
    coef_re = ((ab_re - 1.0) * lr + ab_im * li) / den
    coef_im = (ab_im * lr - (ab_re - 1.0) * li) / den
    br, bi = b_re.astype(f32), b_im.astype(f32)
    bb_re = coef_re[..., None] * br - coef_im[..., None] * bi
    bb_im = coef_re[..., None] * bi + coef_im[..., None] * br
    bu_re = jnp.einsum("gpc,btgc->btgp", bb_re, uf)
    bu_im = jnp.einsum("gpc,btgc->btgp", bb_im, uf)
    a_re = jnp.broadcast_to(ab_re[None, None], (1, T, S5_GROUPS, S5_STATE))
    a_im = jnp.broadcast_to(ab_im[None, None], (1, T, S5_GROUPS, S5_STATE))
    _, _, h_re, h_im = lax.associative_scan(_complex_lin_combine, (a_re, a_im, bu_re, bu_im), axis=1)
    y = (jnp.einsum("gcp,btgp->btgc", c_re.astype(f32), h_re)
         - jnp.einsum("gcp,btgp->btgc", c_im.astype(f32), h_im))
    y = y.reshape(B, T, S5_WIDTH) + d.astype(f32) * u.astype(f32)
    y = jax.nn.gelu(y).astype(u.dtype)
    return y * jax.nn.sigmoid(y @ w_glu + b_glu)


def hybrid_mixer(h, lam_init, rel_bias, w_in, conv_w, conv_b, lru_w_a, lru_b_a, lru_w_x, lru_b_x,
                 lru_lambda, da_lambda, da_subln, s5_lam_re, s5_lam_im, s5_b_re, s5_b_im,
                 s5_c_re, s5_c_im, s5_d, s5_log_step, s5_w_glu, s5_b_glu, b_gate, w_branch, w_out):
    B, T, _ = h.shape
    proj = h @ w_in
    a_gate, a_x, q, k, v, s5_u, g = jnp.split(proj, IN_SPLITS, axis=-1)
    y_a = rg_lru_branch(a_gate, a_x, conv_w, conv_b, lru_w_a, lru_b_a, lru_w_x, lru_b_x, lru_lambda)
    lf = da_lambda.astype(jnp.float32)
    lam = jnp.exp(jnp.sum(lf[0] * lf[1])) - jnp.exp(jnp.sum(lf[2] * lf[3])) + lam_init
    y_b = diff_attention(q.reshape(B, T, DA_HEADS, 2, DA_HEAD_DIM),
                         k.reshape(B, T, DA_HEADS, 2, DA_HEAD_DIM),
                         v.reshape(B, T, DA_HEADS, DA_V_DIM), rel_bias, lam, da_subln, lam_init)
    y_c = s5_branch(s5_u, s5_lam_re, s5_lam_im, s5_b_re, s5_b_im, s5_c_re, s5_c_im, s5_d,
                    s5_log_step, s5_w_glu, s5_b_glu)
    gates = jax.nn.sigmoid(g.reshape(B, T, N_BRANCH, D_MODEL) + b_gate)
    merged = (gates[:, :, 0] * (y_a @ w_branch[0])
              + gates[:, :, 1] * (y_b @ w_branch[1])
              + gates[:, :, 2] * (y_c @ w_branch[2]))
    return merged @ w_out


def swiglu(h, w_ffn_in, w_ffn_out):
    gu = h @ w_ffn_in
    gate, up = gu[..., :D_FF], gu[..., D_FF:]
    return (jax.nn.silu(gate) * up) @ w_ffn_out


def setup_inputs(seed: int = 0) -> dict:
    key = jax.random.key(seed)
    ks = jax.random.split(key, 32)
    f32 = jnp.float32

    def nrm(k, shape, scale):
        return scale * jax.random.normal(k, shape, f32)

    x = nrm(ks[0], (BATCH, SEQ, D_MODEL), 1.0)
    meta = nrm(ks[1], (N_META, D_MODEL), 1.0)
    rel_bias = nrm(ks[2], (REL_BUCKETS, DA_HEADS), 0.5)
    norm_w = 1.0 + nrm(ks[3], (DEPTH, 4, D_MODEL), 0.05)
    w_in = nrm(ks[4], (DEPTH, D_MODEL, N_IN), D_MODEL ** -0.5)
    conv_w = nrm(ks[5], (DEPTH, CONV_W, LRU_WIDTH), CONV_W ** -0.5)
    conv_b = nrm(ks[6], (DEPTH, LRU_WIDTH), 0.02)
    lru_w_a = nrm(ks[7], (DEPTH, LRU_BLOCKS, LRU_BLOCK_W, LRU_BLOCK_W), LRU_BLOCK_W ** -0.5)
    lru_b_a = nrm(ks[8], (DEPTH, LRU_WIDTH), 0.02)
    lru_w_x = nrm(ks[9], (DEPTH, LRU_BLOCKS, LRU_BLOCK_W, LRU_BLOCK_W), LRU_BLOCK_W ** -0.5)
    lru_b_x = nrm(ks[10], (DEPTH, LRU_WIDTH), 0.02)
    a8 = jax.random.uniform(ks[11], (DEPTH, LRU_WIDTH), f32, 0.9, 0.999)
    s = a8 ** (1.0 / LRU_C)
    lru_lambda = jnp.log(s) - jnp.log1p(-s)
    da_lambda = nrm(ks[12], (DEPTH, 4, DA_HEAD_DIM), 0.1)
    da_subln = 1.0 + nrm(ks[13], (DEPTH, DA_V_DIM), 0.05)
    s5_lam_re = -0.5 + nrm(ks[14], (DEPTH, S5_GROUPS, S5_STATE), 0.01)
    s5_lam_im = jnp.broadcast_to(math.pi * jnp.arange(S5_STATE, dtype=f32), (DEPTH, S5_GROUPS, S5_STATE))
    s5_b_re = nrm(ks[15], (DEPTH, S5_GROUPS, S5_STATE, S5_GROUP), (2 * S5_GROUP) ** -0.5)
    s5_b_im = nrm(ks[16], (DEPTH, S5_GROUPS, S5_STATE, S5_GROUP), (2 * S5_GROUP) ** -0.5)
    s5_c_re = nrm(ks[17], (DEPTH, S5_GROUPS, S5_GROUP, S5_STATE), S5_STATE ** -0.5)
    s5_c_im = nrm(ks[18], (DEPTH, S5_GROUPS, S5_GROUP, S5_STATE), S5_STATE ** -0.5)
    s5_d = nrm(ks[19], (DEPTH, S5_WIDTH), 1.0)
    s5_log_step = jax.random.uniform(ks[20], (DEPTH, S5_GROUPS), f32,
                                     math.log(S5_DT_MIN), math.log(S5_DT_MAX))
    s5_w_glu = nrm(ks[21], (DEPTH, S5_WIDTH, S5_WIDTH), S5_WIDTH ** -0.5)
    s5_b_glu = nrm(ks[22], (DEPTH, S5_WIDTH), 0.02)
    b_gate = nrm(ks[23], (DEPTH, N_BRANCH, D_MODEL), 0.02)
    w_branch = nrm(ks[24], (DEPTH, N_BRANCH, LRU_WIDTH, D_MODEL), LRU_WIDTH ** -0.5)
    w_out = nrm(ks[25], (DEPTH, D_MODEL, D_MODEL), D_MODEL ** -0.5)
    w_ffn_in = nrm(ks[26], (DEPTH, D_MODEL, 2 * D_FF), D_MODEL ** -0.5)
    w_ffn_out = nrm(ks[27], (DEPTH, D_FF, D_MODEL), D_FF ** -0.5)
    return {"x": x, "meta": meta, "rel_bias": rel_bias, "norm_w": norm_w, "w_in": w_in,
            "conv_w": conv_w, "conv_b": conv_b, "lru_w_a": lru_w_a, "lru_b_a": lru_b_a,
            "lru_w_x": lru_w_x, "lru_b_x": lru_b_x, "lru_lambda": lru_lambda,
            "da_lambda": da_lambda, "da_subln": da_subln, "s5_lam_re": s5_lam_re,
            "s5_lam_im": s5_lam_im, "s5_b_re": s5_b_re, "s5_b_im": s5_b_im, "s5_c_re": s5_c_re,
            "s5_c_im": s5_c_im, "s5_d": s5_d, "s5_log_step": s5_log_step, "s5_w_glu": s5_w_glu,
            "s5_b_glu": s5_b_glu, "b_gate": b_gate, "w_branch": w_branch, "w_out": w_out,
            "w_ffn_in": w_ffn_in, "w_ffn_out": w_ffn_out}


def reference(x, meta, rel_bias, norm_w, w_in, conv_w, conv_b, lru_w_a, lru_b_a, lru_w_x, lru_b_x,
              lru_lambda, da_lambda, da_subln, s5_lam_re, s5_lam_im, s5_b_re, s5_b_im, s5_c_re,
              s5_c_im, s5_d, s5_log_step, s5_w_glu, s5_b_glu, b_gate, w_branch, w_out,
              w_ffn_in, w_ffn_out):
    B = x.shape[0]
    xs = jnp.concatenate([jnp.broadcast_to(meta.astype(x.dtype)[None], (B, N_META, D_MODEL)), x], axis=1)
    for l in range(DEPTH):
        lam_init = 0.8 - 0.6 * math.exp(-0.3 * l)
        h = rms_norm(xs, norm_w[l, 0])
        mix = hybrid_mixer(h, lam_init, rel_bias, w_in[l], conv_w[l], conv_b[l], lru_w_a[l], lru_b_a[l],
                           lru_w_x[l], lru_b_x[l], lru_lambda[l], da_lambda[l], da_subln[l],
                           s5_lam_re[l], s5_lam_im[l], s5_b_re[l], s5_b_im[l], s5_c_re[l], s5_c_im[l],
                           s5_d[l], s5_log_step[l], s5_w_glu[l], s5_b_glu[l], b_gate[l], w_branch[l],
                           w_out[l])
        xs = xs + rms_norm(mix, norm_w[l, 1])
        h = rms_norm(xs, norm_w[l, 2])
        xs = xs + rms_norm(swiglu(h, w_ffn_in[l], w_ffn_out[l]), norm_w[l, 3])
    return xs[:, N_META:]
```

```python
import math
import numpy as np
import concourse.bass as bass
import concourse.mybir as mybir
from concourse.bass_utils import run_bass_kernel_spmd

F32 = mybir.dt.float32
BF16 = mybir.dt.bfloat16
AF = mybir.ActivationFunctionType
ALU = mybir.AluOpType
AX = mybir.AxisListType

D = 2048
NIN = 12288
DFF = 5632
NMETA = 16
DEPTH = 2
TABC = 384
TABW = 1024
TABR = TABW + 127


class Ctx:
    def __init__(self, nc):
        self.nc = nc
        self.E = {"pe": nc.tensor, "dve": nc.vector, "act": nc.scalar, "pool": nc.gpsimd, "sp": nc.sync, "cv": nc.gpsimd}
        self.sem = {e: nc.alloc_semaphore("c_" + e) for e in ["pe", "dve", "act", "pool"]}
        self.cnt = {e: 0 for e in self.sem}
        self.seen = {e: {} for e in self.E}
        self.dq = {q: [[nc.alloc_semaphore("d_%s%d" % (q, i)), 0] for i in range(n)]
                   for q, n in (("sp", 24), ("pool", 12), ("cv", 40))}
        self.dqi = {"sp": 0, "pool": 0, "cv": 0}
        self.persist = {}
        self.lastw = {}
        self.readers = {}

    def _wait(self, e, tok):
        if tok is None:
            return
        key, sem, val = tok
        if key == "pe" and e == "pe":
            return
        if self.seen[e].get(key, 0) >= val:
            return
        self.E[e].wait_ge(sem, val)
        self.seen[e][key] = val

    def deps(self, e, reads, writes):
        for r in reads:
            self._wait(e, self.lastw.get(r))
        for w in writes:
            self._wait(e, self.lastw.get(w))
            for t in self.readers.get(w, {}).values():
                self._wait(e, t)

    def commit(self, tok, reads, writes):
        for r in reads:
            self.readers.setdefault(r, {})[tok[0]] = tok
        for w in writes:
            self.lastw[w] = tok
            self.readers[w] = {}

    def op(self, e, fn, reads=(), writes=()):
        self.deps(e, reads, writes)
        ins = fn(self.E[e])
        self.cnt[e] += 1
        ins.then_inc(self.sem[e], 1)
        self.commit((e, self.sem[e], self.cnt[e]), reads, writes)

    def dma(self, out, in_, reads=(), writes=(), q="sp", **kw):
        self.deps(q, reads, writes)
        slots = self.dq[q]
        i = self.dqi[q] % len(slots)
        self.dqi[q] += 1
        sem, val = slots[i]
        key = ("d", q, i)
        if val > 0:
            self._wait(q, (key, sem, val))
        self.E[q].dma_start(out=out, in_=in_, **kw).then_inc(sem, 16)
        slots[i][1] = val + 16
        self.commit((key, sem, val + 16), reads, writes)

    def barrier(self, final=False):
        toks = [(e, self.sem[e], self.cnt[e]) for e in self.sem if self.cnt[e] > 0]
        for q, slots in self.dq.items():
            if q == "cv" and not final:
                continue
            for i, (sem, val) in enumerate(slots):
                if val > 0:
                    toks.append((("d", q, i), sem, val))
        for e in self.E:
            if e == "cv":
                continue
            for t in toks:
                if t[0] == e:
                    continue
                self._wait(e, t)
        self.lastw = dict(self.persist)
        self.readers = {}


def t5_bucket_np(n):
    n = np.asarray(n)
    nn = np.maximum(n, 0)
    nf = np.maximum(nn, 1).astype(np.float32)
    large = 16 + (np.log(nf / np.float32(16)) / np.float32(math.log(8.0)) * np.float32(16)).astype(np.int32)
    large = np.minimum(large, 31)
    return np.where(nn < 16, nn, large)


def host_consts():
    c = {}
    c["ident"] = np.eye(128, dtype=np.float32)
    n = np.arange(TABR) - 127 - TABC
    oh = np.zeros((33, TABR), np.float32)
    b = t5_bucket_np(n)
    for y in range(TABR):
        if n[y] < 0:
            oh[32, y] = 1.0
        else:
            oh[b[y], y] = 1.0
    c["onehot"] = oh
    mk = np.zeros((4, 128, 128), np.float32)
    for q in range(4):
        for g2 in range(2):
            gl = 2 * q + g2
            mk[q, g2 * 64:(g2 + 1) * 64, gl * 16:(gl + 1) * 16] = 1.0
    c["maskk"] = np.ascontiguousarray(mk.transpose(1, 0, 2))
    return c


PARAM_SHAPES = {
    "meta": (16, 2048), "rel_bias": (32, 8), "norm_w": (2, 4, 2048), "w_in": (2, 2048, 12288),
    "conv_w": (2, 4, 1024), "conv_b": (2, 1024), "lru_w_a": (2, 8, 128, 128), "lru_b_a": (2, 1024),
    "lru_w_x": (2, 8, 128, 128), "lru_b_x": (2, 1024), "lru_lambda": (2, 1024), "da_lambda": (2, 4, 64),
    "da_subln": (2, 128), "s5_lam_re": (2, 64, 64), "s5_lam_im": (2, 64, 64), "s5_b_re": (2, 64, 64, 16),
    "s5_b_im": (2, 64, 64, 16), "s5_c_re": (2, 64, 16, 64), "s5_c_im": (2, 64, 16, 64), "s5_d": (2, 1024),
    "s5_log_step": (2, 64), "s5_w_glu": (2, 1024, 1024), "s5_b_glu": (2, 1024), "b_gate": (2, 3, 2048),
    "w_branch": (2, 3, 1024, 2048), "w_out": (2, 2048, 2048), "w_ffn_in": (2, 2048, 11264),
    "w_ffn_out": (2, 5632, 2048),
}


def build(NB=8, depth=DEPTH, dbg=()):
    nc = bass.Bass("TRN2", target_bir_lowering=False)
    T = NMETA + 512 * NB
    NTM = 1 + 4 * NB
    SEQ = 512 * NB

    def cblk(i):
        return (0, 16) if i == 0 else (16 + 512 * (i - 1), 512)

    def tmt(j):
        return (0, 16) if j == 0 else (16 + 128 * (j - 1), 128)

    ins = {}
    ins["x"] = nc.dram_tensor("x", [SEQ, D], F32, kind="ExternalInput").ap()
    for k, shp in PARAM_SHAPES.items():
        ins[k] = nc.dram_tensor(k, list(shp), F32, kind="ExternalInput").ap()
    hc = host_consts()
    for k, v in hc.items():
        ins["c_" + k] = nc.dram_tensor("c_" + k, list(v.shape), F32, kind="ExternalInput").ap()
    out = nc.dram_tensor("out", [SEQ, D], F32, kind="ExternalOutput").ap()

    def scratch(name, shape, dt):
        if name in dbg:
            return nc.dram_tensor(name, shape, dt, kind="ExternalOutput").ap()
        return nc.dram_tensor(name, shape, dt).ap()

    XS = scratch("XS", [T, D], F32)
    HF = scratch("HF", [D, T], BF16)
    AG = scratch("AG", [1024, T], F32)
    AXs = scratch("AXs", [1024, T], F32)
    QF = scratch("QF", [1024, T], BF16)
    KF = scratch("KF", [1024, T], BF16)
    VF = scratch("VF", [1024, T], BF16)
    US = scratch("US", [1024, T], F32)
    GS = scratch("GS", [6144, T], F32)
    YS = scratch("YS", [3072, T], BF16)
    YG = scratch("YG", [1024, T], F32)
    MG = scratch("MG", [D, T], BF16)
    TBD = scratch("TBD", [8, 128, TABR], F32)
    WIN = [scratch("WIN%d" % l, [D, NIN], BF16) for l in range(depth)]
    WBR = [scratch("WBR%d" % l, [3072, D], BF16) for l in range(depth)]
    WOUT = [scratch("WOUT%d" % l, [D, D], BF16) for l in range(depth)]
    WF1 = [scratch("WF1%d" % l, [D, 2 * DFF], BF16) for l in range(depth)]
    WF2 = [scratch("WF2%d" % l, [DFF, D], BF16) for l in range(depth)]
    WGLU = [scratch("WGLU%d" % l, [1024, 1024], BF16) for l in range(depth)]
    LWA = [scratch("LWA%d" % l, [1024, 128], BF16) for l in range(depth)]
    LWX = [scratch("LWX%d" % l, [1024, 128], BF16) for l in range(depth)]

    cx = Ctx(nc)
    op, dma = cx.op, cx.dma
    uid = [0]

    def sbt(name, shape, dt):
        uid[0] += 1
        return nc.sbuf_tensor("%s_u%d" % (name, uid[0]), shape, dt)

    ident = nc.alloc_sbuf_tensor("ident", [128, 128], F32)
    ones16 = nc.alloc_sbuf_tensor("ones16", [128, 128], BF16)
    onesf = nc.alloc_sbuf_tensor("onesf", [128, 128], F32)
    PS = [nc.alloc_psum_tensor("ps%d" % i, [128, 512], F32) for i in range(8)]

    dma(ident[:], ins["c_ident"], writes=["ident"])
    if dbg:
        junk = nc.dram_tensor("junk", [len(ins), 4], F32).ap()
        for i_, (k_, ap_) in enumerate(ins.items()):
            flat = bass.AP(ap_.tensor, 0, [[4, 1], [1, 4]])
            dma(junk[i_:i_ + 1, :], flat, writes=[("junk", i_)])
    op("dve", lambda e: e.memset(ones16[:], 1.0), writes=["ones16"])
    op("dve", lambda e: e.memset(onesf[:], 1.0), writes=["onesf"])

    CVS = 1024

    def conv_w(dst, src, rows, key):
        for r in range(0, rows, CVS):
            n = min(CVS, rows - r)
            dma(dst[r:r + n, :], src[r:r + n, :], writes=[(key, r)], q="cv")
            cx.persist[(key, r)] = cx.lastw[(key, r)]

    def wkeys(key, rows):
        return [(key, r) for r in range(0, rows, CVS)]

    for l in range(depth if "noconv" not in dbg else 0):
        conv_w(WIN[l], ins["w_in"][l], D, "WIN%d" % l)
        conv_w(LWA[l], ins["lru_w_a"][l].rearrange("h i j -> (h i) j"), 1024, "LWA%d" % l)
        conv_w(LWX[l], ins["lru_w_x"][l].rearrange("h i j -> (h i) j"), 1024, "LWX%d" % l)
        conv_w(WGLU[l], ins["s5_w_glu"][l], 1024, "WGLU%d" % l)
        conv_w(WBR[l], ins["w_branch"][l].rearrange("b k m -> (b k) m"), 3072, "WBR%d" % l)
        conv_w(WOUT[l], ins["w_out"][l], D, "WOUT%d" % l)
        conv_w(WF1[l], ins["w_ffn_in"][l], D, "WF1%d" % l)
        conv_w(WF2[l], ins["w_ffn_out"][l], DFF, "WF2%d" % l)

    dma(XS[0:16, :], ins["meta"], writes=[("XS", 0)])
    for j in range(1, NTM):
        r0, n = tmt(j)
        dma(XS[r0:r0 + n, :], ins["x"][r0 - 16:r0 - 16 + n, :], writes=[("XS", j)])

    def build_table():
        with (sbt("rb", [33, 8], F32) as rb, sbt("oh", [33, TABR], F32) as oh,
              sbt("lh", [33, 128], F32) as lh, sbt("frow", [128, TABR], F32) as frow):
            op("dve", lambda e: e.memset(rb[32:33, :], -30000.0), writes=["rb32"])
            dma(rb[0:32, :], ins["rel_bias"], writes=["rb"])
            dma(oh[:], ins["c_onehot"], writes=["oh"])
            for h in range(8):
                op("dve", lambda e: e.tensor_copy(out=lh[:], in_=rb[:, h:h + 1].to_broadcast([33, 128])),
                   reads=["rb", "rb32"], writes=["lh"])
                for c0 in range(0, TABR, 512):
                    n = min(512, TABR - c0)
                    pk = "ps%d" % (c0 // 512)
                    op("pe", lambda e: e.matmul(PS[c0 // 512][:, :n], lh[:], oh[:, c0:c0 + n], start=True, stop=True),
                       reads=["lh", "oh"], writes=[pk])
                    op("act", lambda e: e.copy(out=frow[:, c0:c0 + n], in_=PS[c0 // 512][:, :n]),
                       reads=[pk], writes=[("frow", c0)])
                dma(TBD[h], frow[:], reads=[("frow", c0) for c0 in range(0, TABR, 512)], writes=[("TBD", h)])
            cx.barrier()

    if "stop:init" not in dbg:
        build_table()

    def norm_tile_to_hf(P, xt, nr, c0, nwcol, rk):
        sq, ss, xn, hT = P["sq"], P["ss"], P["xn"], P["hT"]
        op("act", lambda e: e.activation(out=sq[:nr, :], in_=xt[:nr, :], func=AF.Square), reads=[rk], writes=["sq"])
        op("dve", lambda e: e.tensor_reduce(out=ss[:nr, 0:1], in_=sq[:nr, :], axis=AX.X, op=ALU.add),
           reads=["sq"], writes=["ss"])
        op("dve", lambda e: e.tensor_scalar(out=ss[:nr, 1:2], in0=ss[:nr, 0:1], scalar1=1.0 / D, scalar2=1e-6,
                                            op0=ALU.mult, op1=ALU.add), reads=["ss"], writes=["ss1"])
        op("act", lambda e: e.sqrt(out=ss[:nr, 2:3], in_=ss[:nr, 1:2]), reads=["ss1"], writes=["ss2"])
        op("dve", lambda e: e.reciprocal(out=ss[:nr, 3:4], in_=ss[:nr, 2:3]), reads=["ss2"], writes=["ss3"])
        op("dve", lambda e: e.tensor_scalar(out=xn[:nr, :], in0=xt[:nr, :], scalar1=ss[:nr, 3:4], scalar2=None,
                                            op0=ALU.mult), reads=[rk, "ss3"], writes=["xn"])
        for f4 in range(4):
            pk = "ps%d" % (4 + f4)
            for i in range(4):
                ft = f4 * 4 + i
                op("pe", lambda e: e.transpose(out=PS[4 + f4][:, i * 128:i * 128 + nr],
                                               in_=xn[:nr, ft * 128:(ft + 1) * 128], identity=ident[:nr, :nr]),
                   reads=["xn", "ident"], writes=[pk])
            src = PS[4 + f4][:].rearrange("p (a b) -> p a b", a=4)[:, :, :nr]
            sc = nwcol[:, f4 * 4:(f4 + 1) * 4].unsqueeze(2).to_broadcast([128, 4, nr])
            op("dve", lambda e: e.tensor_tensor(out=hT[:, f4 * 4:(f4 + 1) * 4, :nr], in0=src, in1=sc, op=ALU.mult),
               reads=[pk, "nwcol"], writes=[("hT", f4)])
        dma(HF.rearrange("(a p) t -> p a t", p=128)[:, :, c0:c0 + nr], hT[:, :, :nr],
            reads=[("hT", f4) for f4 in range(4)], writes=[("HF", c0)], q="pool")

    def norm_bufs(st):
        P = {}
        P["sq"] = st.enter_context(sbt("n_sq", [128, D], F32))
        P["ss"] = st.enter_context(sbt("n_ss", [128, 4], F32))
        P["xn"] = st.enter_context(sbt("n_xn", [128, D], F32))
        P["hT"] = st.enter_context(sbt("n_hT", [128, 16, 128], BF16))
        return P

    from contextlib import ExitStack

    def load_col(dst, src_flat, nft, key):
        dma(dst, src_flat.rearrange("(a p) -> p a", p=128), writes=[key], allow_slow_non_contiguous=True)

    def phase_norm(l, which):
        with ExitStack() as st:
            P = norm_bufs(st)
            nwcol = st.enter_context(sbt("nwcol", [128, 16], F32))
            xt = [st.enter_context(sbt("xt%d" % i, [128, D], F32)) for i in range(2)]
            load_col(nwcol[:], ins["norm_w"][l, which], 16, "nwcol")
            for j in range(NTM):
                r0, n = tmt(j)
                s = j % 2
                dma(xt[s][:n, :], XS[r0:r0 + n, :], reads=[("XS", j)], writes=[("xt", s)])
                norm_tile_to_hf(P, xt[s], n, r0, nwcol, ("xt", s))
            cx.barrier()

    def linear_fm(W, wkey, K, M0, M, xsrc, xkey, sblocks, epilogue, st_bufs=None):
        KC = K // 128
        with ExitStack() as st:
            maxc = max(sum(cblk(b)[1] for b in sb) for sb in sblocks)
            hX = st.enter_context(sbt("l_hX", [128, KC, maxc], BF16))
            wt = [st.enter_context(sbt("l_wt%d" % i, [128, KC, 512], BF16)) for i in range(2)]
            wi = 0
            for sb in sblocks:
                c0 = cblk(sb[0])[0]
                ncol = sum(cblk(b)[1] for b in sb)
                dma(hX[:, :, :ncol], xsrc.rearrange("(a p) t -> p a t", p=128)[:, :, c0:c0 + ncol],
                    reads=[(xkey, cc) for cc in range(c0, c0 + ncol, 16)] if False else [xkey], writes=["hX"])
                for m0 in range(M0, M0 + M, 4):
                    s = wi % 2
                    wi += 1
                    nm = min(4, M0 + M - m0)
                    dma(wt[s][:, :, :nm * 128], W.rearrange("(a p) m -> p a m", p=128)[:, :, m0 * 128:(m0 + nm) * 128],
                        reads=wkeys(wkey, K), writes=[("wt", s)])
                    for i in range(nm):
                        ft = m0 + i
                        half = (ft % 2) * 4
                        off = 0
                        for bi, b in enumerate(sb):
                            n = cblk(b)[1]
                            pk = "ps%d" % (half + bi)
                            for kc in range(KC):
                                op("pe", lambda e: e.matmul(PS[half + bi][:, :n], wt[s][:, kc, i * 128:(i + 1) * 128],
                                                            hX[:, kc, off:off + n], start=(kc == 0), stop=(kc == KC - 1)),
                                   reads=[("wt", s), "hX"], writes=[pk])
                            epilogue(ft, bi, (c0 + off, n), PS[half + bi], pk)
                            off += n
            cx.barrier()

    def superblocks(maxb):
        sbs = [[0]]
        b = 1
        while b <= NB:
            sbs.append(list(range(b, min(b + maxb, NB + 1))))
            b += maxb
        return sbs

    def phase_win(l):
        with ExitStack() as st:
            stg = [st.enter_context(sbt("p2_stg%d" % i, [128, 512], F32)) for i in range(4)]
            stg16 = [st.enter_context(sbt("p2_s16%d" % i, [128, 512], BF16)) for i in range(4)]
            bg = st.enter_context(sbt("p2_bg", [128, 48], F32))
            load_col(bg[:], ins["b_gate"][l].rearrange("a b -> (a b)"), 48, "bg")
            cnt = [0]

            def epi(ft, bi, cr, ps, pk):
                c0, n = cr
                s = cnt[0] % 4
                cnt[0] += 1
                if ft < 8:
                    op("act", lambda e: e.activation(out=stg[s][:, :n], in_=ps[:, :n], func=AF.Gelu),
                       reads=[pk], writes=[("stg", s)])
                    dma(AG[ft * 128:(ft + 1) * 128, c0:c0 + n], stg[s][:, :n], reads=[("stg", s)], writes=[("AG", ft, c0)], q="pool")
                elif ft < 16 or 40 <= ft < 48:
                    dst = AXs if ft < 16 else US
                    r = (ft - 8) if ft < 16 else (ft - 40)
                    op("dve", lambda e: e.tensor_copy(out=stg[s][:, :n], in_=ps[:, :n]), reads=[pk], writes=[("stg", s)])
                    dma(dst[r * 128:(r + 1) * 128, c0:c0 + n], stg[s][:, :n], reads=[("stg", s)],
                        writes=[("AU", ft, c0)], q="pool")
                elif ft < 40:
                    dst, r, sc = (QF, ft - 16, 0.125) if ft < 24 else ((KF, ft - 24, 1.0) if ft < 32 else (VF, ft - 32, 1.0))
                    op("dve", lambda e: e.tensor_scalar(out=stg16[s][:, :n], in0=ps[:, :n], scalar1=sc, scalar2=None,
                                                        op0=ALU.mult), reads=[pk], writes=[("s16", s)])
                    dma(dst[r * 128:(r + 1) * 128, c0:c0 + n], stg16[s][:, :n], reads=[("s16", s)], writes=[("QKV", ft, c0)], q="pool")
                else:
                    r = ft - 48
                    op("act", lambda e: e.activation(out=stg[s][:, :n], in_=ps[:, :n], func=AF.Sigmoid,
                                                     bias=bg[:, r:r + 1]), reads=[pk, "bg"], writes=[("stg", s)])
                    dma(GS[r * 128:(r + 1) * 128, c0:c0 + n], stg[s][:, :n], reads=[("stg", s)], writes=[("GS", ft, c0)], q="pool")

            linear_fm(WIN[l], "WIN%d" % l, D, 0, 48, HF, "HFall", superblocks(4), epi)

    def phase_lru(l):
        with ExitStack() as st:
            def tl(name, dt=F32):
                return st.enter_context(sbt(name, [128, T], dt))
            ax, gg, xc, rr, ig, aa, mm = [tl("l_%d" % i) for i in range(7)]
            xc16 = tl("l_xc16", BF16)
            ya16 = tl("l_ya16", BF16)
            wa = st.enter_context(sbt("l_wa", [128, 8, 128], BF16))
            wx = st.enter_context(sbt("l_wx", [128, 8, 128], BF16))
            cw = st.enter_context(sbt("l_cw", [128, 8, 4], F32))
            cb = st.enter_context(sbt("l_cb", [128, 8], F32))
            ba = st.enter_context(sbt("l_ba", [128, 8], F32))
            bx = st.enter_context(sbt("l_bx", [128, 8], F32))
            lam = st.enter_context(sbt("l_lam", [128, 8, 4], F32))
            dma(wa[:], LWA[l].rearrange("(h i) j -> i h j", i=128), reads=wkeys("LWA%d" % l, 1024), writes=["wa"])
            dma(wx[:], LWX[l].rearrange("(h i) j -> i h j", i=128), reads=wkeys("LWX%d" % l, 1024), writes=["wx"])
            for wi_ in range(4):
                load_col(cw[:, :, wi_], ins["conv_w"][l, wi_], 8, ("cw", wi_))
            load_col(cb[:], ins["conv_b"][l], 8, "cb")
            load_col(ba[:], ins["lru_b_a"][l], 8, "ba")
            load_col(bx[:], ins["lru_b_x"][l], 8, "bx")
            load_col(lam[:, :, 0], ins["lru_lambda"][l], 8, "lam0")
            op("act", lambda e: e.activation(out=lam[:, :, 1], in_=lam[:, :, 0], func=AF.Exp, scale=-1.0),
               reads=["lam0"], writes=["lam1"])
            op("act", lambda e: e.activation(out=lam[:, :, 2], in_=lam[:, :, 1], func=AF.Ln, bias=1.0),
               reads=["lam1"], writes=["lam2"])
            op("dve", lambda e: e.tensor_scalar(out=lam[:, :, 1], in0=lam[:, :, 2], scalar1=-8.0, scalar2=None, op0=ALU.mult),
               reads=["lam2"], writes=["c8"])
            op("dve", lambda e: e.tensor_scalar(out=lam[:, :, 3], in0=lam[:, :, 2], scalar1=-16.0, scalar2=None, op0=ALU.mult),
               reads=["lam2"], writes=["c16"])
            for ct in range(8):
                rs = slice(ct * 128, (ct + 1) * 128)
                dma(ax[:], AXs[rs, :], writes=["ax"])
                dma(gg[:], AG[rs, :], writes=["gg"])
                op("dve", lambda e: e.tensor_scalar(out=xc[:], in0=ax[:], scalar1=cw[:, ct, 3:4], scalar2=cb[:, ct:ct + 1],
                                                    op0=ALU.mult, op1=ALU.add), reads=["ax", "cb"] + [("cw", i_) for i_ in range(4)], writes=["xc"])
                for sft in (1, 2, 3):
                    op("dve", lambda e: e.scalar_tensor_tensor(out=xc[:, sft:], in0=ax[:, :T - sft],
                                                               scalar=cw[:, ct, 3 - sft:4 - sft], in1=xc[:, sft:],
                                                               op0=ALU.mult, op1=ALU.add),
                       reads=["ax", "xc"] + [("cw", i_) for i_ in range(4)], writes=["xc"])
                op("act", lambda e: e.copy(out=xc16[:], in_=xc[:]), reads=["xc"], writes=["xc16"])
                for b in range(NB + 1):
                    c0, n = cblk(b)
                    pa, pb = "ps%d" % ((b % 2) * 2), "ps%d" % ((b % 2) * 2 + 1)
                    op("pe", lambda e: e.matmul(PS[(b % 2) * 2][:, :n], wa[:, ct, :], xc16[:, c0:c0 + n], start=True, stop=True),
                       reads=["wa", "xc16"], writes=[pa])
                    op("pe", lambda e: e.matmul(PS[(b % 2) * 2 + 1][:, :n], wx[:, ct, :], xc16[:, c0:c0 + n], start=True, stop=True),
                       reads=["wx", "xc16"], writes=[pb])
                    op("act", lambda e: e.activation(out=rr[:, c0:c0 + n], in_=PS[(b % 2) * 2][:, :n], func=AF.Sigmoid,
                                                     bias=ba[:, ct:ct + 1]), reads=[pa, "ba"], writes=[("rr", b)])
                    op("act", lambda e: e.activation(out=ig[:, c0:c0 + n], in_=PS[(b % 2) * 2 + 1][:, :n], func=AF.Sigmoid,
                                                     bias=bx[:, ct:ct + 1]), reads=[pb, "bx"], writes=[("ig", b)])
                rrk = [("rr", b) for b in range(NB + 1)]
                igk = [("ig", b) for b in range(NB + 1)]
                op("act", lambda e: e.activation(out=aa[:], in_=rr[:], func=AF.Exp, scale=lam[:, ct, 1:2]),
                   reads=rrk + ["c8"], writes=["aa"])
                op("act", lambda e: e.activation(out=mm[:], in_=rr[:], func=AF.Exp, scale=lam[:, ct, 3:4]),
                   reads=rrk + ["c16"], writes=["mm"])
                op("dve", lambda e: e.tensor_scalar(out=mm[:], in0=mm[:], scalar1=-1.0, scalar2=1.0, op0=ALU.mult, op1=ALU.add),
                   reads=["mm"], writes=["mm"])
                op("act", lambda e: e.sqrt(out=mm[:], in_=mm[:]), reads=["mm"], writes=["mm"])
                op("dve", lambda e: e.tensor_tensor(out=ig[:], in0=ig[:], in1=xc[:], op=ALU.mult),
                   reads=igk + ["xc"], writes=["igx"])
                op("dve", lambda e: e.tensor_tensor(out=mm[:], in0=mm[:], in1=ig[:], op=ALU.mult),
                   reads=["mm", "igx"], writes=["mm"])
                op("dve", lambda e: e.tensor_tensor_scan(out=rr[:], data0=aa[:], data1=mm[:], initial=0.0,
                                                         op0=ALU.mult, op1=ALU.add),
                   reads=["aa", "mm"], writes=rrk + ["hh"])
                op("dve", lambda e: e.tensor_tensor(out=ya16[:], in0=rr[:], in1=gg[:], op=ALU.mult),
                   reads=["hh", "gg"], writes=["ya16"])
                dma(YS[rs, :], ya16[:], reads=["ya16"], writes=[("YS", ct)], q="pool")
            cx.barrier()

    def phase_attn(l):
        lam_init = 0.8 - 0.6 * math.exp(-0.3 * l)
        with ExitStack() as st:
            qz = [st.enter_context(sbt("a_qz%d" % i, [128, T], BF16)) for i in range(2)]
            op("pool", lambda e: e.memset(qz[0][:], 0.0), writes=[("qz", 0)])
            op("pool", lambda e: e.memset(qz[1][:], 0.0), writes=[("qz", 1)])
            k16 = st.enter_context(sbt("a_k", [128, T], BF16))
            v16 = st.enter_context(sbt("a_v", [128, T], BF16))
            vT = st.enter_context(sbt("a_vT", [128, NTM, 128], BF16))
            id16 = st.enter_context(sbt("a_id16", [128, 128], BF16))
            pT = [st.enter_context(sbt("a_pT%d" % i, [128, 512], BF16)) for i in range(4)]
            ssb = [st.enter_context(sbt("a_ss%d" % i, [128, 512], F32)) for i in range(3)]
            o0 = st.enter_context(sbt("a_o0", [128, 512], F32))
            o1 = st.enter_context(sbt("a_o1", [128, 512], F32))
            rc = st.enter_context(sbt("a_rc", [128, 512], F32))
            rc2 = st.enter_context(sbt("a_rc2", [128, 512], F32))
            rc3 = st.enter_context(sbt("a_rc3", [128, 512], F32))
            sqb = st.enter_context(sbt("a_sq", [128, 512], F32))
            yb = [st.enter_context(sbt("a_yb%d" % i, [128, 512], BF16)) for i in range(2)]
            dl = st.enter_context(sbt("a_dl", [128, 4, 64], F32))
            dsc = st.enter_context(sbt("a_dsc", [128, 8], F32))
            sw = st.enter_context(sbt("a_sw", [128, 2], F32))
            tb = st.enter_context(sbt("a_tb", [128, 8, TABW], F32))
            for h in range(8):
                src = bass.AP(TBD.tensor, h * 128 * TABR + 127, [[TABR - 1, 128], [1, TABW]])
                dma(tb[:, h, :], src, writes=["tb"])
            dma(dl[:], bass.AP(ins["da_lambda"].tensor, l * 256, [[0, 128], [64, 4], [1, 64]]), writes=["dl"])
            op("dve", lambda e: e.tensor_tensor(out=dl[:, 0, :], in0=dl[:, 0, :], in1=dl[:, 1, :], op=ALU.mult),
               reads=["dl"], writes=["dl0"])
            op("dve", lambda e: e.tensor_tensor(out=dl[:, 2, :], in0=dl[:, 2, :], in1=dl[:, 3, :], op=ALU.mult),
               reads=["dl"], writes=["dl2"])
            op("dve", lambda e: e.tensor_reduce(out=dsc[:, 0:1], in_=dl[:, 0, :], axis=AX.X, op=ALU.add),
               reads=["dl0"], writes=["d0"])
            op("dve", lambda e: e.tensor_reduce(out=dsc[:, 1:2], in_=dl[:, 2, :], axis=AX.X, op=ALU.add),
               reads=["dl2"], writes=["d1"])
            op("act", lambda e: e.activation(out=dsc[:, 2:4], in_=dsc[:, 0:2], func=AF.Exp), reads=["d0", "d1"], writes=["d2"])
            op("dve", lambda e: e.tensor_tensor(out=dsc[:, 4:5], in0=dsc[:, 3:4], in1=dsc[:, 2:3], op=ALU.subtract),
               reads=["d2"], writes=["d4"])
            op("dve", lambda e: e.tensor_scalar(out=dsc[:, 5:6], in0=dsc[:, 4:5], scalar1=-lam_init, scalar2=None, op0=ALU.add),
               reads=["d4"], writes=["neglam"])
            load_col(sw[:, 0:1], ins["da_subln"][l], 1, "sw0")
            op("dve", lambda e: e.tensor_scalar(out=sw[:, 1:2], in0=sw[:, 0:1], scalar1=1.0 - lam_init, scalar2=None, op0=ALU.mult),
               reads=["sw0"], writes=["sw"])
            op("dve", lambda e: e.tensor_copy(out=id16[:], in_=ident[:]), reads=["ident"], writes=["id16"])
            pi = 0
            si = 0
            yi = 0
            pending = []
            for h in range(8):
                rs = slice(h * 128, (h + 1) * 128)
                dma(qz[0][0:64, :], QF[h * 128:h * 128 + 64, :], writes=[("qz", 0)])
                dma(qz[1][64:128, :], QF[h * 128 + 64:h * 128 + 128, :], writes=[("qz", 1)])
                dma(k16[:], KF[rs, :], writes=["k16"])
                dma(v16[:], VF[rs, :], writes=["v16"])
                for j in range(NTM):
                    r0, n = tmt(j)
                    pk = "ps%d" % (7 * (j % 2))
                    pst = PS[7 * (j % 2)][:].bitcast(BF16)
                    op("pe", lambda e: e.transpose(out=pst[:n, 0:128], in_=v16[:, r0:r0 + n], identity=id16[:]),
                       reads=["v16", "id16"], writes=[pk])
                    op("act", lambda e: e.copy(out=vT[:n, j, :], in_=pst[:n, 0:128]), reads=[pk], writes=[("vT", j)])
                for qb in range(NB + 1):
                    q0, nq = cblk(qb)
                    kts = [j for j in range(NTM) if tmt(j)[0] <= q0 + nq - 1]
                    steps = [(c, ji, j) for c in range(2) for ji, j in enumerate(kts)]
                    LA = 2
                    info = {}
                    for idx in range(len(steps) + LA):
                        if (idx == 4 or idx == len(steps) + LA - 1) and pending:
                            pending.pop(0)()
                        if idx < len(steps):
                            c, ji, j = steps[idx]
                            k0, nk = tmt(j)
                            delta = q0 - k0
                            sslot = si % 3
                            pss = PS[sslot]
                            pks = "ps%d" % sslot
                            op("pe", lambda e: e.matmul(pss[:nk, :nq], k16[:, k0:k0 + nk],
                                                        qz[c][:, q0:q0 + nq], start=True, stop=True),
                               reads=["k16", ("qz", c)], writes=[pks])
                            p = pT[pi % 4]
                            pkk = ("pT", pi % 4)
                            if delta < 240:
                                x0 = delta + TABC
                                sb_ = ssb[si % 3]
                                op("dve", lambda e: e.tensor_tensor(out=sb_[:nk, :nq], in0=pss[:nk, :nq],
                                                                    in1=tb[:nk, h, x0:x0 + nq], op=ALU.add),
                                   reads=[pks, "tb"], writes=[("ssb", si % 3)])
                                op("act", lambda e: e.activation(out=p[:nk, :nq], in_=sb_[:nk, :nq], func=AF.Exp),
                                   reads=[("ssb", si % 3)], writes=[pkk])
                            else:
                                op("act", lambda e: e.activation(out=p[:nk, :nq], in_=pss[:nk, :nq], func=AF.Exp,
                                                                 bias=tb[:nk, h, TABW - 1:TABW]),
                                   reads=[pks, "tb"], writes=[pkk])
                            info[idx] = (p, pkk, nk)
                            si += 1
                            pi += 1
                        if idx - LA >= 0:
                            c, ji, j = steps[idx - LA]
                            p, pkk, nk = info.pop(idx - LA)
                            ps_o, ps_r = PS[3 + c], PS[5 + c]
                            ko, kr = "ps%d" % (3 + c), "ps%d" % (5 + c)
                            first, last = ji == 0, ji == len(kts) - 1
                            op("pe", lambda e: e.matmul(ps_o[:, :nq], vT[:nk, j, :], p[:nk, :nq], start=first, stop=last),
                               reads=[("vT", j), pkk], writes=[ko])
                            op("pe", lambda e: e.matmul(ps_r[:, :nq], ones16[:nk, :], p[:nk, :nq], start=first, stop=last),
                               reads=["ones16", pkk], writes=[kr])
                    op("act", lambda e: e.activation(out=rc[:, :nq], in_=PS[5][:, :nq], func=AF.Ln), reads=["ps5"], writes=["rc"])
                    op("act", lambda e: e.activation(out=rc[:, :nq], in_=rc[:, :nq], func=AF.Exp, scale=-1.0), reads=["rc"], writes=["rc"])
                    op("dve", lambda e: e.tensor_tensor(out=o0[:, :nq], in0=PS[3][:, :nq], in1=rc[:, :nq], op=ALU.mult),
                       reads=["ps3", "rc"], writes=["o0"])
                    op("act", lambda e: e.activation(out=rc2[:, :nq], in_=PS[6][:, :nq], func=AF.Ln), reads=["ps6"], writes=["rc2"])
                    op("act", lambda e: e.activation(out=rc2[:, :nq], in_=rc2[:, :nq], func=AF.Exp, scale=-1.0), reads=["rc2"], writes=["rc2"])
                    op("dve", lambda e: e.tensor_tensor(out=o1[:, :nq], in0=PS[4][:, :nq], in1=rc2[:, :nq], op=ALU.mult),
                       reads=["ps4", "rc2"], writes=["o1"])
                    op("dve", lambda e: e.scalar_tensor_tensor(out=o0[:, :nq], in0=o1[:, :nq], scalar=dsc[:, 5:6],
                                                               in1=o0[:, :nq], op0=ALU.mult, op1=ALU.add),
                       reads=["o1", "o0", "neglam"], writes=["o0"])
                    op("dve", lambda e: e.tensor_tensor(out=sqb[:, :nq], in0=o0[:, :nq], in1=o0[:, :nq], op=ALU.mult),
                       reads=["o0"], writes=["sqb"])

                    def part2(h=h, qb=qb, q0=q0, nq=nq):
                        nonlocal yi
                        op("pe", lambda e: e.matmul(PS[7][:, :nq], onesf[:], sqb[:, :nq], start=True, stop=True),
                           reads=["onesf", "sqb"], writes=["ps7"])
                        op("dve", lambda e: e.tensor_scalar(out=rc3[:, :nq], in0=PS[7][:, :nq], scalar1=1.0 / 128, scalar2=1e-5,
                                                            op0=ALU.mult, op1=ALU.add), reads=["ps7"], writes=["rc3"])
                        op("act", lambda e: e.activation(out=rc3[:, :nq], in_=rc3[:, :nq], func=AF.Ln), reads=["rc3"], writes=["rc3"])
                        op("act", lambda e: e.activation(out=rc3[:, :nq], in_=rc3[:, :nq], func=AF.Exp, scale=-0.5),
                           reads=["rc3"], writes=["rc3"])
                        op("dve", lambda e: e.tensor_tensor(out=o0[:, :nq], in0=o0[:, :nq], in1=rc3[:, :nq], op=ALU.mult),
                           reads=["o0", "rc3"], writes=["o0"])
                        y_ = yb[yi % 2]
                        yk = ("yb", yi % 2)
                        yi += 1
                        op("dve", lambda e: e.tensor_scalar(out=y_[:, :nq], in0=o0[:, :nq], scalar1=sw[:, 1:2], scalar2=None,
                                                            op0=ALU.mult), reads=["o0", "sw"], writes=[yk])
                        dma(YS[1024 + h * 128:1024 + (h + 1) * 128, q0:q0 + nq], y_[:, :nq], reads=[yk], writes=[("YS", h, qb)],
                            q="pool")
                    pending.append(part2)
            while pending:
                pending.pop(0)()
            cx.barrier()

    def phase_s5(l):
        NLV = max(1, int(math.ceil(math.log2(T))))
        with ExitStack() as st, ExitStack() as st1:
            def t2(name, shape, dt=F32):
                return st.enter_context(sbt(name, shape, dt))

            def t1(name, shape, dt=F32):
                return st1.enter_context(sbt(name, shape, dt))
            pwr = t2("s_pwr", [128, 32, NLV]); pwi = t2("s_pwi", [128, 32, NLV]); pwn = t2("s_pwn", [128, 32, NLV])
            dcol = t2("s_dcol", [128, 8])
            mag = t2("s_mag", [128, 32])
            ur = t2("s_ur", [128, 32, 8]); ui = t2("s_ui", [128, 32, 8]); nui = t2("s_nui", [128, 32, 8])
            lpr = t2("s_lpr", [128, 32, 8]); lpi = t2("s_lpi", [128, 32, 8]); nlpi = t2("s_nlpi", [128, 32, 8])
            bbr = t2("s_bbr", [128, 32, 16]); bbi = t2("s_bbi", [128, 32, 16])
            mk = t2("s_mk", [128, 4, 128])
            cld = t2("s_cld", [128, 2, 2, 64])
            lr = t1("s_lr", [128, 32]); li = t1("s_li", [128, 32]); stp = t1("s_stp", [128, 32])
            w = [t1("s_w%d" % i, [128, 32]) for i in range(8)]
            cre = t1("s_cre", [128, 32]); cim = t1("s_cim", [128, 32])
            bre = t1("s_bre", [128, 32, 16]); bim = t1("s_bim", [128, 32, 16])
            btmp = t1("s_btmp", [128, 32, 16])
            dma(mk[:], ins["c_maskk"], writes=["mk"])
            load_col(dcol[:], ins["s5_d"][l], 8, "dcol")
            for g2 in range(2):
                ps_ = slice(g2 * 64, (g2 + 1) * 64)
                dma(lr[ps_, :], ins["s5_lam_re"][l].rearrange("(k g) p -> g p k", g=2)[g2], writes=[("lr", g2)],
                    allow_slow_non_contiguous=True)
                dma(li[ps_, :], ins["s5_lam_im"][l].rearrange("(k g) p -> g p k", g=2)[g2], writes=[("li", g2)],
                    allow_slow_non_contiguous=True)
                dma(stp[ps_, :], bass.AP(ins["s5_log_step"].tensor, l * 64 + g2, [[0, 64], [2, 32]]), writes=[("stp", g2)],
                    allow_slow_non_contiguous=True)
                dma(bre[ps_], ins["s5_b_re"][l].rearrange("(k g) p c -> g p k c", g=2)[g2], writes=[("bre", g2)])
                dma(bim[ps_], ins["s5_b_im"][l].rearrange("(k g) p c -> g p k c", g=2)[g2], writes=[("bim", g2)])
            K2 = [("lr", 0), ("lr", 1), ("li", 0), ("li", 1), ("stp", 0), ("stp", 1)]

            def tt(o, a, b, o_, rd, wr):
                op("dve", lambda e: e.tensor_tensor(out=o, in0=a, in1=b, op=o_), reads=rd, writes=wr)

            def ts(o, a, s1, s2, o0_, o1_, rd, wr):
                if o1_ is None:
                    op("dve", lambda e: e.tensor_scalar(out=o, in0=a, scalar1=s1, scalar2=None, op0=o0_), reads=rd, writes=wr)
                else:
                    op("dve", lambda e: e.tensor_scalar(out=o, in0=a, scalar1=s1, scalar2=s2, op0=o0_, op1=o1_), reads=rd, writes=wr)

            op("act", lambda e: e.activation(out=stp[:], in_=stp[:], func=AF.Exp), reads=K2, writes=["step"])
            tt(w[0][:], lr[:], stp[:], ALU.mult, K2 + ["step"], ["w0"])
            op("act", lambda e: e.activation(out=w[0][:], in_=w[0][:], func=AF.Exp), reads=["w0"], writes=["mag"])
            tt(w[1][:], li[:], stp[:], ALU.mult, K2 + ["step"], ["ang"])
            op("act", lambda e: e.activation(out=w[2][:], in_=w[1][:], func=AF.Sin, scale=1.0 / 16), reads=["ang"], writes=["sn"])
            op("act", lambda e: e.activation(out=w[3][:], in_=w[1][:], func=AF.Sin, scale=1.0 / 16, bias=math.pi / 2),
               reads=["ang"], writes=["cs"])
            for it in range(4):
                tt(w[4][:], w[2][:], w[3][:], ALU.mult, ["sn", "cs"], ["sc"])
                tt(w[5][:], w[3][:], w[3][:], ALU.mult, ["cs"], ["cc"])
                tt(w[6][:], w[2][:], w[2][:], ALU.mult, ["sn"], ["s2"])
                ts(w[2][:], w[4][:], 2.0, None, ALU.mult, None, ["sc", "s2"], ["sn"])
                tt(w[3][:], w[5][:], w[6][:], ALU.subtract, ["cc", "s2", "sc"], ["cs"])
            tt(pwr[:, :, 0], w[0][:], w[3][:], ALU.mult, ["mag", "cs"], ["pw"])
            tt(pwi[:, :, 0], w[0][:], w[2][:], ALU.mult, ["mag", "sn"], ["pw"])
            for lv in range(1, NLV):
                tt(w[4][:], pwr[:, :, lv - 1], pwr[:, :, lv - 1], ALU.mult, ["pw"], ["q0"])
                tt(w[5][:], pwi[:, :, lv - 1], pwi[:, :, lv - 1], ALU.mult, ["pw"], ["q1"])
                tt(w[6][:], pwr[:, :, lv - 1], pwi[:, :, lv - 1], ALU.mult, ["pw"], ["q2"])
                tt(pwr[:, :, lv], w[4][:], w[5][:], ALU.subtract, ["q0", "q1"], ["pw"])
                ts(pwi[:, :, lv], w[6][:], 2.0, None, ALU.mult, None, ["q2"], ["pw"])
            ts(pwn[:], pwi[:], -1.0, None, ALU.mult, None, ["pw"], ["pwn"])
            ts(w[4][:], pwr[:, :, 0], -1.0, None, ALU.add, None, ["pw"], ["am1"])
            tt(w[5][:], lr[:], lr[:], ALU.mult, K2, ["d0"])
            tt(w[6][:], li[:], li[:], ALU.mult, K2, ["d1"])
            tt(w[5][:], w[5][:], w[6][:], ALU.add, ["d0", "d1"], ["den"])
            op("dve", lambda e: e.reciprocal(out=w[5][:], in_=w[5][:]), reads=["den"], writes=["rden"])
            tt(w[6][:], w[4][:], lr[:], ALU.mult, ["am1"] + K2, ["e0"])
            tt(w[7][:], pwi[:, :, 0], li[:], ALU.mult, ["pw"] + K2, ["e1"])
            tt(w[6][:], w[6][:], w[7][:], ALU.add, ["e0", "e1"], ["e2"])
            tt(cre[:], w[6][:], w[5][:], ALU.mult, ["e2", "rden"], ["cre"])
            tt(w[6][:], pwi[:, :, 0], lr[:], ALU.mult, ["pw", "e2", "cre"] + K2, ["f0"])
            tt(w[7][:], w[4][:], li[:], ALU.mult, ["am1", "e1", "e2"] + K2, ["f1"])
            tt(w[6][:], w[6][:], w[7][:], ALU.subtract, ["f0", "f1"], ["f2"])
            tt(cim[:], w[6][:], w[5][:], ALU.mult, ["f2", "rden"], ["cim"])
            BK = [("bre", 0), ("bre", 1), ("bim", 0), ("bim", 1)]
            crb = cre[:].unsqueeze(2).to_broadcast([128, 32, 16])
            cib = cim[:].unsqueeze(2).to_broadcast([128, 32, 16])
            tt(bbr[:], bre[:], crb, ALU.mult, BK + ["cre"], ["bbr"])
            tt(btmp[:], bim[:], cib, ALU.mult, BK + ["cim"], ["btmp"])
            tt(bbr[:], bbr[:], btmp[:], ALU.subtract, ["bbr", "btmp"], ["bbr"])
            tt(bbi[:], bim[:], crb, ALU.mult, BK + ["cre"], ["bbi"])
            tt(btmp[:], bre[:], cib, ALU.mult, BK + ["cim", "bbr"], ["btmp"])
            tt(bbi[:], bbi[:], btmp[:], ALU.add, ["bbi", "btmp"], ["bbi"])
            op("dve", lambda e: e.tensor_copy(out=mag[:], in_=w[0][:]), reads=["mag"], writes=["magk"])
            op("dve", lambda e: e.memset(ur[:, :, 0], 1.0), writes=["tab"])
            op("dve", lambda e: e.memset(ui[:, :, 0], 0.0), writes=["tab"])
            op("dve", lambda e: e.tensor_copy(out=lpr[:, :, 0], in_=pwr[:, :, 0]), reads=["pw"], writes=["tab"])
            op("dve", lambda e: e.tensor_copy(out=lpi[:, :, 0], in_=pwi[:, :, 0]), reads=["pw"], writes=["tab"])
            for j in range(1, 8):
                for (tr, ti, mr, mi) in ((ur, ui, w[3], w[2]), (lpr, lpi, pwr[:, :, 0], pwi[:, :, 0])):
                    mr_ = mr[:] if hasattr(mr, "shape") and len(mr.shape) == 2 and not isinstance(mr, bass.AP) else mr
                    mi_ = mi[:] if hasattr(mi, "shape") and len(mi.shape) == 2 and not isinstance(mi, bass.AP) else mi
                    tt(w[4][:], tr[:, :, j - 1], mr_, ALU.mult, ["tab", "cs", "sn", "pw"], ["t4"])
                    tt(w[5][:], ti[:, :, j - 1], mi_, ALU.mult, ["tab", "cs", "sn", "pw"], ["t5"])
                    tt(w[6][:], tr[:, :, j - 1], mi_, ALU.mult, ["tab", "cs", "sn", "pw"], ["t6"])
                    tt(w[7][:], ti[:, :, j - 1], mr_, ALU.mult, ["tab", "cs", "sn", "pw"], ["t7"])
                    tt(tr[:, :, j], w[4][:], w[5][:], ALU.subtract, ["t4", "t5"], ["tab"])
                    tt(ti[:, :, j], w[6][:], w[7][:], ALU.add, ["t6", "t7"], ["tab"])
            ts(nui[:], ui[:], -1.0, None, ALU.mult, None, ["tab"], ["tab"])
            ts(nlpi[:], lpi[:], -1.0, None, ALU.mult, None, ["tab"], ["tab"])
            cx.barrier()
            st1.close()
            NCH = T // 8
            pieces = []
            c_ = 0
            while c_ < NCH:
                n_ = min(NCH - c_, 257 if NCH > 512 else 512)
                pieces.append((c_, n_))
                c_ += n_
            NLC = 0
            while (1 << NLC) < NCH:
                NLC += 1
            u32 = t2("s_u32", [128, T]); u16 = t2("s_u16", [128, T], BF16); ybuf = t2("s_ybuf", [128, T])
            Bps = [[t2("s_Bp%d_%d" % (b_, i), [128, T]) for i in range(2)] for b_ in range(2)]
            g16 = [t2("s_g%d" % i, [128, T], BF16) for i in range(2)]
            dec = t2("s_dec", [128, T])
            Z = [t2("s_Z%d" % i, [128, NCH]) for i in range(4)]
            H16 = [t2("s_H%d" % i, [128, NCH], BF16) for i in range(2)]
            cstfs = [t2("s_cstf%d" % i, [128, 4, 2, 128]) for i in range(2)]
            rot = t2("s_rot", [128, 2, 8, 16]); rta = t2("s_rta", [128, 8, 16]); rtb = t2("s_rtb", [128, 8, 16])
            bpre8 = t2("s_bpre8", [128, 8, 128])
            bstj = t2("s_bstj", [128, 8, 2, 128], BF16)
            mda = t2("s_mda", [128, 8, 128]); mdb = t2("s_mdb", [128, 8, 128])
            MD = t2("s_MD", [128, 4, 8, 128], BF16)

            def v8(ap):
                return ap.rearrange("p (c j) -> p c j", j=8)
            op("pool", lambda e: e.memset(dec[:], 0.0), writes=["dec"])
            op("pool", lambda e: e.memset(H16[0][:, 0:1], 0.0), writes=[("H16", 0)])
            op("pool", lambda e: e.memset(H16[1][:, 0:1], 0.0), writes=[("H16", 1)])
            ybk = [("ybuf", b) for b in range(len(pieces) * 8)]
            pcnt = 0
            pcn = [0]

            def ct_begin(ct):
                rs = slice(ct * 128, (ct + 1) * 128)
                cstf = cstfs[ct % 2]
                dma(u32[:], US[rs, :], writes=["u32"])
                op("act", lambda e: e.copy(out=u16[:], in_=u32[:]), reads=["u32"], writes=["u16"])
                for ri, cs_ in enumerate((ins["s5_c_re"], ins["s5_c_im"])):
                    srcc = cs_[l].rearrange("(t g) c p -> t (g c) p", g=8)[ct]
                    for dup in range(2):
                        dma(cld[:, ri, dup, :], srcc, writes=[("cld", ri, dup)])
                for ri in range(2):
                    pk = "ps%d" % (6 + ri)
                    op("pe", lambda e: e.transpose(out=PS[6 + ri][:, 0:128], in_=cld[:, ri].rearrange("p a b -> p (a b)"),
                                                   identity=ident[:]),
                       reads=[("cld", ri, 0), ("cld", ri, 1), "ident"], writes=[pk])
                    for q in range(4):
                        if ri == 0:
                            tt(cstf[:, q, 0, :], PS[6][:, 0:128], mk[:, q, :], ALU.mult, [pk, "mk"], [("cstf", ct % 2, q)])
                        else:
                            op("dve", lambda e: e.scalar_tensor_tensor(out=cstf[:, q, 1, :], in0=PS[7][:, 0:128], scalar=-1.0,
                                                                       in1=mk[:, q, :], op0=ALU.mult, op1=ALU.mult),
                               reads=[pk, "mk"], writes=[("cstf", ct % 2, q)])

            def stageA(k):
                ct, q = k // 4, k % 4
                Bpk = Bps[k % 2]
                bk = k % 2
                bbr_b = bbr[:, k, :].unsqueeze(1).to_broadcast([128, 8, 16])
                bbi_b = bbi[:, k, :].unsqueeze(1).to_broadcast([128, 8, 16])
                ur_b = ur[:, k, :].unsqueeze(2).to_broadcast([128, 8, 16])
                ui_b = ui[:, k, :].unsqueeze(2).to_broadcast([128, 8, 16])
                tt(rta[:], bbr_b, ur_b, ALU.mult, ["bbr", "tab"], ["rta"])
                tt(rtb[:], bbi_b, ui_b, ALU.mult, ["bbi", "tab"], ["rtb"])
                tt(rot[:, 0], rta[:], rtb[:], ALU.add, ["rta", "rtb"], [("rot", 0)])
                tt(rta[:], bbi_b, ur_b, ALU.mult, ["bbi", "tab", ("rot", 0)], ["rta"])
                tt(rtb[:], bbr_b, ui_b, ALU.mult, ["bbr", "tab", ("rot", 0)], ["rtb"])
                tt(rot[:, 1], rta[:], rtb[:], ALU.subtract, ["rta", "rtb"], [("rot", 1)])
                for ri in range(2):
                    src = rot[:, ri].unsqueeze(2).to_broadcast([128, 8, 8, 16])
                    msk = mk[:, k % 4, :].rearrange("p (a c) -> p a c", a=8).unsqueeze(1).to_broadcast([128, 8, 8, 16])
                    tt(bpre8[:].rearrange("p j (a c) -> p j a c", a=8), src, msk, ALU.mult, [("rot", ri), "mk"], ["bpre8"])
                    for j4 in range(2):
                        pidx = 6 + j4
                        pk = "ps%d" % pidx
                        for jj in range(4):
                            op("pe", lambda e: e.transpose(out=PS[pidx][:, jj * 128:(jj + 1) * 128], in_=bpre8[:, j4 * 4 + jj, :],
                                                           identity=ident[:]), reads=["bpre8", "ident"], writes=[pk])
                        op("act", lambda e: e.copy(out=bstj[:, j4 * 4:(j4 + 1) * 4, ri, :],
                                                   in_=PS[pidx][:].rearrange("p (a b) -> p a b", a=4)),
                           reads=[pk], writes=[("bstj", ri, j4)])
                bstk = [("bstj", ri, j4) for ri in range(2) for j4 in range(2)]
                for j in range(8):
                    for ri in range(2):
                        for (pc0, pn) in pieces:
                            pidx = pcn[0] % 4
                            pcn[0] += 1
                            pk = "ps%d" % pidx
                            op("pe", lambda e: e.matmul(PS[pidx][:, :pn], bstj[:, j, ri, :], v8(u16[:])[:, pc0:pc0 + pn, j],
                                                        start=True, stop=True), reads=bstk + ["u16"], writes=[pk])
                            op("act", lambda e: e.copy(out=v8(Bpk[ri][:])[:, pc0:pc0 + pn, j], in_=PS[pidx][:, :pn]),
                               reads=[pk], writes=[("Bp", bk, ri)])

            def stageB(k):
                ct, q = k // 4, k % 4
                Bpk = Bps[k % 2]
                bk = k % 2
                cstf = cstfs[ct % 2]
                c0b = cstf[:, q, 0, :].unsqueeze(1).to_broadcast([128, 8, 128])
                c1b = cstf[:, q, 1, :].unsqueeze(1).to_broadcast([128, 8, 128])

                def tb_(tab):
                    return tab[:, k, :].unsqueeze(2).to_broadcast([128, 8, 128])
                for i_, (ta, tb2) in enumerate(((ur, ui), (nui, ur), (lpr, lpi), (nlpi, lpr))):
                    op("pool", lambda e: e.tensor_tensor(out=mda[:], in0=c0b, in1=tb_(ta), op=ALU.mult),
                       reads=[("cstf", ct % 2, q), "tab"], writes=["mda"])
                    op("pool", lambda e: e.tensor_tensor(out=mdb[:], in0=c1b, in1=tb_(tb2), op=ALU.mult),
                       reads=[("cstf", ct % 2, q), "tab"], writes=["mdb"])
                    op("pool", lambda e: e.tensor_tensor(out=MD[:, i_], in0=mda[:], in1=mdb[:], op=ALU.add),
                       reads=["mda", "mdb"], writes=[("MD", i_)])
                op("pool", lambda e: e.tensor_scalar(out=dec[:], in0=dec[:], scalar1=0.0, scalar2=mag[:, k:k + 1],
                                                     op0=ALU.mult, op1=ALU.add), reads=["dec", "magk"], writes=["dec"])
                op("pool", lambda e: e.memset(v8(dec[:])[:, :, 0], 0.0), reads=["dec"], writes=["dec"])
                for ri in range(2):
                    op("dve", lambda e: e.tensor_tensor_scan(out=g16[ri][:], data0=dec[:], data1=Bpk[ri][:], initial=0.0,
                                                             op0=ALU.mult, op1=ALU.add),
                       reads=["dec", ("Bp", bk, ri)], writes=[("g16", ri)])
                g7r, g7i = v8(g16[0][:])[:, :, 7], v8(g16[1][:])[:, :, 7]
                G2 = [("g16", 0), ("g16", 1)]
                ts(Z[0][:], g7r, ur[:, k, 7:8], None, ALU.mult, None, G2 + ["tab"], [("Z", 0)])
                op("dve", lambda e: e.scalar_tensor_tensor(out=Z[0][:], in0=g7i, scalar=nui[:, k, 7:8], in1=Z[0][:],
                                                           op0=ALU.mult, op1=ALU.add), reads=G2 + ["tab", ("Z", 0)], writes=[("Z", 0)])
                ts(Z[1][:], g7r, ui[:, k, 7:8], None, ALU.mult, None, G2 + ["tab"], [("Z", 1)])
                op("dve", lambda e: e.scalar_tensor_tensor(out=Z[1][:], in0=g7i, scalar=ur[:, k, 7:8], in1=Z[1][:],
                                                           op0=ALU.mult, op1=ALU.add), reads=G2 + ["tab", ("Z", 1)], writes=[("Z", 1)])
                cur = 0
                for m in range(NLC):
                    d = 1 << m
                    lv = 3 + m
                    sr, si_ = Z[cur * 2], Z[cur * 2 + 1]
                    dr, di = Z[(1 - cur) * 2], Z[(1 - cur) * 2 + 1]
                    ks = [("Z", cur * 2), ("Z", cur * 2 + 1)]
                    kd0, kd1 = ("Z", (1 - cur) * 2), ("Z", (1 - cur) * 2 + 1)
                    op("dve", lambda e: e.scalar_tensor_tensor(out=dr[:, d:], in0=sr[:, :NCH - d], scalar=pwr[:, k, lv:lv + 1],
                                                               in1=sr[:, d:], op0=ALU.mult, op1=ALU.add),
                       reads=ks + ["pw"], writes=[kd0])
                    op("dve", lambda e: e.scalar_tensor_tensor(out=dr[:, d:], in0=si_[:, :NCH - d], scalar=pwn[:, k, lv:lv + 1],
                                                               in1=dr[:, d:], op0=ALU.mult, op1=ALU.add),
                       reads=ks + ["pwn", kd0], writes=[kd0])
                    op("dve", lambda e: e.scalar_tensor_tensor(out=di[:, d:], in0=si_[:, :NCH - d], scalar=pwr[:, k, lv:lv + 1],
                                                               in1=si_[:, d:], op0=ALU.mult, op1=ALU.add),
                       reads=ks + ["pw"], writes=[kd1])
                    op("dve", lambda e: e.scalar_tensor_tensor(out=di[:, d:], in0=sr[:, :NCH - d], scalar=pwi[:, k, lv:lv + 1],
                                                               in1=di[:, d:], op0=ALU.mult, op1=ALU.add),
                       reads=ks + ["pw", kd1], writes=[kd1])
                    op("dve", lambda e: e.tensor_copy(out=dr[:, :d], in_=sr[:, :d]), reads=ks, writes=[kd0])
                    op("dve", lambda e: e.tensor_copy(out=di[:, :d], in_=si_[:, :d]), reads=ks, writes=[kd1])
                    cur = 1 - cur
                for ri in range(2):
                    op("act", lambda e: e.copy(out=H16[ri][:, 1:NCH], in_=Z[cur * 2 + ri][:, 0:NCH - 1]),
                       reads=[("Z", cur * 2 + ri)], writes=[("H16", ri)])
                for j in range(8):
                    for pi_, (pc0, pn) in enumerate(pieces):
                        pidx = 4 + pcn[0] % 2
                        pcn[0] += 1
                        pk = "ps%d" % pidx
                        rhs = [v8(g16[0][:])[:, pc0:pc0 + pn, j], v8(g16[1][:])[:, pc0:pc0 + pn, j],
                               H16[0][:, pc0:pc0 + pn], H16[1][:, pc0:pc0 + pn]]
                        rk = [("g16", 0), ("g16", 1), ("H16", 0), ("H16", 1)]
                        for i_ in range(4):
                            op("pe", lambda e: e.matmul(PS[pidx][:, :pn], MD[:, i_, j, :], rhs[i_], start=(i_ == 0), stop=(i_ == 3)),
                               reads=[("MD", i_), rk[i_]], writes=[pk])
                        yv = v8(ybuf[:])[:, pc0:pc0 + pn, j]
                        yk = ("ybuf", j * len(pieces) + pi_)
                        if q == 0:
                            op("dve", lambda e: e.scalar_tensor_tensor(out=yv, in0=v8(u32[:])[:, pc0:pc0 + pn, j],
                                                                       scalar=dcol[:, ct:ct + 1], in1=PS[pidx][:, :pn],
                                                                       op0=ALU.mult, op1=ALU.add),
                               reads=[pk, "u32", "dcol"], writes=[yk])
                        else:
                            tt(yv, yv, PS[pidx][:, :pn], ALU.add, [pk, yk], [yk])

            def ct_end(ct):
                rs = slice(ct * 128, (ct + 1) * 128)
                op("act", lambda e: e.activation(out=ybuf[:], in_=ybuf[:], func=AF.Gelu), reads=ybk, writes=ybk)
                dma(YG[rs, :], ybuf[:], reads=ybk, writes=[("YG", ct)], q="pool")

            for k in range(32):
                if k % 4 == 0:
                    ct_begin(k // 4)
                stageA(k)
                if k >= 1:
                    stageB(k - 1)
                    if (k - 1) % 4 == 3:
                        ct_end((k - 1) // 4)
            stageB(31)
            ct_end(7)
            cx.barrier()
        with ExitStack() as st:
            bgl = st.enter_context(sbt("g_b", [128, 8], F32))
            yg32 = st.enter_context(sbt("g_y32", [128, 8, 512], F32))
            yg16 = st.enter_context(sbt("g_y16", [128, 8, 512], BF16))
            wg = st.enter_context(sbt("g_w", [128, 8, 1024], BF16))
            sg = st.enter_context(sbt("g_sg", [128, 512], F32))
            yc = [st.enter_context(sbt("g_yc%d" % i, [128, 512], BF16)) for i in range(2)]
            load_col(bgl[:], ins["s5_b_glu"][l], 8, "bgl")
            dma(wg[:], WGLU[l].rearrange("(a p) m -> p a m", p=128), reads=wkeys("WGLU%d" % l, 1024), writes=["wg"])
            oi = 0
            for b in range(NB + 1):
                c0, n = cblk(b)
                dma(yg32[:, :, :n], YG.rearrange("(a p) t -> p a t", p=128)[:, :, c0:c0 + n], writes=["yg32"])
                op("act", lambda e: e.copy(out=yg16[:, :, :n], in_=yg32[:, :, :n]), reads=["yg32"], writes=["yg16"])
                for ft in range(8):
                    pidx = ft % 2
                    pk = "ps%d" % pidx
                    for kc in range(8):
                        op("pe", lambda e: e.matmul(PS[pidx][:, :n], wg[:, kc, ft * 128:(ft + 1) * 128], yg16[:, kc, :n],
                                                    start=(kc == 0), stop=(kc == 7)), reads=["wg", "yg16"], writes=[pk])
                    op("act", lambda e: e.activation(out=sg[:, :n], in_=PS[pidx][:, :n], func=AF.Sigmoid, bias=bgl[:, ft:ft + 1]),
                       reads=[pk, "bgl"], writes=["sg"])
                    y_ = yc[oi % 2]
                    yk = ("yc", oi % 2)
                    oi += 1
                    op("dve", lambda e: e.tensor_tensor(out=y_[:, :n], in0=sg[:, :n], in1=yg32[:, ft, :n], op=ALU.mult),
                       reads=["sg", "yg32"], writes=[yk])
                    dma(YS[2048 + ft * 128:2048 + (ft + 1) * 128, c0:c0 + n], y_[:, :n], reads=[yk], writes=[("YS", ft, b)], q="pool")
            cx.barrier()

    def phase_branch(l):
        with ExitStack() as st:
            hX = st.enter_context(sbt("b_hX", [128, 16, 1024], BF16))
            yX = st.enter_context(sbt("b_yX", [128, 24, 1024], BF16))
            wt = [st.enter_context(sbt("b_wt%d" % i, [128, 24, 128], BF16)) for i in range(2)]
            wg = [st.enter_context(sbt("b_wg%d" % i, [128, 3, 16, 128], BF16)) for i in range(2)]
            sg = [st.enter_context(sbt("b_sg%d" % i, [128, 512], F32)) for i in range(3)]
            m0 = st.enter_context(sbt("b_m0", [128, 512], F32))
            m1 = st.enter_context(sbt("b_m1", [128, 512], F32))
            mg = [st.enter_context(sbt("b_mg%d" % i, [128, 512], BF16)) for i in range(2)]
            bg = st.enter_context(sbt("b_bg", [128, 48], F32))
            load_col(bg[:], ins["b_gate"][l].rearrange("a b -> (a b)"), 48, "bg")
            Wv = WIN[l].rearrange("(a p) m -> p a m", p=128)
            it = 0
            pc = 0
            mi = 0
            for sb in superblocks(2):
                c0 = cblk(sb[0])[0]
                ncol = sum(cblk(b)[1] for b in sb)
                dma(hX[:, :, :ncol], HF.rearrange("(a p) t -> p a t", p=128)[:, :, c0:c0 + ncol], writes=["hX"])
                dma(yX[:, :, :ncol], YS.rearrange("(a p) t -> p a t", p=128)[:, :, c0:c0 + ncol], writes=["yX"])
                for ft in range(16):
                    s = it % 2
                    it += 1
                    dma(wt[s][:], WBR[l].rearrange("(a p) m -> p a m", p=128)[:, :, ft * 128:(ft + 1) * 128],
                        reads=wkeys("WBR%d" % l, 3072), writes=[("wt", s)])
                    for br in range(3):
                        g0 = 6144 + br * 2048 + ft * 128
                        dma(wg[s][:, br], Wv[:, :, g0:g0 + 128], reads=wkeys("WIN%d" % l, D), writes=[("wg", s, br)])
                    off = 0
                    for b in sb:
                        n = cblk(b)[1]
                        for br in range(3):
                            pg, pb = pc % 8, (pc + 1) % 8
                            pc += 2
                            for kc in range(16):
                                op("pe", lambda e: e.matmul(PS[pg][:, :n], wg[s][:, br, kc, :], hX[:, kc, off:off + n],
                                                            start=(kc == 0), stop=(kc == 15)),
                                   reads=[("wg", s, br), "hX"], writes=["ps%d" % pg])
                            for kc in range(8):
                                op("pe", lambda e: e.matmul(PS[pb][:, :n], wt[s][:, br * 8 + kc, :], yX[:, br * 8 + kc, off:off + n],
                                                            start=(kc == 0), stop=(kc == 7)),
                                   reads=[("wt", s), "yX"], writes=["ps%d" % pb])
                            r = br * 16 + ft
                            if "noepi" in dbg:
                                continue
                            op("act", lambda e: e.activation(out=sg[br][:, :n], in_=PS[pg][:, :n], func=AF.Sigmoid,
                                                             bias=bg[:, r:r + 1]), reads=["ps%d" % pg, "bg"], writes=[("sg", br)])
                            if "nodve" in dbg:
                                continue
                            if br == 0:
                                op("dve", lambda e: e.tensor_tensor(out=m0[:, :n], in0=PS[pb][:, :n], in1=sg[br][:, :n], op=ALU.mult),
                                   reads=["ps%d" % pb, ("sg", br)], writes=["m0"])
                            else:
                                op("dve", lambda e: e.tensor_tensor(out=m1[:, :n], in0=PS[pb][:, :n], in1=sg[br][:, :n], op=ALU.mult),
                                   reads=["ps%d" % pb, ("sg", br)], writes=["m1"])
                                if br == 1:
                                    op("dve", lambda e: e.tensor_tensor(out=m0[:, :n], in0=m0[:, :n], in1=m1[:, :n], op=ALU.add),
                                       reads=["m0", "m1"], writes=["m0"])
                        ms = mi % 2
                        mi += 1
                        if "noepi" in dbg or "nodve" in dbg:
                            off += n
                            continue
                        op("dve", lambda e: e.tensor_tensor(out=mg[ms][:, :n], in0=m0[:, :n], in1=m1[:, :n], op=ALU.add),
                           reads=["m0", "m1"], writes=[("mg", ms)])
                        if "nomgdma" not in dbg:
                            dma(MG[ft * 128:(ft + 1) * 128, c0 + off:c0 + off + n], mg[ms][:, :n], reads=[("mg", ms)],
                                writes=[("MG", ft, b)], q="pool")
                        off += n
            cx.barrier()

    def tm_epilogue(P, mixsrc, mixkeys, j, nwb, nwcol_next, xt, xk, last_out):
        r0, n = tmt(j)
        sq, ss, tmp = P["esq"], P["ss2"], P["esq"]
        dma(xt[:n, :], XS[r0:r0 + n, :], reads=[("XS", j)], writes=[xk], q="pool")
        op("act", lambda e: e.activation(out=sq[:n, :], in_=mixsrc[:n, :], func=AF.Square), reads=mixkeys, writes=["esq"])
        op("dve", lambda e: e.tensor_reduce(out=ss[:n, 0:1], in_=sq[:n, :], axis=AX.X, op=ALU.add), reads=["esq"], writes=["e0"])
        op("dve", lambda e: e.tensor_scalar(out=ss[:n, 1:2], in0=ss[:n, 0:1], scalar1=1.0 / D, scalar2=1e-6,
                                            op0=ALU.mult, op1=ALU.add), reads=["e0"], writes=["e1"])
        op("act", lambda e: e.sqrt(out=ss[:n, 2:3], in_=ss[:n, 1:2]), reads=["e1"], writes=["e2"])
        op("dve", lambda e: e.reciprocal(out=ss[:n, 3:4], in_=ss[:n, 2:3]), reads=["e2"], writes=["e3"])
        op("dve", lambda e: e.tensor_scalar(out=tmp[:n, :], in0=mixsrc[:n, :], scalar1=ss[:n, 3:4], scalar2=None, op0=ALU.mult),
           reads=mixkeys + ["e3", "esq"], writes=["esq"])
        op("pool", lambda e: e.tensor_tensor(out=tmp[:n, :], in0=tmp[:n, :], in1=nwb[:n, :], op=ALU.mult),
           reads=["esq", "nwb"], writes=["esq"])
        op("pool", lambda e: e.tensor_tensor(out=xt[:n, :], in0=xt[:n, :], in1=tmp[:n, :], op=ALU.add),
           reads=["esq", xk], writes=[xk])
        if last_out:
            if j >= 1:
                dma(out[r0 - 16:r0 - 16 + n, :], xt[:n, :], reads=[xk], writes=[("OUT", j)], q="pool")
        else:
            dma(XS[r0:r0 + n, :], xt[:n, :], reads=[xk], writes=[("XS", j)], q="pool")
            if nwcol_next is not None:
                norm_tile_to_hf(P, xt, n, r0, nwcol_next, xk)

    def epi_bufs(st):
        P = norm_bufs(st)
        P["ss2"] = st.enter_context(sbt("e_ss", [128, 4], F32))
        P["esq"] = st.enter_context(sbt("e_sq", [128, D], F32))
        return P

    def load_bcast_row(dst, src_row, key):
        dma(dst, bass.AP(src_row.tensor, src_row.offset, [[0, 128], [1, D]]), writes=[key])

    def phase_wout(l):
        with ExitStack() as st:
            P = epi_bufs(st)
            wo = st.enter_context(sbt("o_w", [128, 16, 1024], BF16))
            mT = [st.enter_context(sbt("o_m%d" % i, [128, 16, 128], BF16)) for i in range(2)]
            mixs = [st.enter_context(sbt("o_mix%d" % i, [128, D], F32)) for i in range(2)]
            nwb = st.enter_context(sbt("o_nwb", [128, D], F32))
            nwc = st.enter_context(sbt("o_nwc", [128, 16], F32))
            xt = [st.enter_context(sbt("o_xt%d" % i, [128, D], F32)) for i in range(2)]
            load_bcast_row(nwb[:], ins["norm_w"][l, 1], "nwb")
            load_col(nwc[:], ins["norm_w"][l, 2], 16, "nwcol")
            for half in range(2):
                dma(wo[:], WOUT[l].rearrange("(a p) m -> p a m", p=128)[:, :, half * 1024:(half + 1) * 1024],
                    reads=wkeys("WOUT%d" % l, D), writes=["wo"])
                break
            wo2 = st.enter_context(sbt("o_w2", [128, 16, 1024], BF16))
            dma(wo2[:], WOUT[l].rearrange("(a p) m -> p a m", p=128)[:, :, 1024:2048],
                reads=wkeys("WOUT%d" % l, D), writes=["wo2"])
            for j in range(NTM):
                r0, n = tmt(j)
                s = j % 2
                dma(mT[s][:, :, :n], MG.rearrange("(a p) t -> p a t", p=128)[:, :, r0:r0 + n], writes=[("mT", s)])
                for fb in range(4):
                    wsrc, wk = (wo, "wo") if fb < 2 else (wo2, "wo2")
                    pidx = fb
                    pk = "ps%d" % pidx
                    for kc in range(16):
                        op("pe", lambda e: e.matmul(PS[pidx][:n, :], mT[s][:, kc, :n], wsrc[:, kc, (fb % 2) * 512:(fb % 2 + 1) * 512],
                                                    start=(kc == 0), stop=(kc == 15)), reads=[("mT", s), wk], writes=[pk])
                    op("act", lambda e: e.copy(out=mixs[s][:n, fb * 512:(fb + 1) * 512], in_=PS[pidx][:n, :]),
                       reads=[pk], writes=[("mix", s, fb)])
                tm_epilogue(P, mixs[s], [("mix", s, fb) for fb in range(4)], j, nwb, nwc, xt[s], ("xt", s), False)
            cx.barrier()

    def phase_ffn(l, last):
        with ExitStack() as st:
            P = epi_bufs(st)
            hX = st.enter_context(sbt("f_hX", [128, 16, 512], BF16))
            act16 = st.enter_context(sbt("f_act", [128, 44, 512], BF16))
            w1 = [st.enter_context(sbt("f_w1%d" % i, [128, 2, 16, 128], BF16)) for i in range(2)]
            w2 = [st.enter_context(sbt("f_w2%d" % i, [128, 44, 256], BF16)) for i in range(2)]
            sl = st.enter_context(sbt("f_sl", [128, 512], F32))
            mix = [st.enter_context(sbt("f_mix%d" % i, [128, D], F32)) for i in range(4)]
            nwb = st.enter_context(sbt("f_nwb", [128, D], F32))
            nwc = st.enter_context(sbt("f_nwc", [128, 16], F32))
            xt = st.enter_context(sbt("f_xt", [128, D], F32))
            load_bcast_row(nwb[:], ins["norm_w"][l, 3], "nwb")
            if not last:
                load_col(nwc[:], ins["norm_w"][l + 1, 0], 16, "nwcol")
            W1v = WF1[l].rearrange("(a p) m -> p a m", p=128)
            W2v = WF2[l].rearrange("(a p) m -> p a m", p=128)
            i1 = 0
            i2 = 0
            for b in range(NB + 1):
                c0, n = cblk(b)
                dma(hX[:, :, :n], HF.rearrange("(a p) t -> p a t", p=128)[:, :, c0:c0 + n], reads=["HFall"], writes=["hX"])
                for ft in range(44):
                    s = i1 % 2
                    i1 += 1
                    dma(w1[s][:, 0], W1v[:, :, ft * 128:(ft + 1) * 128], reads=wkeys("WF1%d" % l, D), writes=[("w1", s, 0)])
                    dma(w1[s][:, 1], W1v[:, :, DFF + ft * 128:DFF + (ft + 1) * 128], reads=wkeys("WF1%d" % l, D),
                        writes=[("w1", s, 1)])
                    pg, pu = (ft % 2) * 2, (ft % 2) * 2 + 1
                    for gu, pidx in ((0, pg), (1, pu)):
                        for kc in range(16):
                            op("pe", lambda e: e.matmul(PS[pidx][:, :n], w1[s][:, gu, kc, :], hX[:, kc, :n],
                                                        start=(kc == 0), stop=(kc == 15)),
                               reads=[("w1", s, gu), "hX"], writes=["ps%d" % pidx])
                    op("act", lambda e: e.activation(out=sl[:, :n], in_=PS[pg][:, :n], func=AF.Silu),
                       reads=["ps%d" % pg], writes=["sl"])
                    op("dve", lambda e: e.tensor_tensor(out=act16[:, ft, :n], in0=sl[:, :n], in1=PS[pu][:, :n], op=ALU.mult),
                       reads=["sl", "ps%d" % pu], writes=[("act", ft)])
                ntile = 1 if b == 0 else 4
                for fq in range(8):
                    s = i2 % 2
                    i2 += 1
                    dma(w2[s][:], W2v[:, :, fq * 256:(fq + 1) * 256], reads=wkeys("WF2%d" % l, DFF), writes=[("w2", s)])
                    for tt_ in range(ntile):
                        nt = 16 if b == 0 else 128
                        pidx = 4 + (fq * ntile + tt_) % 4
                        pk = "ps%d" % pidx
                        for kc in range(44):
                            op("pe", lambda e: e.matmul(PS[pidx][:nt, :256], act16[:, kc, tt_ * 128:tt_ * 128 + nt], w2[s][:, kc, :],
                                                        start=(kc == 0), stop=(kc == 43)),
                               reads=[("act", kc), ("w2", s)], writes=[pk])
                        op("act", lambda e: e.copy(out=mix[tt_][:nt, fq * 256:(fq + 1) * 256], in_=PS[pidx][:nt, :256]),
                           reads=[pk], writes=[("mix", tt_, fq)])
                for tt_ in range(ntile):
                    j = 0 if b == 0 else 1 + 4 * (b - 1) + tt_
                    tm_epilogue(P, mix[tt_], [("mix", tt_, fq) for fq in range(8)], j, nwb, None if last else nwc,
                                xt, "xt", last)
            cx.barrier()

    stop_after = None
    for d_ in dbg:
        if d_.startswith("stop:"):
            stop_after = d_[5:]
    for l in range(depth):
        last = l == depth - 1
        if stop_after in ("init", "table"):
            break
        if l == 0:
            phase_norm(l, 0)
        cx.lastw["HFall"] = None
        if stop_after == "norm":
            break
        phase_win(l)
        if stop_after == "win":
            break
        phase_lru(l)
        if stop_after == "lru":
            break
        phase_attn(l)
        if stop_after == "attn":
            break
        phase_s5(l)
        if stop_after == "s5":
            break
        phase_branch(l)
        if stop_after == "branch":
            break
        phase_wout(l)
        if stop_after == "wout":
            break
        phase_ffn(l, last)
    cx.barrier(final=True)
    return nc, hc


_CACHE = {}


def kernel(**inputs):
    x = np.ascontiguousarray(np.asarray(inputs["x"], dtype=np.float32))
    B = x.shape[0]
    if "nc" not in _CACHE:
        _CACHE["nc"] = build(NB=x.shape[1] // 512)
    nc, hc = _CACHE["nc"]
    base = {k: np.ascontiguousarray(np.asarray(inputs[k], dtype=np.float32)) for k in PARAM_SHAPES}
    for k, v in hc.items():
        base["c_" + k] = v
    in_maps = []
    for c in range(8):
        m = dict(base)
        m["x"] = x[c % B]
        in_maps.append(m)
    res = run_bass_kernel_spmd(nc, in_maps, core_ids=list(range(8)))
    return np.stack([res.results[b]["out"] for b in range(B)], axis=0).astype(np.float32)
```

```python
import math
import numpy as np
import concourse.bass as bass
import concourse.mybir as mybir
from concourse.bass_utils import run_bass_kernel_spmd

F32 = mybir.dt.float32
BF16 = mybir.dt.bfloat16
AF = mybir.ActivationFunctionType
ALU = mybir.AluOpType
AX = mybir.AxisListType

D = 2048
NIN = 12288
DFF = 5632
NMETA = 16
DEPTH = 2
TABC = 384
TABW = 1024
TABR = TABW + 127


class Ctx:
    def __init__(self, nc):
        self.nc = nc
        self.E = {"pe": nc.tensor, "dve": nc.vector, "act": nc.scalar, "pool": nc.gpsimd, "sp": nc.sync, "cv": nc.gpsimd}
        self.sem = {e: nc.alloc_semaphore("c_" + e) for e in ["pe", "dve", "act", "pool"]}
        self.cnt = {e: 0 for e in self.sem}
        self.seen = {e: {} for e in self.E}
        self.dq = {q: [[nc.alloc_semaphore("d_%s%d" % (q, i)), 0] for i in range(n)]
                   for q, n in (("sp", 24), ("pool", 12), ("cv", 40))}
        self.dqi = {"sp": 0, "pool": 0, "cv": 0}
        self.persist = {}
        self.lastw = {}
        self.readers = {}

    def _wait(self, e, tok):
        if tok is None:
            return
        key, sem, val = tok
        if key == "pe" and e == "pe":
            return
        if self.seen[e].get(key, 0) >= val:
            return
        self.E[e].wait_ge(sem, val)
        self.seen[e][key] = val

    def deps(self, e, reads, writes):
        for r in reads:
            self._wait(e, self.lastw.get(r))
        for w in writes:
            self._wait(e, self.lastw.get(w))
            for t in self.readers.get(w, {}).values():
                self._wait(e, t)

    def commit(self, tok, reads, writes):
        for r in reads:
            self.readers.setdefault(r, {})[tok[0]] = tok
        for w in writes:
            self.lastw[w] = tok
            self.readers[w] = {}

    def op(self, e, fn, reads=(), writes=()):
        self.deps(e, reads, writes)
        ins = fn(self.E[e])
        self.cnt[e] += 1
        ins.then_inc(self.sem[e], 1)
        self.commit((e, self.sem[e], self.cnt[e]), reads, writes)

    def dma(self, out, in_, reads=(), writes=(), q="sp", **kw):
        self.deps(q, reads, writes)
        slots = self.dq[q]
        i = self.dqi[q] % len(slots)
        self.dqi[q] += 1
        sem, val = slots[i]
        key = ("d", q, i)
        if val > 0:
            self._wait(q, (key, sem, val))
        self.E[q].dma_start(out=out, in_=in_, **kw).then_inc(sem, 16)
        slots[i][1] = val + 16
        self.commit((key, sem, val + 16), reads, writes)

    def barrier(self, final=False):
        toks = [(e, self.sem[e], self.cnt[e]) for e in self.sem if self.cnt[e] > 0]
        for q, slots in self.dq.items():
            if q == "cv" and not final:
                continue
            for i, (sem, val) in enumerate(slots):
                if val > 0:
                    toks.append((("d", q, i), sem, val))
        for e in self.E:
            if e == "cv":
                continue
            for t in toks:
                if t[0] == e:
                    continue
                self._wait(e, t)
        self.lastw = dict(self.persist)
        self.readers = {}


def t5_bucket_np(n):
    n = np.asarray(n)
    nn = np.maximum(n, 0)
    nf = np.maximum(nn, 1).astype(np.float32)
    large = 16 + (np.log(nf / np.float32(16)) / np.float32(math.log(8.0)) * np.float32(16)).astype(np.int32)
    large = np.minimum(large, 31)
    return np.where(nn < 16, nn, large)


def host_consts():
    c = {}
    c["ident"] = np.eye(128, dtype=np.float32)
    n = np.arange(TABR) - 127 - TABC
    oh = np.zeros((33, TABR), np.float32)
    b = t5_bucket_np(n)
    for y in range(TABR):
        if n[y] < 0:
            oh[32, y] = 1.0
        else:
            oh[b[y], y] = 1.0
    c["onehot"] = oh
    mk = np.zeros((4, 128, 128), np.float32)
    for q in range(4):
        for g2 in range(2):
            gl = 2 * q + g2
            mk[q, g2 * 64:(g2 + 1) * 64, gl * 16:(gl + 1) * 16] = 1.0
    c["maskk"] = np.ascontiguousarray(mk.transpose(1, 0, 2))
    return c


PARAM_SHAPES = {
    "meta": (16, 2048), "rel_bias": (32, 8), "norm_w": (2, 4, 2048), "w_in": (2, 2048, 12288),
    "conv_w": (2, 4, 1024), "conv_b": (2, 1024), "lru_w_a": (2, 8, 128, 128), "lru_b_a": (2, 1024),
    "lru_w_x": (2, 8, 128, 128), "lru_b_x": (2, 1024), "lru_lambda": (2, 1024), "da_lambda": (2, 4, 64),
    "da_subln": (2, 128), "s5_lam_re": (2, 64, 64), "s5_lam_im": (2, 64, 64), "s5_b_re": (2, 64, 64, 16),
    "s5_b_im": (2, 64, 64, 16), "s5_c_re": (2, 64, 16, 64), "s5_c_im": (2, 64, 16, 64), "s5_d": (2, 1024),
    "s5_log_step": (2, 64), "s5_w_glu": (2, 1024, 1024), "s5_b_glu": (2, 1024), "b_gate": (2, 3, 2048),
    "w_branch": (2, 3, 1024, 2048), "w_out": (2, 2048, 2048), "w_ffn_in": (2, 2048, 11264),
    "w_ffn_out": (2, 5632, 2048),
}


def build(NB=8, depth=DEPTH, dbg=()):
    nc = bass.Bass("TRN2", target_bir_lowering=False)
    T = NMETA + 512 * NB
    NTM = 1 + 4 * NB
    SEQ = 512 * NB

    def cblk(i):
        return (0, 16) if i == 0 else (16 + 512 * (i - 1), 512)

    def tmt(j):
        return (0, 16) if j == 0 else (16 + 128 * (j - 1), 128)

    ins = {}
    ins["x"] = nc.dram_tensor("x", [SEQ, D], F32, kind="ExternalInput").ap()
    for k, shp in PARAM_SHAPES.items():
        ins[k] = nc.dram_tensor(k, list(shp), F32, kind="ExternalInput").ap()
    hc = host_consts()
    for k, v in hc.items():
        ins["c_" + k] = nc.dram_tensor("c_" + k, list(v.shape), F32, kind="ExternalInput").ap()
    out = nc.dram_tensor("out", [SEQ, D], F32, kind="ExternalOutput").ap()

    def scratch(name, shape, dt):
        if name in dbg:
            return nc.dram_tensor(name, shape, dt, kind="ExternalOutput").ap()
        return nc.dram_tensor(name, shape, dt).ap()

    XS = scratch("XS", [T, D], F32)
    HF = scratch("HF", [D, T], BF16)
    AG = scratch("AG", [1024, T], F32)
    AXs = scratch("AXs", [1024, T], F32)
    QF = scratch("QF", [1024, T], BF16)
    KF = scratch("KF", [1024, T], BF16)
    VF = scratch("VF", [1024, T], BF16)
    US = scratch("US", [1024, T], F32)
    GS = scratch("GS", [6144, T], F32)
    YS = scratch("YS", [3072, T], BF16)
    YG = scratch("YG", [1024, T], F32)
    MG = scratch("MG", [D, T], BF16)
    TBD = scratch("TBD", [8, 128, TABR], F32)
    WIN = [scratch("WIN%d" % l, [D, NIN], BF16) for l in range(depth)]
    WBR = [scratch("WBR%d" % l, [3072, D], BF16) for l in range(depth)]
    WOUT = [scratch("WOUT%d" % l, [D, D], BF16) for l in range(depth)]
    WF1 = [scratch("WF1%d" % l, [D, 2 * DFF], BF16) for l in range(depth)]
    WF2 = [scratch("WF2%d" % l, [DFF, D], BF16) for l in range(depth)]
    WGLU = [scratch("WGLU%d" % l, [1024, 1024], BF16) for l in range(depth)]
    LWA = [scratch("LWA%d" % l, [1024, 128], BF16) for l in range(depth)]
    LWX = [scratch("LWX%d" % l, [1024, 128], BF16) for l in range(depth)]

    cx = Ctx(nc)
    op, dma = cx.op, cx.dma
    uid = [0]

    def sbt(name, shape, dt):
        uid[0] += 1
        return nc.sbuf_tensor("%s_u%d" % (name, uid[0]), shape, dt)

    ident = nc.alloc_sbuf_tensor("ident", [128, 128], F32)
    ones16 = nc.alloc_sbuf_tensor("ones16", [128, 128], BF16)
    onesf = nc.alloc_sbuf_tensor("onesf", [128, 128], F32)
    PS = [nc.alloc_psum_tensor("ps%d" % i, [128, 512], F32) for i in range(8)]

    dma(ident[:], ins["c_ident"], writes=["ident"])
    if dbg:
        junk = nc.dram_tensor("junk", [len(ins), 4], F32).ap()
        for i_, (k_, ap_) in enumerate(ins.items()):
            flat = bass.AP(ap_.tensor, 0, [[4, 1], [1, 4]])
            dma(junk[i_:i_ + 1, :], flat, writes=[("junk", i_)])
    op("dve", lambda e: e.memset(ones16[:], 1.0), writes=["ones16"])
    op("dve", lambda e: e.memset(onesf[:], 1.0), writes=["onesf"])

    CVS = 1024

    def conv_w(dst, src, rows, key):
        for r in range(0, rows, CVS):
            n = min(CVS, rows - r)
            dma(dst[r:r + n, :], src[r:r + n, :], writes=[(key, r)], q="cv")
            cx.persist[(key, r)] = cx.lastw[(key, r)]

    def wkeys(key, rows):
        return [(key, r) for r in range(0, rows, CVS)]

    def conv_group(l, names):
        if "noconv" in dbg or l >= depth:
            return
        if "WIN" in names:
            conv_w(WIN[l], ins["w_in"][l], D, "WIN%d" % l)
        if "MIX" in names:
            conv_w(LWA[l], ins["lru_w_a"][l].rearrange("h i j -> (h i) j"), 1024, "LWA%d" % l)
            conv_w(LWX[l], ins["lru_w_x"][l].rearrange("h i j -> (h i) j"), 1024, "LWX%d" % l)
            conv_w(WGLU[l], ins["s5_w_glu"][l], 1024, "WGLU%d" % l)
            conv_w(WBR[l], ins["w_branch"][l].rearrange("b k m -> (b k) m"), 3072, "WBR%d" % l)
        if "WOUT" in names:
            conv_w(WOUT[l], ins["w_out"][l], D, "WOUT%d" % l)
        if "WF1" in names:
            conv_w(WF1[l], ins["w_ffn_in"][l], D, "WF1%d" % l)
        if "WF2" in names:
            conv_w(WF2[l], ins["w_ffn_out"][l], DFF, "WF2%d" % l)

    conv_group(0, ["WIN"])

    dma(XS[0:16, :], ins["meta"], writes=[("XS", 0)])
    for j in range(1, NTM):
        r0, n = tmt(j)
        dma(XS[r0:r0 + n, :], ins["x"][r0 - 16:r0 - 16 + n, :], writes=[("XS", j)])

    def build_table():
        with (sbt("rb", [33, 8], F32) as rb, sbt("oh", [33, TABR], F32) as oh,
              sbt("lh", [33, 128], F32) as lh, sbt("frow", [128, TABR], F32) as frow):
            op("dve", lambda e: e.memset(rb[32:33, :], -30000.0), writes=["rb32"])
            dma(rb[0:32, :], ins["rel_bias"], writes=["rb"])
            dma(oh[:], ins["c_onehot"], writes=["oh"])
            for h in range(8):
                op("dve", lambda e: e.tensor_copy(out=lh[:], in_=rb[:, h:h + 1].to_broadcast([33, 128])),
                   reads=["rb", "rb32"], writes=["lh"])
                for c0 in range(0, TABR, 512):
                    n = min(512, TABR - c0)
                    pk = "ps%d" % (c0 // 512)
                    op("pe", lambda e: e.matmul(PS[c0 // 512][:, :n], lh[:], oh[:, c0:c0 + n], start=True, stop=True),
                       reads=["lh", "oh"], writes=[pk])
                    op("act", lambda e: e.copy(out=frow[:, c0:c0 + n], in_=PS[c0 // 512][:, :n]),
                       reads=[pk], writes=[("frow", c0)])
                dma(TBD[h], frow[:], reads=[("frow", c0) for c0 in range(0, TABR, 512)], writes=[("TBD", h)])
            cx.barrier()

    if "stop:init" not in dbg:
        build_table()

    def norm_tile_to_hf(P, xt, nr, c0, nwcol, rk):
        sq, ss, xn, hT = P["sq"], P["ss"], P["xn"], P["hT"]
        op("act", lambda e: e.activation(out=sq[:nr, :], in_=xt[:nr, :], func=AF.Square), reads=[rk], writes=["sq"])
        op("dve", lambda e: e.tensor_reduce(out=ss[:nr, 0:1], in_=sq[:nr, :], axis=AX.X, op=ALU.add),
           reads=["sq"], writes=["ss"])
        op("dve", lambda e: e.tensor_scalar(out=ss[:nr, 1:2], in0=ss[:nr, 0:1], scalar1=1.0 / D, scalar2=1e-6,
                                            op0=ALU.mult, op1=ALU.add), reads=["ss"], writes=["ss1"])
        op("act", lambda e: e.sqrt(out=ss[:nr, 2:3], in_=ss[:nr, 1:2]), reads=["ss1"], writes=["ss2"])
        op("dve", lambda e: e.reciprocal(out=ss[:nr, 3:4], in_=ss[:nr, 2:3]), reads=["ss2"], writes=["ss3"])
        op("dve", lambda e: e.tensor_scalar(out=xn[:nr, :], in0=xt[:nr, :], scalar1=ss[:nr, 3:4], scalar2=None,
                                            op0=ALU.mult), reads=[rk, "ss3"], writes=["xn"])
        for f4 in range(4):
            pk = "ps%d" % (4 + f4)
            for i in range(4):
                ft = f4 * 4 + i
                op("pe", lambda e: e.transpose(out=PS[4 + f4][:, i * 128:i * 128 + nr],
                                               in_=xn[:nr, ft * 128:(ft + 1) * 128], identity=ident[:nr, :nr]),
                   reads=["xn", "ident"], writes=[pk])
            src = PS[4 + f4][:].rearrange("p (a b) -> p a b", a=4)[:, :, :nr]
            sc = nwcol[:, f4 * 4:(f4 + 1) * 4].unsqueeze(2).to_broadcast([128, 4, nr])
            op("dve", lambda e: e.tensor_tensor(out=hT[:, f4 * 4:(f4 + 1) * 4, :nr], in0=src, in1=sc, op=ALU.mult),
               reads=[pk, "nwcol"], writes=[("hT", f4)])
        dma(HF.rearrange("(a p) t -> p a t", p=128)[:, :, c0:c0 + nr], hT[:, :, :nr],
            reads=[("hT", f4) for f4 in range(4)], writes=[("HF", c0)], q="pool")

    def norm_bufs(st):
        P = {}
        P["sq"] = st.enter_context(sbt("n_sq", [128, D], F32))
        P["ss"] = st.enter_context(sbt("n_ss", [128, 4], F32))
        P["xn"] = st.enter_context(sbt("n_xn", [128, D], F32))
        P["hT"] = st.enter_context(sbt("n_hT", [128, 16, 128], BF16))
        return P

    from contextlib import ExitStack

    def load_col(dst, src_flat, nft, key):
        dma(dst, src_flat.rearrange("(a p) -> p a", p=128), writes=[key], allow_slow_non_contiguous=True)

    def phase_norm(l, which):
        with ExitStack() as st:
            P = norm_bufs(st)
            nwcol = st.enter_context(sbt("nwcol", [128, 16], F32))
            xt = [st.enter_context(sbt("xt%d" % i, [128, D], F32)) for i in range(2)]
            load_col(nwcol[:], ins["norm_w"][l, which], 16, "nwcol")
            for j in range(NTM):
                r0, n = tmt(j)
                s = j % 2
                dma(xt[s][:n, :], XS[r0:r0 + n, :], reads=[("XS", j)], writes=[("xt", s)])
                norm_tile_to_hf(P, xt[s], n, r0, nwcol, ("xt", s))
            cx.barrier()

    def linear_fm(W, wkey, K, M0, M, xsrc, xkey, sblocks, epilogue, st_bufs=None):
        KC = K // 128
        with ExitStack() as st:
            maxc = max(sum(cblk(b)[1] for b in sb) for sb in sblocks)
            hX = st.enter_context(sbt("l_hX", [128, KC, maxc], BF16))
            wt = [st.enter_context(sbt("l_wt%d" % i, [128, KC, 512], BF16)) for i in range(2)]
            wi = 0
            for sb in sblocks:
                c0 = cblk(sb[0])[0]
                ncol = sum(cblk(b)[1] for b in sb)
                dma(hX[:, :, :ncol], xsrc.rearrange("(a p) t -> p a t", p=128)[:, :, c0:c0 + ncol],
                    reads=[(xkey, cc) for cc in range(c0, c0 + ncol, 16)] if False else [xkey], writes=["hX"])
                for m0 in range(M0, M0 + M, 4):
                    s = wi % 2
                    wi += 1
                    nm = min(4, M0 + M - m0)
                    dma(wt[s][:, :, :nm * 128], W.rearrange("(a p) m -> p a m", p=128)[:, :, m0 * 128:(m0 + nm) * 128],
                        reads=wkeys(wkey, K), writes=[("wt", s)])
                    for i in range(nm):
                        ft = m0 + i
                        half = (ft % 2) * 4
                        off = 0
                        for bi, b in enumerate(sb):
                            n = cblk(b)[1]
                            pk = "ps%d" % (half + bi)
                            for kc in range(KC):
                                op("pe", lambda e: e.matmul(PS[half + bi][:, :n], wt[s][:, kc, i * 128:(i + 1) * 128],
                                                            hX[:, kc, off:off + n], start=(kc == 0), stop=(kc == KC - 1)),
                                   reads=[("wt", s), "hX"], writes=[pk])
                            epilogue(ft, bi, (c0 + off, n), PS[half + bi], pk)
                            off += n
            cx.barrier()

    def superblocks(maxb):
        sbs = [[0]]
        b = 1
        while b <= NB:
            sbs.append(list(range(b, min(b + maxb, NB + 1))))
            b += maxb
        return sbs

    def phase_win(l):
        with ExitStack() as st:
            stg = [st.enter_context(sbt("p2_stg%d" % i, [128, 512], F32)) for i in range(4)]
            stg16 = [st.enter_context(sbt("p2_s16%d" % i, [128, 512], BF16)) for i in range(4)]
            bg = st.enter_context(sbt("p2_bg", [128, 48], F32))
            load_col(bg[:], ins["b_gate"][l].rearrange("a b -> (a b)"), 48, "bg")
            cnt = [0]

            def epi(ft, bi, cr, ps, pk):
                c0, n = cr
                s = cnt[0] % 4
                cnt[0] += 1
                if ft < 8:
                    op("act", lambda e: e.activation(out=stg[s][:, :n], in_=ps[:, :n], func=AF.Gelu),
                       reads=[pk], writes=[("stg", s)])
                    dma(AG[ft * 128:(ft + 1) * 128, c0:c0 + n], stg[s][:, :n], reads=[("stg", s)], writes=[("AG", ft, c0)], q="pool")
                elif ft < 16 or 40 <= ft < 48:
                    dst = AXs if ft < 16 else US
                    r = (ft - 8) if ft < 16 else (ft - 40)
                    op("dve", lambda e: e.tensor_copy(out=stg[s][:, :n], in_=ps[:, :n]), reads=[pk], writes=[("stg", s)])
                    dma(dst[r * 128:(r + 1) * 128, c0:c0 + n], stg[s][:, :n], reads=[("stg", s)],
                        writes=[("AU", ft, c0)], q="pool")
                elif ft < 40:
                    dst, r, sc = (QF, ft - 16, 0.125) if ft < 24 else ((KF, ft - 24, 1.0) if ft < 32 else (VF, ft - 32, 1.0))
                    op("dve", lambda e: e.tensor_scalar(out=stg16[s][:, :n], in0=ps[:, :n], scalar1=sc, scalar2=None,
                                                        op0=ALU.mult), reads=[pk], writes=[("s16", s)])
                    dma(dst[r * 128:(r + 1) * 128, c0:c0 + n], stg16[s][:, :n], reads=[("s16", s)], writes=[("QKV", ft, c0)], q="pool")
                else:
                    r = ft - 48
                    op("act", lambda e: e.activation(out=stg[s][:, :n], in_=ps[:, :n], func=AF.Sigmoid,
                                                     bias=bg[:, r:r + 1]), reads=[pk, "bg"], writes=[("stg", s)])
                    dma(GS[r * 128:(r + 1) * 128, c0:c0 + n], stg[s][:, :n], reads=[("stg", s)], writes=[("GS", ft, c0)], q="pool")

            linear_fm(WIN[l], "WIN%d" % l, D, 0, 48, HF, "HFall", superblocks(4), epi)

    def phase_lru(l):
        with ExitStack() as st:
            def tl(name, dt=F32):
                return st.enter_context(sbt(name, [128, T], dt))
            ax, gg, xc, rr, ig, aa, mm = [tl("l_%d" % i) for i in range(7)]
            xc16 = tl("l_xc16", BF16)
            ya16 = tl("l_ya16", BF16)
            wa = st.enter_context(sbt("l_wa", [128, 8, 128], BF16))
            wx = st.enter_context(sbt("l_wx", [128, 8, 128], BF16))
            cw = st.enter_context(sbt("l_cw", [128, 8, 4], F32))
            cb = st.enter_context(sbt("l_cb", [128, 8], F32))
            ba = st.enter_context(sbt("l_ba", [128, 8], F32))
            bx = st.enter_context(sbt("l_bx", [128, 8], F32))
            lam = st.enter_context(sbt("l_lam", [128, 8, 4], F32))
            dma(wa[:], LWA[l].rearrange("(h i) j -> i h j", i=128), reads=wkeys("LWA%d" % l, 1024), writes=["wa"])
            dma(wx[:], LWX[l].rearrange("(h i) j -> i h j", i=128), reads=wkeys("LWX%d" % l, 1024), writes=["wx"])
            for wi_ in range(4):
                load_col(cw[:, :, wi_], ins["conv_w"][l, wi_], 8, ("cw", wi_))
            load_col(cb[:], ins["conv_b"][l], 8, "cb")
            load_col(ba[:], ins["lru_b_a"][l], 8, "ba")
            load_col(bx[:], ins["lru_b_x"][l], 8, "bx")
            load_col(lam[:, :, 0], ins["lru_lambda"][l], 8, "lam0")
            op("act", lambda e: e.activation(out=lam[:, :, 1], in_=lam[:, :, 0], func=AF.Exp, scale=-1.0),
               reads=["lam0"], writes=["lam1"])
            op("act", lambda e: e.activation(out=lam[:, :, 2], in_=lam[:, :, 1], func=AF.Ln, bias=1.0),
               reads=["lam1"], writes=["lam2"])
            op("dve", lambda e: e.tensor_scalar(out=lam[:, :, 1], in0=lam[:, :, 2], scalar1=-8.0, scalar2=None, op0=ALU.mult),
               reads=["lam2"], writes=["c8"])
            op("dve", lambda e: e.tensor_scalar(out=lam[:, :, 3], in0=lam[:, :, 2], scalar1=-16.0, scalar2=None, op0=ALU.mult),
               reads=["lam2"], writes=["c16"])
            for ct in range(8):
                rs = slice(ct * 128, (ct + 1) * 128)
                dma(ax[:], AXs[rs, :], writes=["ax"])
                dma(gg[:], AG[rs, :], writes=["gg"])
                op("dve", lambda e: e.tensor_scalar(out=xc[:], in0=ax[:], scalar1=cw[:, ct, 3:4], scalar2=cb[:, ct:ct + 1],
                                                    op0=ALU.mult, op1=ALU.add), reads=["ax", "cb"] + [("cw", i_) for i_ in range(4)], writes=["xc"])
                for sft in (1, 2, 3):
                    op("dve", lambda e: e.scalar_tensor_tensor(out=xc[:, sft:], in0=ax[:, :T - sft],
                                                               scalar=cw[:, ct, 3 - sft:4 - sft], in1=xc[:, sft:],
                                                               op0=ALU.mult, op1=ALU.add),
                       reads=["ax", "xc"] + [("cw", i_) for i_ in range(4)], writes=["xc"])
                op("act", lambda e: e.copy(out=xc16[:], in_=xc[:]), reads=["xc"], writes=["xc16"])
                for b in range(NB + 1):
                    c0, n = cblk(b)
                    pa, pb = "ps%d" % ((b % 2) * 2), "ps%d" % ((b % 2) * 2 + 1)
                    op("pe", lambda e: e.matmul(PS[(b % 2) * 2][:, :n], wa[:, ct, :], xc16[:, c0:c0 + n], start=True, stop=True),
                       reads=["wa", "xc16"], writes=[pa])
                    op("pe", lambda e: e.matmul(PS[(b % 2) * 2 + 1][:, :n], wx[:, ct, :], xc16[:, c0:c0 + n], start=True, stop=True),
                       reads=["wx", "xc16"], writes=[pb])
                    op("act", lambda e: e.activation(out=rr[:, c0:c0 + n], in_=PS[(b % 2) * 2][:, :n], func=AF.Sigmoid,
                                                     bias=ba[:, ct:ct + 1]), reads=[pa, "ba"], writes=[("rr", b)])
                    op("act", lambda e: e.activation(out=ig[:, c0:c0 + n], in_=PS[(b % 2) * 2 + 1][:, :n], func=AF.Sigmoid,
                                                     bias=bx[:, ct:ct + 1]), reads=[pb, "bx"], writes=[("ig", b)])
                rrk = [("rr", b) for b in range(NB + 1)]
                igk = [("ig", b) for b in range(NB + 1)]
                op("act", lambda e: e.activation(out=aa[:], in_=rr[:], func=AF.Exp, scale=lam[:, ct, 1:2]),
                   reads=rrk + ["c8"], writes=["aa"])
                op("act", lambda e: e.activation(out=mm[:], in_=rr[:], func=AF.Exp, scale=lam[:, ct, 3:4]),
                   reads=rrk + ["c16"], writes=["mm"])
                op("dve", lambda e: e.tensor_scalar(out=mm[:], in0=mm[:], scalar1=-1.0, scalar2=1.0, op0=ALU.mult, op1=ALU.add),
                   reads=["mm"], writes=["mm"])
                op("act", lambda e: e.sqrt(out=mm[:], in_=mm[:]), reads=["mm"], writes=["mm"])
                op("dve", lambda e: e.tensor_tensor(out=ig[:], in0=ig[:], in1=xc[:], op=ALU.mult),
                   reads=igk + ["xc"], writes=["igx"])
                op("dve", lambda e: e.tensor_tensor(out=mm[:], in0=mm[:], in1=ig[:], op=ALU.mult),
                   reads=["mm", "igx"], writes=["mm"])
                op("dve", lambda e: e.tensor_tensor_scan(out=rr[:], data0=aa[:], data1=mm[:], initial=0.0,
                                                         op0=ALU.mult, op1=ALU.add),
                   reads=["aa", "mm"], writes=rrk + ["hh"])
                op("dve", lambda e: e.tensor_tensor(out=ya16[:], in0=rr[:], in1=gg[:], op=ALU.mult),
                   reads=["hh", "gg"], writes=["ya16"])
                dma(YS[rs, :], ya16[:], reads=["ya16"], writes=[("YS", ct)], q="pool")
            cx.barrier()

    def phase_attn(l):
        lam_init = 0.8 - 0.6 * math.exp(-0.3 * l)
        with ExitStack() as st:
            qz = [st.enter_context(sbt("a_qz%d" % i, [128, T], BF16)) for i in range(2)]
            op("pool", lambda e: e.memset(qz[0][:], 0.0), writes=[("qz", 0)])
            op("pool", lambda e: e.memset(qz[1][:], 0.0), writes=[("qz", 1)])
            k16 = st.enter_context(sbt("a_k", [128, T], BF16))
            v16 = st.enter_context(sbt("a_v", [128, T], BF16))
            vT = st.enter_context(sbt("a_vT", [128, NTM, 128], BF16))
            id16 = st.enter_context(sbt("a_id16", [128, 128], BF16))
            pT = [st.enter_context(sbt("a_pT%d" % i, [128, 512], BF16)) for i in range(4)]
            ssb = [st.enter_context(sbt("a_ss%d" % i, [128, 512], F32)) for i in range(3)]
            o0 = st.enter_context(sbt("a_o0", [128, 512], F32))
            o1 = st.enter_context(sbt("a_o1", [128, 512], F32))
            rc = st.enter_context(sbt("a_rc", [128, 512], F32))
            rc2 = st.enter_context(sbt("a_rc2", [128, 512], F32))
            rc3 = st.enter_context(sbt("a_rc3", [128, 512], F32))
            sqb = st.enter_context(sbt("a_sq", [128, 512], F32))
            yb = [st.enter_context(sbt("a_yb%d" % i, [128, 512], BF16)) for i in range(2)]
            dl = st.enter_context(sbt("a_dl", [128, 4, 64], F32))
            dsc = st.enter_context(sbt("a_dsc", [128, 8], F32))
            sw = st.enter_context(sbt("a_sw", [128, 2], F32))
            tb = st.enter_context(sbt("a_tb", [128, 8, TABW], F32))
            for h in range(8):
                src = bass.AP(TBD.tensor, h * 128 * TABR + 127, [[TABR - 1, 128], [1, TABW]])
                dma(tb[:, h, :], src, writes=["tb"])
            dma(dl[:], bass.AP(ins["da_lambda"].tensor, l * 256, [[0, 128], [64, 4], [1, 64]]), writes=["dl"])
            op("dve", lambda e: e.tensor_tensor(out=dl[:, 0, :], in0=dl[:, 0, :], in1=dl[:, 1, :], op=ALU.mult),
               reads=["dl"], writes=["dl0"])
            op("dve", lambda e: e.tensor_tensor(out=dl[:, 2, :], in0=dl[:, 2, :], in1=dl[:, 3, :], op=ALU.mult),
               reads=["dl"], writes=["dl2"])
            op("dve", lambda e: e.tensor_reduce(out=dsc[:, 0:1], in_=dl[:, 0, :], axis=AX.X, op=ALU.add),
               reads=["dl0"], writes=["d0"])
            op("dve", lambda e: e.tensor_reduce(out=dsc[:, 1:2], in_=dl[:, 2, :], axis=AX.X, op=ALU.add),
               reads=["dl2"], writes=["d1"])
            op("act", lambda e: e.activation(out=dsc[:, 2:4], in_=dsc[:, 0:2], func=AF.Exp), reads=["d0", "d1"], writes=["d2"])
            op("dve", lambda e: e.tensor_tensor(out=dsc[:, 4:5], in0=dsc[:, 3:4], in1=dsc[:, 2:3], op=ALU.subtract),
               reads=["d2"], writes=["d4"])
            op("dve", lambda e: e.tensor_scalar(out=dsc[:, 5:6], in0=dsc[:, 4:5], scalar1=-lam_init, scalar2=None, op0=ALU.add),
               reads=["d4"], writes=["neglam"])
            load_col(sw[:, 0:1], ins["da_subln"][l], 1, "sw0")
            op("dve", lambda e: e.tensor_scalar(out=sw[:, 1:2], in0=sw[:, 0:1], scalar1=1.0 - lam_init, scalar2=None, op0=ALU.mult),
               reads=["sw0"], writes=["sw"])
            op("dve", lambda e: e.tensor_copy(out=id16[:], in_=ident[:]), reads=["ident"], writes=["id16"])
            pi = 0
            si = 0
            yi = 0
            pending = []
            for h in range(8):
                rs = slice(h * 128, (h + 1) * 128)
                dma(qz[0][0:64, :], QF[h * 128:h * 128 + 64, :], writes=[("qz", 0)])
                dma(qz[1][64:128, :], QF[h * 128 + 64:h * 128 + 128, :], writes=[("qz", 1)])
                dma(k16[:], KF[rs, :], writes=["k16"])
                dma(v16[:], VF[rs, :], writes=["v16"])
                for j in range(NTM):
                    r0, n = tmt(j)
                    pk = "ps%d" % (7 * (j % 2))
                    pst = PS[7 * (j % 2)][:].bitcast(BF16)
                    op("pe", lambda e: e.transpose(out=pst[:n, 0:128], in_=v16[:, r0:r0 + n], identity=id16[:]),
                       reads=["v16", "id16"], writes=[pk])
                    op("act", lambda e: e.copy(out=vT[:n, j, :], in_=pst[:n, 0:128]), reads=[pk], writes=[("vT", j)])
                for qb in range(NB + 1):
                    q0, nq = cblk(qb)
                    kts = [j for j in range(NTM) if tmt(j)[0] <= q0 + nq - 1]
                    steps = [(c, ji, j) for c in range(2) for ji, j in enumerate(kts)]
                    LA = 2
                    info = {}
                    for idx in range(len(steps) + LA):
                        if (idx == 4 or idx == len(steps) + LA - 1) and pending:
                            pending.pop(0)()
                        if idx < len(steps):
                            c, ji, j = steps[idx]
                            k0, nk = tmt(j)
                            delta = q0 - k0
                            sslot = si % 3
                            pss = PS[sslot]
                            pks = "ps%d" % sslot
                            op("pe", lambda e: e.matmul(pss[:nk, :nq], k16[:, k0:k0 + nk],
                                                        qz[c][:, q0:q0 + nq], start=True, stop=True),
                               reads=["k16", ("qz", c)], writes=[pks])
                            p = pT[pi % 4]
                            pkk = ("pT", pi % 4)
                            if delta < 240:
                                x0 = delta + TABC
                                sb_ = ssb[si % 3]
                                op("dve", lambda e: e.tensor_tensor(out=sb_[:nk, :nq], in0=pss[:nk, :nq],
                                                                    in1=tb[:nk, h, x0:x0 + nq], op=ALU.add),
                                   reads=[pks, "tb"], writes=[("ssb", si % 3)])
                                op("act", lambda e: e.activation(out=p[:nk, :nq], in_=sb_[:nk, :nq], func=AF.Exp),
                                   reads=[("ssb", si % 3)], writes=[pkk])
                            else:
                                op("act", lambda e: e.activation(out=p[:nk, :nq], in_=pss[:nk, :nq], func=AF.Exp,
                                                                 bias=tb[:nk, h, TABW - 1:TABW]),
                                   reads=[pks, "tb"], writes=[pkk])
                            info[idx] = (p, pkk, nk)
                            si += 1
                            pi += 1
                        if idx - LA >= 0:
                            c, ji, j = steps[idx - LA]
                            p, pkk, nk = info.pop(idx - LA)
                            ps_o, ps_r = PS[3 + c], PS[5 + c]
                            ko, kr = "ps%d" % (3 + c), "ps%d" % (5 + c)
                            first, last = ji == 0, ji == len(kts) - 1
                            op("pe", lambda e: e.matmul(ps_o[:, :nq], vT[:nk, j, :], p[:nk, :nq], start=first, stop=last),
                               reads=[("vT", j), pkk], writes=[ko])
                            op("pe", lambda e: e.matmul(ps_r[:, :nq], ones16[:nk, :], p[:nk, :nq], start=first, stop=last),
                               reads=["ones16", pkk], writes=[kr])
                    op("act", lambda e: e.activation(out=rc[:, :nq], in_=PS[5][:, :nq], func=AF.Ln), reads=["ps5"], writes=["rc"])
                    op("act", lambda e: e.activation(out=rc[:, :nq], in_=rc[:, :nq], func=AF.Exp, scale=-1.0), reads=["rc"], writes=["rc"])
                    op("dve", lambda e: e.tensor_tensor(out=o0[:, :nq], in0=PS[3][:, :nq], in1=rc[:, :nq], op=ALU.mult),
                       reads=["ps3", "rc"], writes=["o0"])
                    op("act", lambda e: e.activation(out=rc2[:, :nq], in_=PS[6][:, :nq], func=AF.Ln), reads=["ps6"], writes=["rc2"])
                    op("act", lambda e: e.activation(out=rc2[:, :nq], in_=rc2[:, :nq], func=AF.Exp, scale=-1.0), reads=["rc2"], writes=["rc2"])
                    op("dve", lambda e: e.tensor_tensor(out=o1[:, :nq], in0=PS[4][:, :nq], in1=rc2[:, :nq], op=ALU.mult),
                       reads=["ps4", "rc2"], writes=["o1"])
                    op("dve", lambda e: e.scalar_tensor_tensor(out=o0[:, :nq], in0=o1[:, :nq], scalar=dsc[:, 5:6],
                                                               in1=o0[:, :nq], op0=ALU.mult, op1=ALU.add),
                       reads=["o1", "o0", "neglam"], writes=["o0"])
                    op("dve", lambda e: e.tensor_tensor(out=sqb[:, :nq], in0=o0[:, :nq], in1=o0[:, :nq], op=ALU.mult),
                       reads=["o0"], writes=["sqb"])

                    def part2(h=h, qb=qb, q0=q0, nq=nq):
                        nonlocal yi
                        op("pe", lambda e: e.matmul(PS[7][:, :nq], onesf[:], sqb[:, :nq], start=True, stop=True),
                           reads=["onesf", "sqb"], writes=["ps7"])
                        op("dve", lambda e: e.tensor_scalar(out=rc3[:, :nq], in0=PS[7][:, :nq], scalar1=1.0 / 128, scalar2=1e-5,
                                                            op0=ALU.mult, op1=ALU.add), reads=["ps7"], writes=["rc3"])
                        op("act", lambda e: e.activation(out=rc3[:, :nq], in_=rc3[:, :nq], func=AF.Ln), reads=["rc3"], writes=["rc3"])
                        op("act", lambda e: e.activation(out=rc3[:, :nq], in_=rc3[:, :nq], func=AF.Exp, scale=-0.5),
                           reads=["rc3"], writes=["rc3"])
                        op("dve", lambda e: e.tensor_tensor(out=o0[:, :nq], in0=o0[:, :nq], in1=rc3[:, :nq], op=ALU.mult),
                           reads=["o0", "rc3"], writes=["o0"])
                        y_ = yb[yi % 2]
                        yk = ("yb", yi % 2)
                        yi += 1
                        op("dve", lambda e: e.tensor_scalar(out=y_[:, :nq], in0=o0[:, :nq], scalar1=sw[:, 1:2], scalar2=None,
                                                            op0=ALU.mult), reads=["o0", "sw"], writes=[yk])
                        dma(YS[1024 + h * 128:1024 + (h + 1) * 128, q0:q0 + nq], y_[:, :nq], reads=[yk], writes=[("YS", h, qb)],
                            q="pool")
                    pending.append(part2)
            while pending:
                pending.pop(0)()
            cx.barrier()

    def phase_s5(l):
        NLV = max(1, int(math.ceil(math.log2(T))))
        with ExitStack() as st, ExitStack() as st1:
            def t2(name, shape, dt=F32):
                return st.enter_context(sbt(name, shape, dt))

            def t1(name, shape, dt=F32):
                return st1.enter_context(sbt(name, shape, dt))
            pwr = t2("s_pwr", [128, 32, NLV]); pwi = t2("s_pwi", [128, 32, NLV]); pwn = t2("s_pwn", [128, 32, NLV])
            dcol = t2("s_dcol", [128, 8])
            mag = t2("s_mag", [128, 32])
            ur = t2("s_ur", [128, 32, 8]); ui = t2("s_ui", [128, 32, 8]); nui = t2("s_nui", [128, 32, 8])
            lpr = t2("s_lpr", [128, 32, 8]); lpi = t2("s_lpi", [128, 32, 8]); nlpi = t2("s_nlpi", [128, 32, 8])
            bbr = t2("s_bbr", [128, 32, 16]); bbi = t2("s_bbi", [128, 32, 16])
            mk = t2("s_mk", [128, 4, 128])
            cld = t2("s_cld", [128, 2, 2, 64])
            lr = t1("s_lr", [128, 32]); li = t1("s_li", [128, 32]); stp = t1("s_stp", [128, 32])
            w = [t1("s_w%d" % i, [128, 32]) for i in range(8)]
            cre = t1("s_cre", [128, 32]); cim = t1("s_cim", [128, 32])
            bre = t1("s_bre", [128, 32, 16]); bim = t1("s_bim", [128, 32, 16])
            btmp = t1("s_btmp", [128, 32, 16])
            dma(mk[:], ins["c_maskk"], writes=["mk"])
            load_col(dcol[:], ins["s5_d"][l], 8, "dcol")
            for g2 in range(2):
                ps_ = slice(g2 * 64, (g2 + 1) * 64)
                dma(lr[ps_, :], ins["s5_lam_re"][l].rearrange("(k g) p -> g p k", g=2)[g2], writes=[("lr", g2)],
                    allow_slow_non_contiguous=True)
                dma(li[ps_, :], ins["s5_lam_im"][l].rearrange("(k g) p -> g p k", g=2)[g2], writes=[("li", g2)],
                    allow_slow_non_contiguous=True)
                dma(stp[ps_, :], bass.AP(ins["s5_log_step"].tensor, l * 64 + g2, [[0, 64], [2, 32]]), writes=[("stp", g2)],
                    allow_slow_non_contiguous=True)
                dma(bre[ps_], ins["s5_b_re"][l].rearrange("(k g) p c -> g p k c", g=2)[g2], writes=[("bre", g2)])
                dma(bim[ps_], ins["s5_b_im"][l].rearrange("(k g) p c -> g p k c", g=2)[g2], writes=[("bim", g2)])
            K2 = [("lr", 0), ("lr", 1), ("li", 0), ("li", 1), ("stp", 0), ("stp", 1)]

            def tt(o, a, b, o_, rd, wr):
                op("dve", lambda e: e.tensor_tensor(out=o, in0=a, in1=b, op=o_), reads=rd, writes=wr)

            def ts(o, a, s1, s2, o0_, o1_, rd, wr):
                if o1_ is None:
                    op("dve", lambda e: e.tensor_scalar(out=o, in0=a, scalar1=s1, scalar2=None, op0=o0_), reads=rd, writes=wr)
                else:
                    op("dve", lambda e: e.tensor_scalar(out=o, in0=a, scalar1=s1, scalar2=s2, op0=o0_, op1=o1_), reads=rd, writes=wr)

            op("act", lambda e: e.activation(out=stp[:], in_=stp[:], func=AF.Exp), reads=K2, writes=["step"])
            tt(w[0][:], lr[:], stp[:], ALU.mult, K2 + ["step"], ["w0"])
            op("act", lambda e: e.activation(out=w[0][:], in_=w[0][:], func=AF.Exp), reads=["w0"], writes=["mag"])
            tt(w[1][:], li[:], stp[:], ALU.mult, K2 + ["step"], ["ang"])
            op("act", lambda e: e.activation(out=w[2][:], in_=w[1][:], func=AF.Sin, scale=1.0 / 16), reads=["ang"], writes=["sn"])
            op("act", lambda e: e.activation(out=w[3][:], in_=w[1][:], func=AF.Sin, scale=1.0 / 16, bias=math.pi / 2),
               reads=["ang"], writes=["cs"])
            for it in range(4):
                tt(w[4][:], w[2][:], w[3][:], ALU.mult, ["sn", "cs"], ["sc"])
                tt(w[5][:], w[3][:], w[3][:], ALU.mult, ["cs"], ["cc"])
                tt(w[6][:], w[2][:], w[2][:], ALU.mult, ["sn"], ["s2"])
                ts(w[2][:], w[4][:], 2.0, None, ALU.mult, None, ["sc", "s2"], ["sn"])
                tt(w[3][:], w[5][:], w[6][:], ALU.subtract, ["cc", "s2", "sc"], ["cs"])
            tt(pwr[:, :, 0], w[0][:], w[3][:], ALU.mult, ["mag", "cs"], ["pw"])
            tt(pwi[:, :, 0], w[0][:], w[2][:], ALU.mult, ["mag", "sn"], ["pw"])
            for lv in range(1, NLV):
                tt(w[4][:], pwr[:, :, lv - 1], pwr[:, :, lv - 1], ALU.mult, ["pw"], ["q0"])
                tt(w[5][:], pwi[:, :, lv - 1], pwi[:, :, lv - 1], ALU.mult, ["pw"], ["q1"])
                tt(w[6][:], pwr[:, :, lv - 1], pwi[:, :, lv - 1], ALU.mult, ["pw"], ["q2"])
                tt(pwr[:, :, lv], w[4][:], w[5][:], ALU.subtract, ["q0", "q1"], ["pw"])
                ts(pwi[:, :, lv], w[6][:], 2.0, None, ALU.mult, None, ["q2"], ["pw"])
            ts(pwn[:], pwi[:], -1.0, None, ALU.mult, None, ["pw"], ["pwn"])
            ts(w[4][:], pwr[:, :, 0], -1.0, None, ALU.add, None, ["pw"], ["am1"])
            tt(w[5][:], lr[:], lr[:], ALU.mult, K2, ["d0"])
            tt(w[6][:], li[:], li[:], ALU.mult, K2, ["d1"])
            tt(w[5][:], w[5][:], w[6][:], ALU.add, ["d0", "d1"], ["den"])
            op("dve", lambda e: e.reciprocal(out=w[5][:], in_=w[5][:]), reads=["den"], writes=["rden"])
            tt(w[6][:], w[4][:], lr[:], ALU.mult, ["am1"] + K2, ["e0"])
            tt(w[7][:], pwi[:, :, 0], li[:], ALU.mult, ["pw"] + K2, ["e1"])
            tt(w[6][:], w[6][:], w[7][:], ALU.add, ["e0", "e1"], ["e2"])
            tt(cre[:], w[6][:], w[5][:], ALU.mult, ["e2", "rden"], ["cre"])
            tt(w[6][:], pwi[:, :, 0], lr[:], ALU.mult, ["pw", "e2", "cre"] + K2, ["f0"])
            tt(w[7][:], w[4][:], li[:], ALU.mult, ["am1", "e1", "e2"] + K2, ["f1"])
            tt(w[6][:], w[6][:], w[7][:], ALU.subtract, ["f0", "f1"], ["f2"])
            tt(cim[:], w[6][:], w[5][:], ALU.mult, ["f2", "rden"], ["cim"])
            BK = [("bre", 0), ("bre", 1), ("bim", 0), ("bim", 1)]
            crb = cre[:].unsqueeze(2).to_broadcast([128, 32, 16])
            cib = cim[:].unsqueeze(2).to_broadcast([128, 32, 16])
            tt(bbr[:], bre[:], crb, ALU.mult, BK + ["cre"], ["bbr"])
            tt(btmp[:], bim[:], cib, ALU.mult, BK + ["cim"], ["btmp"])
            tt(bbr[:], bbr[:], btmp[:], ALU.subtract, ["bbr", "btmp"], ["bbr"])
            tt(bbi[:], bim[:], crb, ALU.mult, BK + ["cre"], ["bbi"])
            tt(btmp[:], bre[:], cib, ALU.mult, BK + ["cim", "bbr"], ["btmp"])
            tt(bbi[:], bbi[:], btmp[:], ALU.add, ["bbi", "btmp"], ["bbi"])
            op("dve", lambda e: e.tensor_copy(out=mag[:], in_=w[0][:]), reads=["mag"], writes=["magk"])
            op("dve", lambda e: e.memset(ur[:, :, 0], 1.0), writes=["tab"])
            op("dve", lambda e: e.memset(ui[:, :, 0], 0.0), writes=["tab"])
            op("dve", lambda e: e.tensor_copy(out=lpr[:, :, 0], in_=pwr[:, :, 0]), reads=["pw"], writes=["tab"])
            op("dve", lambda e: e.tensor_copy(out=lpi[:, :, 0], in_=pwi[:, :, 0]), reads=["pw"], writes=["tab"])
            for j in range(1, 8):
                for (tr, ti, mr, mi) in ((ur, ui, w[3], w[2]), (lpr, lpi, pwr[:, :, 0], pwi[:, :, 0])):
                    mr_ = mr[:] if hasattr(mr, "shape") and len(mr.shape) == 2 and not isinstance(mr, bass.AP) else mr
                    mi_ = mi[:] if hasattr(mi, "shape") and len(mi.shape) == 2 and not isinstance(mi, bass.AP) else mi
                    tt(w[4][:], tr[:, :, j - 1], mr_, ALU.mult, ["tab", "cs", "sn", "pw"], ["t4"])
                    tt(w[5][:], ti[:, :, j - 1], mi_, ALU.mult, ["tab", "cs", "sn", "pw"], ["t5"])
                    tt(w[6][:], tr[:, :, j - 1], mi_, ALU.mult, ["tab", "cs", "sn", "pw"], ["t6"])
                    tt(w[7][:], ti[:, :, j - 1], mr_, ALU.mult, ["tab", "cs", "sn", "pw"], ["t7"])
                    tt(tr[:, :, j], w[4][:], w[5][:], ALU.subtract, ["t4", "t5"], ["tab"])
                    tt(ti[:, :, j], w[6][:], w[7][:], ALU.add, ["t6", "t7"], ["tab"])
            ts(nui[:], ui[:], -1.0, None, ALU.mult, None, ["tab"], ["tab"])
            ts(nlpi[:], lpi[:], -1.0, None, ALU.mult, None, ["tab"], ["tab"])
            cx.barrier()
            st1.close()
            NCH = T // 8
            pieces = []
            c_ = 0
            while c_ < NCH:
                n_ = min(NCH - c_, 257 if NCH > 512 else 512)
                pieces.append((c_, n_))
                c_ += n_
            NLC = 0
            while (1 << NLC) < NCH:
                NLC += 1
            u32 = t2("s_u32", [128, T]); u16 = t2("s_u16", [128, T], BF16); ybuf = t2("s_ybuf", [128, T])
            Bps = [[t2("s_Bp%d_%d" % (b_, i), [128, T]) for i in range(2)] for b_ in range(2)]
            g16 = [t2("s_g%d" % i, [128, T], BF16) for i in range(2)]
            dec = t2("s_dec", [128, T])
            Z = [t2("s_Z%d" % i, [128, NCH]) for i in range(4)]
            H16 = [t2("s_H%d" % i, [128, NCH], BF16) for i in range(2)]
            cstfs = [t2("s_cstf%d" % i, [128, 4, 2, 128]) for i in range(2)]
            rot = t2("s_rot", [128, 2, 8, 16]); rta = t2("s_rta", [128, 8, 16]); rtb = t2("s_rtb", [128, 8, 16])
            bpre8 = t2("s_bpre8", [128, 8, 128])
            bstj = t2("s_bstj", [128, 8, 2, 128], BF16)
            mda = t2("s_mda", [128, 8, 128]); mdb = t2("s_mdb", [128, 8, 128])
            MD = t2("s_MD", [128, 4, 8, 128], BF16)

            def v8(ap):
                return ap.rearrange("p (c j) -> p c j", j=8)
            op("pool", lambda e: e.memset(dec[:], 0.0), writes=["dec"])
            op("pool", lambda e: e.memset(H16[0][:, 0:1], 0.0), writes=[("H16", 0)])
            op("pool", lambda e: e.memset(H16[1][:, 0:1], 0.0), writes=[("H16", 1)])
            ybk = [("ybuf", b) for b in range(len(pieces) * 8)]
            pcnt = 0
            pcn = [0]

            def ct_begin(ct):
                rs = slice(ct * 128, (ct + 1) * 128)
                cstf = cstfs[ct % 2]
                dma(u32[:], US[rs, :], writes=["u32"])
                op("act", lambda e: e.copy(out=u16[:], in_=u32[:]), reads=["u32"], writes=["u16"])
                for ri, cs_ in enumerate((ins["s5_c_re"], ins["s5_c_im"])):
                    srcc = cs_[l].rearrange("(t g) c p -> t (g c) p", g=8)[ct]
                    for dup in range(2):
                        dma(cld[:, ri, dup, :], srcc, writes=[("cld", ri, dup)])
                for ri in range(2):
                    pk = "ps%d" % (6 + ri)
                    op("pe", lambda e: e.transpose(out=PS[6 + ri][:, 0:128], in_=cld[:, ri].rearrange("p a b -> p (a b)"),
                                                   identity=ident[:]),
                       reads=[("cld", ri, 0), ("cld", ri, 1), "ident"], writes=[pk])
                    for q in range(4):
                        if ri == 0:
                            tt(cstf[:, q, 0, :], PS[6][:, 0:128], mk[:, q, :], ALU.mult, [pk, "mk"], [("cstf", ct % 2, q)])
                        else:
                            op("dve", lambda e: e.scalar_tensor_tensor(out=cstf[:, q, 1, :], in0=PS[7][:, 0:128], scalar=-1.0,
                                                                       in1=mk[:, q, :], op0=ALU.mult, op1=ALU.mult),
                               reads=[pk, "mk"], writes=[("cstf", ct % 2, q)])

            def stageA(k):
                ct, q = k // 4, k % 4
                Bpk = Bps[k % 2]
                bk = k % 2
                bbr_b = bbr[:, k, :].unsqueeze(1).to_broadcast([128, 8, 16])
                bbi_b = bbi[:, k, :].unsqueeze(1).to_broadcast([128, 8, 16])
                ur_b = ur[:, k, :].unsqueeze(2).to_broadcast([128, 8, 16])
                ui_b = ui[:, k, :].unsqueeze(2).to_broadcast([128, 8, 16])
                tt(rta[:], bbr_b, ur_b, ALU.mult, ["bbr", "tab"], ["rta"])
                tt(rtb[:], bbi_b, ui_b, ALU.mult, ["bbi", "tab"], ["rtb"])
                tt(rot[:, 0], rta[:], rtb[:], ALU.add, ["rta", "rtb"], [("rot", 0)])
                tt(rta[:], bbi_b, ur_b, ALU.mult, ["bbi", "tab", ("rot", 0)], ["rta"])
                tt(rtb[:], bbr_b, ui_b, ALU.mult, ["bbr", "tab", ("rot", 0)], ["rtb"])
                tt(rot[:, 1], rta[:], rtb[:], ALU.subtract, ["rta", "rtb"], [("rot", 1)])
                for ri in range(2):
                    src = rot[:, ri].unsqueeze(2).to_broadcast([128, 8, 8, 16])
                    msk = mk[:, k % 4, :].rearrange("p (a c) -> p a c", a=8).unsqueeze(1).to_broadcast([128, 8, 8, 16])
                    tt(bpre8[:].rearrange("p j (a c) -> p j a c", a=8), src, msk, ALU.mult, [("rot", ri), "mk"], ["bpre8"])
                    for j4 in range(2):
                        pidx = 6 + j4
                        pk = "ps%d" % pidx
                        for jj in range(4):
                            op("pe", lambda e: e.transpose(out=PS[pidx][:, jj * 128:(jj + 1) * 128], in_=bpre8[:, j4 * 4 + jj, :],
                                                           identity=ident[:]), reads=["bpre8", "ident"], writes=[pk])
                        op("act", lambda e: e.copy(out=bstj[:, j4 * 4:(j4 + 1) * 4, ri, :],
                                                   in_=PS[pidx][:].rearrange("p (a b) -> p a b", a=4)),
                           reads=[pk], writes=[("bstj", ri, j4)])
                bstk = [("bstj", ri, j4) for ri in range(2) for j4 in range(2)]
                for j in range(8):
                    for ri in range(2):
                        for (pc0, pn) in pieces:
                            pidx = pcn[0] % 4
                            pcn[0] += 1
                            pk = "ps%d" % pidx
                            op("pe", lambda e: e.matmul(PS[pidx][:, :pn], bstj[:, j, ri, :], v8(u16[:])[:, pc0:pc0 + pn, j],
                                                        start=True, stop=True), reads=bstk + ["u16"], writes=[pk])
                            op("act", lambda e: e.copy(out=v8(Bpk[ri][:])[:, pc0:pc0 + pn, j], in_=PS[pidx][:, :pn]),
                               reads=[pk], writes=[("Bp", bk, ri)])

            def stageB(k):
                ct, q = k // 4, k % 4
                Bpk = Bps[k % 2]
                bk = k % 2
                cstf = cstfs[ct % 2]
                c0b = cstf[:, q, 0, :].unsqueeze(1).to_broadcast([128, 8, 128])
                c1b = cstf[:, q, 1, :].unsqueeze(1).to_broadcast([128, 8, 128])

                def tb_(tab):
                    return tab[:, k, :].unsqueeze(2).to_broadcast([128, 8, 128])
                for i_, (ta, tb2) in enumerate(((ur, ui), (nui, ur), (lpr, lpi), (nlpi, lpr))):
                    op("pool", lambda e: e.tensor_tensor(out=mda[:], in0=c0b, in1=tb_(ta), op=ALU.mult),
                       reads=[("cstf", ct % 2, q), "tab"], writes=["mda"])
                    op("pool", lambda e: e.tensor_tensor(out=mdb[:], in0=c1b, in1=tb_(tb2), op=ALU.mult),
                       reads=[("cstf", ct % 2, q), "tab"], writes=["mdb"])
                    op("pool", lambda e: e.tensor_tensor(out=MD[:, i_], in0=mda[:], in1=mdb[:], op=ALU.add),
                       reads=["mda", "mdb"], writes=[("MD", i_)])
                op("pool", lambda e: e.tensor_scalar(out=dec[:], in0=dec[:], scalar1=0.0, scalar2=mag[:, k:k + 1],
                                                     op0=ALU.mult, op1=ALU.add), reads=["dec", "magk"], writes=["dec"])
                op("pool", lambda e: e.memset(v8(dec[:])[:, :, 0], 0.0), reads=["dec"], writes=["dec"])
                for ri in range(2):
                    op("dve", lambda e: e.tensor_tensor_scan(out=g16[ri][:], data0=dec[:], data1=Bpk[ri][:], initial=0.0,
                                                             op0=ALU.mult, op1=ALU.add),
                       reads=["dec", ("Bp", bk, ri)], writes=[("g16", ri)])
                g7r, g7i = v8(g16[0][:])[:, :, 7], v8(g16[1][:])[:, :, 7]
                G2 = [("g16", 0), ("g16", 1)]
                ts(Z[0][:], g7r, ur[:, k, 7:8], None, ALU.mult, None, G2 + ["tab"], [("Z", 0)])
                op("dve", lambda e: e.scalar_tensor_tensor(out=Z[0][:], in0=g7i, scalar=nui[:, k, 7:8], in1=Z[0][:],
                                                           op0=ALU.mult, op1=ALU.add), reads=G2 + ["tab", ("Z", 0)], writes=[("Z", 0)])
                ts(Z[1][:], g7r, ui[:, k, 7:8], None, ALU.mult, None, G2 + ["tab"], [("Z", 1)])
                op("dve", lambda e: e.scalar_tensor_tensor(out=Z[1][:], in0=g7i, scalar=ur[:, k, 7:8], in1=Z[1][:],
                                                           op0=ALU.mult, op1=ALU.add), reads=G2 + ["tab", ("Z", 1)], writes=[("Z", 1)])
                cur = 0
                for m in range(NLC):
                    d = 1 << m
                    lv = 3 + m
                    sr, si_ = Z[cur * 2], Z[cur * 2 + 1]
                    dr, di = Z[(1 - cur) * 2], Z[(1 - cur) * 2 + 1]
                    ks = [("Z", cur * 2), ("Z", cur * 2 + 1)]
                    kd0, kd1 = ("Z", (1 - cur) * 2), ("Z", (1 - cur) * 2 + 1)
                    op("dve", lambda e: e.scalar_tensor_tensor(out=dr[:, d:], in0=sr[:, :NCH - d], scalar=pwr[:, k, lv:lv + 1],
                                                               in1=sr[:, d:], op0=ALU.mult, op1=ALU.add),
                       reads=ks + ["pw"], writes=[kd0])
                    op("dve", lambda e: e.scalar_tensor_tensor(out=dr[:, d:], in0=si_[:, :NCH - d], scalar=pwn[:, k, lv:lv + 1],
                                                               in1=dr[:, d:], op0=ALU.mult, op1=ALU.add),
                       reads=ks + ["pwn", kd0], writes=[kd0])
                    op("dve", lambda e: e.scalar_tensor_tensor(out=di[:, d:], in0=si_[:, :NCH - d], scalar=pwr[:, k, lv:lv + 1],
                                                               in1=si_[:, d:], op0=ALU.mult, op1=ALU.add),
                       reads=ks + ["pw"], writes=[kd1])
                    op("dve", lambda e: e.scalar_tensor_tensor(out=di[:, d:], in0=sr[:, :NCH - d], scalar=pwi[:, k, lv:lv + 1],
                                                               in1=di[:, d:], op0=ALU.mult, op1=ALU.add),
                       reads=ks + ["pw", kd1], writes=[kd1])
                    op("dve", lambda e: e.tensor_copy(out=dr[:, :d], in_=sr[:, :d]), reads=ks, writes=[kd0])
                    op("dve", lambda e: e.tensor_copy(out=di[:, :d], in_=si_[:, :d]), reads=ks, writes=[kd1])
                    cur = 1 - cur
                for ri in range(2):
                    op("act", lambda e: e.copy(out=H16[ri][:, 1:NCH], in_=Z[cur * 2 + ri][:, 0:NCH - 1]),
                       reads=[("Z", cur * 2 + ri)], writes=[("H16", ri)])
                for j in range(8):
                    for pi_, (pc0, pn) in enumerate(pieces):
                        pidx = 4 + pcn[0] % 2
                        pcn[0] += 1
                        pk = "ps%d" % pidx
                        rhs = [v8(g16[0][:])[:, pc0:pc0 + pn, j], v8(g16[1][:])[:, pc0:pc0 + pn, j],
                               H16[0][:, pc0:pc0 + pn], H16[1][:, pc0:pc0 + pn]]
                        rk = [("g16", 0), ("g16", 1), ("H16", 0), ("H16", 1)]
                        for i_ in range(4):
                            op("pe", lambda e: e.matmul(PS[pidx][:, :pn], MD[:, i_, j, :], rhs[i_], start=(i_ == 0), stop=(i_ == 3)),
                               reads=[("MD", i_), rk[i_]], writes=[pk])
                        yv = v8(ybuf[:])[:, pc0:pc0 + pn, j]
                        yk = ("ybuf", j * len(pieces) + pi_)
                        if q == 0:
                            op("dve", lambda e: e.scalar_tensor_tensor(out=yv, in0=v8(u32[:])[:, pc0:pc0 + pn, j],
                                                                       scalar=dcol[:, ct:ct + 1], in1=PS[pidx][:, :pn],
                                                                       op0=ALU.mult, op1=ALU.add),
                               reads=[pk, "u32", "dcol"], writes=[yk])
                        else:
                            tt(yv, yv, PS[pidx][:, :pn], ALU.add, [pk, yk], [yk])

            def ct_end(ct):
                rs = slice(ct * 128, (ct + 1) * 128)
                op("act", lambda e: e.activation(out=ybuf[:], in_=ybuf[:], func=AF.Gelu), reads=ybk, writes=ybk)
                dma(YG[rs, :], ybuf[:], reads=ybk, writes=[("YG", ct)], q="pool")

            for k in range(32):
                if k % 4 == 0:
                    ct_begin(k // 4)
                stageA(k)
                if k >= 1:
                    stageB(k - 1)
                    if (k - 1) % 4 == 3:
                        ct_end((k - 1) // 4)
            stageB(31)
            ct_end(7)
            cx.barrier()
        with ExitStack() as st:
            bgl = st.enter_context(sbt("g_b", [128, 8], F32))
            yg32 = st.enter_context(sbt("g_y32", [128, 8, 512], F32))
            yg16 = st.enter_context(sbt("g_y16", [128, 8, 512], BF16))
            wg = st.enter_context(sbt("g_w", [128, 8, 1024], BF16))
            sg = st.enter_context(sbt("g_sg", [128, 512], F32))
            yc = [st.enter_context(sbt("g_yc%d" % i, [128, 512], BF16)) for i in range(2)]
            load_col(bgl[:], ins["s5_b_glu"][l], 8, "bgl")
            dma(wg[:], WGLU[l].rearrange("(a p) m -> p a m", p=128), reads=wkeys("WGLU%d" % l, 1024), writes=["wg"])
            oi = 0
            for b in range(NB + 1):
                c0, n = cblk(b)
                dma(yg32[:, :, :n], YG.rearrange("(a p) t -> p a t", p=128)[:, :, c0:c0 + n], writes=["yg32"])
                op("act", lambda e: e.copy(out=yg16[:, :, :n], in_=yg32[:, :, :n]), reads=["yg32"], writes=["yg16"])
                for ft in range(8):
                    pidx = ft % 2
                    pk = "ps%d" % pidx
                    for kc in range(8):
                        op("pe", lambda e: e.matmul(PS[pidx][:, :n], wg[:, kc, ft * 128:(ft + 1) * 128], yg16[:, kc, :n],
                                                    start=(kc == 0), stop=(kc == 7)), reads=["wg", "yg16"], writes=[pk])
                    op("act", lambda e: e.activation(out=sg[:, :n], in_=PS[pidx][:, :n], func=AF.Sigmoid, bias=bgl[:, ft:ft + 1]),
                       reads=[pk, "bgl"], writes=["sg"])
                    y_ = yc[oi % 2]
                    yk = ("yc", oi % 2)
                    oi += 1
                    op("dve", lambda e: e.tensor_tensor(out=y_[:, :n], in0=sg[:, :n], in1=yg32[:, ft, :n], op=ALU.mult),
                       reads=["sg", "yg32"], writes=[yk])
                    dma(YS[2048 + ft * 128:2048 + (ft + 1) * 128, c0:c0 + n], y_[:, :n], reads=[yk], writes=[("YS", ft, b)], q="pool")
            cx.barrier()

    def phase_branch(l):
        with ExitStack() as st:
            hX = st.enter_context(sbt("b_hX", [128, 16, 1024], BF16))
            yX = st.enter_context(sbt("b_yX", [128, 24, 1024], BF16))
            wt = [st.enter_context(sbt("b_wt%d" % i, [128, 24, 128], BF16)) for i in range(2)]
            wg = [st.enter_context(sbt("b_wg%d" % i, [128, 3, 16, 128], BF16)) for i in range(2)]
            sg = [st.enter_context(sbt("b_sg%d" % i, [128, 512], F32)) for i in range(3)]
            m0 = st.enter_context(sbt("b_m0", [128, 512], F32))
            m1 = st.enter_context(sbt("b_m1", [128, 512], F32))
            mg = [st.enter_context(sbt("b_mg%d" % i, [128, 512], BF16)) for i in range(2)]
            bg = st.enter_context(sbt("b_bg", [128, 48], F32))
            load_col(bg[:], ins["b_gate"][l].rearrange("a b -> (a b)"), 48, "bg")
            Wv = WIN[l].rearrange("(a p) m -> p a m", p=128)
            it = 0
            pc = 0
            mi = 0
            for sb in superblocks(2):
                c0 = cblk(sb[0])[0]
                ncol = sum(cblk(b)[1] for b in sb)
                dma(hX[:, :, :ncol], HF.rearrange("(a p) t -> p a t", p=128)[:, :, c0:c0 + ncol], writes=["hX"])
                dma(yX[:, :, :ncol], YS.rearrange("(a p) t -> p a t", p=128)[:, :, c0:c0 + ncol], writes=["yX"])
                for ft in range(16):
                    s = it % 2
                    it += 1
                    dma(wt[s][:], WBR[l].rearrange("(a p) m -> p a m", p=128)[:, :, ft * 128:(ft + 1) * 128],
                        reads=wkeys("WBR%d" % l, 3072), writes=[("wt", s)])
                    for br in range(3):
                        g0 = 6144 + br * 2048 + ft * 128
                        dma(wg[s][:, br], Wv[:, :, g0:g0 + 128], reads=wkeys("WIN%d" % l, D), writes=[("wg", s, br)])
                    off = 0
                    for b in sb:
                        n = cblk(b)[1]
                        for br in range(3):
                            pg, pb = pc % 8, (pc + 1) % 8
                            pc += 2
                            for kc in range(16):
                                op("pe", lambda e: e.matmul(PS[pg][:, :n], wg[s][:, br, kc, :], hX[:, kc, off:off + n],
                                                            start=(kc == 0), stop=(kc == 15)),
                                   reads=[("wg", s, br), "hX"], writes=["ps%d" % pg])
                            for kc in range(8):
                                op("pe", lambda e: e.matmul(PS[pb][:, :n], wt[s][:, br * 8 + kc, :], yX[:, br * 8 + kc, off:off + n],
                                                            start=(kc == 0), stop=(kc == 7)),
                                   reads=[("wt", s), "yX"], writes=["ps%d" % pb])
                            r = br * 16 + ft
                            if "noepi" in dbg:
                                continue
                            op("act", lambda e: e.activation(out=sg[br][:, :n], in_=PS[pg][:, :n], func=AF.Sigmoid,
                                                             bias=bg[:, r:r + 1]), reads=["ps%d" % pg, "bg"], writes=[("sg", br)])
                            if "nodve" in dbg:
                                continue
                            if br == 0:
                                op("dve", lambda e: e.tensor_tensor(out=m0[:, :n], in0=PS[pb][:, :n], in1=sg[br][:, :n], op=ALU.mult),
                                   reads=["ps%d" % pb, ("sg", br)], writes=["m0"])
                            else:
                                op("dve", lambda e: e.tensor_tensor(out=m1[:, :n], in0=PS[pb][:, :n], in1=sg[br][:, :n], op=ALU.mult),
                                   reads=["ps%d" % pb, ("sg", br)], writes=["m1"])
                                if br == 1:
                                    op("dve", lambda e: e.tensor_tensor(out=m0[:, :n], in0=m0[:, :n], in1=m1[:, :n], op=ALU.add),
                                       reads=["m0", "m1"], writes=["m0"])
                        ms = mi % 2
                        mi += 1
                        if "noepi" in dbg or "nodve" in dbg:
                            off += n
                            continue
                        op("dve", lambda e: e.tensor_tensor(out=mg[ms][:, :n], in0=m0[:, :n], in1=m1[:, :n], op=ALU.add),
                           reads=["m0", "m1"], writes=[("mg", ms)])
                        if "nomgdma" not in dbg:
                            dma(MG[ft * 128:(ft + 1) * 128, c0 + off:c0 + off + n], mg[ms][:, :n], reads=[("mg", ms)],
                                writes=[("MG", ft, b)], q="pool")
                        off += n
            cx.barrier()

    def tm_epilogue(P, mixsrc, mixkeys, j, nwb, nwcol_next, xt, xk, last_out):
        r0, n = tmt(j)
        sq, ss, tmp = P["esq"], P["ss2"], P["esq"]
        dma(xt[:n, :], XS[r0:r0 + n, :], reads=[("XS", j)], writes=[xk], q="pool")
        op("act", lambda e: e.activation(out=sq[:n, :], in_=mixsrc[:n, :], func=AF.Square), reads=mixkeys, writes=["esq"])
        op("dve", lambda e: e.tensor_reduce(out=ss[:n, 0:1], in_=sq[:n, :], axis=AX.X, op=ALU.add), reads=["esq"], writes=["e0"])
        op("dve", lambda e: e.tensor_scalar(out=ss[:n, 1:2], in0=ss[:n, 0:1], scalar1=1.0 / D, scalar2=1e-6,
                                            op0=ALU.mult, op1=ALU.add), reads=["e0"], writes=["e1"])
        op("act", lambda e: e.sqrt(out=ss[:n, 2:3], in_=ss[:n, 1:2]), reads=["e1"], writes=["e2"])
        op("dve", lambda e: e.reciprocal(out=ss[:n, 3:4], in_=ss[:n, 2:3]), reads=["e2"], writes=["e3"])
        op("dve", lambda e: e.tensor_scalar(out=tmp[:n, :], in0=mixsrc[:n, :], scalar1=ss[:n, 3:4], scalar2=None, op0=ALU.mult),
           reads=mixkeys + ["e3", "esq"], writes=["esq"])
        op("pool", lambda e: e.tensor_tensor(out=tmp[:n, :], in0=tmp[:n, :], in1=nwb[:n, :], op=ALU.mult),
           reads=["esq", "nwb"], writes=["esq"])
        op("pool", lambda e: e.tensor_tensor(out=xt[:n, :], in0=xt[:n, :], in1=tmp[:n, :], op=ALU.add),
           reads=["esq", xk], writes=[xk])
        if last_out:
            if j >= 1:
                dma(out[r0 - 16:r0 - 16 + n, :], xt[:n, :], reads=[xk], writes=[("OUT", j)], q="pool")
        else:
            dma(XS[r0:r0 + n, :], xt[:n, :], reads=[xk], writes=[("XS", j)], q="pool")
            if nwcol_next is not None:
                norm_tile_to_hf(P, xt, n, r0, nwcol_next, xk)

    def epi_bufs(st):
        P = norm_bufs(st)
        P["ss2"] = st.enter_context(sbt("e_ss", [128, 4], F32))
        P["esq"] = st.enter_context(sbt("e_sq", [128, D], F32))
        return P

    def load_bcast_row(dst, src_row, key):
        dma(dst, bass.AP(src_row.tensor, src_row.offset, [[0, 128], [1, D]]), writes=[key])

    def phase_wout(l):
        with ExitStack() as st:
            P = epi_bufs(st)
            wo = st.enter_context(sbt("o_w", [128, 16, 1024], BF16))
            mT = [st.enter_context(sbt("o_m%d" % i, [128, 16, 128], BF16)) for i in range(2)]
            mixs = [st.enter_context(sbt("o_mix%d" % i, [128, D], F32)) for i in range(2)]
            nwb = st.enter_context(sbt("o_nwb", [128, D], F32))
            nwc = st.enter_context(sbt("o_nwc", [128, 16], F32))
            xt = [st.enter_context(sbt("o_xt%d" % i, [128, D], F32)) for i in range(2)]
            load_bcast_row(nwb[:], ins["norm_w"][l, 1], "nwb")
            load_col(nwc[:], ins["norm_w"][l, 2], 16, "nwcol")
            for half in range(2):
                dma(wo[:], WOUT[l].rearrange("(a p) m -> p a m", p=128)[:, :, half * 1024:(half + 1) * 1024],
                    reads=wkeys("WOUT%d" % l, D), writes=["wo"])
                break
            wo2 = st.enter_context(sbt("o_w2", [128, 16, 1024], BF16))
            dma(wo2[:], WOUT[l].rearrange("(a p) m -> p a m", p=128)[:, :, 1024:2048],
                reads=wkeys("WOUT%d" % l, D), writes=["wo2"])
            for j in range(NTM):
                r0, n = tmt(j)
                s = j % 2
                dma(mT[s][:, :, :n], MG.rearrange("(a p) t -> p a t", p=128)[:, :, r0:r0 + n], writes=[("mT", s)])
                for fb in range(4):
                    wsrc, wk = (wo, "wo") if fb < 2 else (wo2, "wo2")
                    pidx = fb
                    pk = "ps%d" % pidx
                    for kc in range(16):
                        op("pe", lambda e: e.matmul(PS[pidx][:n, :], mT[s][:, kc, :n], wsrc[:, kc, (fb % 2) * 512:(fb % 2 + 1) * 512],
                                                    start=(kc == 0), stop=(kc == 15)), reads=[("mT", s), wk], writes=[pk])
                    op("act", lambda e: e.copy(out=mixs[s][:n, fb * 512:(fb + 1) * 512], in_=PS[pidx][:n, :]),
                       reads=[pk], writes=[("mix", s, fb)])
                tm_epilogue(P, mixs[s], [("mix", s, fb) for fb in range(4)], j, nwb, nwc, xt[s], ("xt", s), False)
            cx.barrier()

    def phase_ffn(l, last):
        with ExitStack() as st:
            P = epi_bufs(st)
            hX = st.enter_context(sbt("f_hX", [128, 16, 512], BF16))
            act16 = st.enter_context(sbt("f_act", [128, 44, 512], BF16))
            w1 = [st.enter_context(sbt("f_w1%d" % i, [128, 2, 16, 128], BF16)) for i in range(2)]
            w2 = [st.enter_context(sbt("f_w2%d" % i, [128, 44, 256], BF16)) for i in range(2)]
            sl = st.enter_context(sbt("f_sl", [128, 512], F32))
            mix = [st.enter_context(sbt("f_mix%d" % i, [128, D], F32)) for i in range(4)]
            nwb = st.enter_context(sbt("f_nwb", [128, D], F32))
            nwc = st.enter_context(sbt("f_nwc", [128, 16], F32))
            xt = st.enter_context(sbt("f_xt", [128, D], F32))
            load_bcast_row(nwb[:], ins["norm_w"][l, 3], "nwb")
            if not last:
                load_col(nwc[:], ins["norm_w"][l + 1, 0], 16, "nwcol")
            W1v = WF1[l].rearrange("(a p) m -> p a m", p=128)
            W2v = WF2[l].rearrange("(a p) m -> p a m", p=128)
            i1 = 0
            i2 = 0
            for b in range(NB + 1):
                c0, n = cblk(b)
                dma(hX[:, :, :n], HF.rearrange("(a p) t -> p a t", p=128)[:, :, c0:c0 + n], reads=["HFall"], writes=["hX"])
                for ft in range(44):
                    s = i1 % 2
                    i1 += 1
                    dma(w1[s][:, 0], W1v[:, :, ft * 128:(ft + 1) * 128], reads=wkeys("WF1%d" % l, D), writes=[("w1", s, 0)])
                    dma(w1[s][:, 1], W1v[:, :, DFF + ft * 128:DFF + (ft + 1) * 128], reads=wkeys("WF1%d" % l, D),
                        writes=[("w1", s, 1)])
                    pg, pu = (ft % 2) * 2, (ft % 2) * 2 + 1
                    for gu, pidx in ((0, pg), (1, pu)):
                        for kc in range(16):
                            op("pe", lambda e: e.matmul(PS[pidx][:, :n], w1[s][:, gu, kc, :], hX[:, kc, :n],
                                                        start=(kc == 0), stop=(kc == 15)),
                               reads=[("w1", s, gu), "hX"], writes=["ps%d" % pidx])
                    op("act", lambda e: e.activation(out=sl[:, :n], in_=PS[pg][:, :n], func=AF.Silu),
                       reads=["ps%d" % pg], writes=["sl"])
                    op("dve", lambda e: e.tensor_tensor(out=act16[:, ft, :n], in0=sl[:, :n], in1=PS[pu][:, :n], op=ALU.mult),
                       reads=["sl", "ps%d" % pu], writes=[("act", ft)])
                ntile = 1 if b == 0 else 4
                for fq in range(8):
                    s = i2 % 2
                    i2 += 1
                    dma(w2[s][:], W2v[:, :, fq * 256:(fq + 1) * 256], reads=wkeys("WF2%d" % l, DFF), writes=[("w2", s)])
                    for tt_ in range(ntile):
                        nt = 16 if b == 0 else 128
                        pidx = 4 + (fq * ntile + tt_) % 4
                        pk = "ps%d" % pidx
                        for kc in range(44):
                            op("pe", lambda e: e.matmul(PS[pidx][:nt, :256], act16[:, kc, tt_ * 128:tt_ * 128 + nt], w2[s][:, kc, :],
                                                        start=(kc == 0), stop=(kc == 43)),
                               reads=[("act", kc), ("w2", s)], writes=[pk])
                        op("act", lambda e: e.copy(out=mix[tt_][:nt, fq * 256:(fq + 1) * 256], in_=PS[pidx][:nt, :256]),
                           reads=[pk], writes=[("mix", tt_, fq)])
                for tt_ in range(ntile):
                    j = 0 if b == 0 else 1 + 4 * (b - 1) + tt_
                    tm_epilogue(P, mix[tt_], [("mix", tt_, fq) for fq in range(8)], j, nwb, None if last else nwc,
                                xt, "xt", last)
            cx.barrier()

    stop_after = None
    for d_ in dbg:
        if d_.startswith("stop:"):
            stop_after = d_[5:]
    for l in range(depth):
        last = l == depth - 1
        if stop_after in ("init", "table"):
            break
        if l == 0:
            phase_norm(l, 0)
            conv_group(0, ["MIX"])
        if stop_after == "norm":
            break
        phase_win(l)
        if l == 0:
            conv_group(0, ["WOUT", "WF1", "WF2"])
        if stop_after == "win":
            break
        phase_lru(l)
        conv_group(l + 1, ["WIN"])
        if stop_after == "lru":
            break
        phase_attn(l)
        conv_group(l + 1, ["MIX"])
        if stop_after == "attn":
            break
        phase_s5(l)
        conv_group(l + 1, ["WOUT", "WF1"])
        if stop_after == "s5":
            break
        phase_branch(l)
        conv_group(l + 1, ["WF2"])
        if stop_after == "branch":
            break
        phase_wout(l)
        if stop_after == "wout":
            break
        phase_ffn(l, last)
    cx.barrier(final=True)
    return nc, hc


_CACHE = {}


def kernel(**inputs):
    x = np.ascontiguousarray(np.asarray(inputs["x"], dtype=np.float32))
    B = x.shape[0]
    if "nc" not in _CACHE:
        _CACHE["nc"] = build(NB=x.shape[1] // 512)
    nc, hc = _CACHE["nc"]
    base = {k: np.ascontiguousarray(np.asarray(inputs[k], dtype=np.float32)) for k in PARAM_SHAPES}
    for k, v in hc.items():
        base["c_" + k] = v
    in_maps = []
    for c in range(8):
        m = dict(base)
        m["x"] = x[c % B]
        in_maps.append(m)
    res = run_bass_kernel_spmd(nc, in_maps, core_ids=list(range(8)))
    return np.stack([res.results[b]["out"] for b in range(B)], axis=0).astype(np.float32)
```

```python
import math
import numpy as np
import concourse.bass as bass
import concourse.mybir as mybir
from concourse.bass_utils import run_bass_kernel_spmd

F32 = mybir.dt.float32
BF16 = mybir.dt.bfloat16
AF = mybir.ActivationFunctionType
ALU = mybir.AluOpType
AX = mybir.AxisListType

D = 2048
NIN = 12288
DFF = 5632
NMETA = 16
DEPTH = 2
TABC = 384
TABW = 1024
TABR = TABW + 127


class Ctx:
    def __init__(self, nc):
        self.nc = nc
        self.E = {"pe": nc.tensor, "dve": nc.vector, "act": nc.scalar, "pool": nc.gpsimd, "sp": nc.sync, "cv": nc.gpsimd}
        self.sem = {e: nc.alloc_semaphore("c_" + e) for e in ["pe", "dve", "act", "pool"]}
        self.cnt = {e: 0 for e in self.sem}
        self.seen = {e: {} for e in self.E}
        self.dq = {q: [[nc.alloc_semaphore("d_%s%d" % (q, i)), 0] for i in range(n)]
                   for q, n in (("sp", 24), ("pool", 12), ("cv", 40))}
        self.dqi = {"sp": 0, "pool": 0, "cv": 0}
        self.persist = {}
        self.lastw = {}
        self.readers = {}

    def _wait(self, e, tok):
        if tok is None:
            return
        key, sem, val = tok
        if key == "pe" and e == "pe":
            return
        if self.seen[e].get(key, 0) >= val:
            return
        self.E[e].wait_ge(sem, val)
        self.seen[e][key] = val

    def deps(self, e, reads, writes):
        for r in reads:
            self._wait(e, self.lastw.get(r))
        for w in writes:
            self._wait(e, self.lastw.get(w))
            for t in self.readers.get(w, {}).values():
                self._wait(e, t)

    def commit(self, tok, reads, writes):
        for r in reads:
            self.readers.setdefault(r, {})[tok[0]] = tok
        for w in writes:
            self.lastw[w] = tok
            self.readers[w] = {}

    def op(self, e, fn, reads=(), writes=()):
        self.deps(e, reads, writes)
        ins = fn(self.E[e])
        self.cnt[e] += 1
        ins.then_inc(self.sem[e], 1)
        self.commit((e, self.sem[e], self.cnt[e]), reads, writes)

    def dma(self, out, in_, reads=(), writes=(), q="sp", **kw):
        self.deps(q, reads, writes)
        slots = self.dq[q]
        i = self.dqi[q] % len(slots)
        self.dqi[q] += 1
        sem, val = slots[i]
        key = ("d", q, i)
        if val > 0:
            self._wait(q, (key, sem, val))
        self.E[q].dma_start(out=out, in_=in_, **kw).then_inc(sem, 16)
        slots[i][1] = val + 16
        self.commit((key, sem, val + 16), reads, writes)

    def barrier(self, final=False):
        toks = [(e, self.sem[e], self.cnt[e]) for e in self.sem if self.cnt[e] > 0]
        for q, slots in self.dq.items():
            if q == "cv" and not final:
                continue
            for i, (sem, val) in enumerate(slots):
                if val > 0:
                    toks.append((("d", q, i), sem, val))
        for e in self.E:
            if e == "cv":
                continue
            for t in toks:
                if t[0] == e:
                    continue
                self._wait(e, t)
        self.lastw = dict(self.persist)
        self.readers = {}


def t5_bucket_np(n):
    n = np.asarray(n)
    nn = np.maximum(n, 0)
    nf = np.maximum(nn, 1).astype(np.float32)
    large = 16 + (np.log(nf / np.float32(16)) / np.float32(math.log(8.0)) * np.float32(16)).astype(np.int32)
    large = np.minimum(large, 31)
    return np.where(nn < 16, nn, large)


def host_consts():
    c = {}
    c["ident"] = np.eye(128, dtype=np.float32)
    n = np.arange(TABR) - 127 - TABC
    oh = np.zeros((33, TABR), np.float32)
    b = t5_bucket_np(n)
    for y in range(TABR):
        if n[y] < 0:
            oh[32, y] = 1.0
        else:
            oh[b[y], y] = 1.0
    c["onehot"] = oh
    mk = np.zeros((4, 128, 128), np.float32)
    for q in range(4):
        for g2 in range(2):
            gl = 2 * q + g2
            mk[q, g2 * 64:(g2 + 1) * 64, gl * 16:(gl + 1) * 16] = 1.0
    c["maskk"] = np.ascontiguousarray(mk.transpose(1, 0, 2))
    return c


PARAM_SHAPES = {
    "meta": (16, 2048), "rel_bias": (32, 8), "norm_w": (2, 4, 2048), "w_in": (2, 2048, 12288),
    "conv_w": (2, 4, 1024), "conv_b": (2, 1024), "lru_w_a": (2, 8, 128, 128), "lru_b_a": (2, 1024),
    "lru_w_x": (2, 8, 128, 128), "lru_b_x": (2, 1024), "lru_lambda": (2, 1024), "da_lambda": (2, 4, 64),
    "da_subln": (2, 128), "s5_lam_re": (2, 64, 64), "s5_lam_im": (2, 64, 64), "s5_b_re": (2, 64, 64, 16),
    "s5_b_im": (2, 64, 64, 16), "s5_c_re": (2, 64, 16, 64), "s5_c_im": (2, 64, 16, 64), "s5_d": (2, 1024),
    "s5_log_step": (2, 64), "s5_w_glu": (2, 1024, 1024), "s5_b_glu": (2, 1024), "b_gate": (2, 3, 2048),
    "w_branch": (2, 3, 1024, 2048), "w_out": (2, 2048, 2048), "w_ffn_in": (2, 2048, 11264),
    "w_ffn_out": (2, 5632, 2048),
}


def build(NB=8, depth=DEPTH, dbg=()):
    nc = bass.Bass("TRN2", target_bir_lowering=False)
    T = NMETA + 512 * NB
    NTM = 1 + 4 * NB
    SEQ = 512 * NB

    def cblk(i):
        return (0, 16) if i == 0 else (16 + 512 * (i - 1), 512)

    def tmt(j):
        return (0, 16) if j == 0 else (16 + 128 * (j - 1), 128)

    ins = {}
    ins["x"] = nc.dram_tensor("x", [SEQ, D], F32, kind="ExternalInput").ap()
    for k, shp in PARAM_SHAPES.items():
        ins[k] = nc.dram_tensor(k, list(shp), F32, kind="ExternalInput").ap()
    hc = host_consts()
    for k, v in hc.items():
        ins["c_" + k] = nc.dram_tensor("c_" + k, list(v.shape), F32, kind="ExternalInput").ap()
    out = nc.dram_tensor("out", [SEQ, D], F32, kind="ExternalOutput").ap()

    def scratch(name, shape, dt):
        if name in dbg:
            return nc.dram_tensor(name, shape, dt, kind="ExternalOutput").ap()
        return nc.dram_tensor(name, shape, dt).ap()

    XS = scratch("XS", [T, D], F32)
    HF = scratch("HF", [D, T], BF16)
    AG = scratch("AG", [1024, T], F32)
    AXs = scratch("AXs", [1024, T], F32)
    QF = scratch("QF", [1024, T], BF16)
    KF = scratch("KF", [1024, T], BF16)
    VF = scratch("VF", [1024, T], BF16)
    US = scratch("US", [1024, T], F32)
    GS = scratch("GS", [6144, T], F32)
    YS = scratch("YS", [3072, T], BF16)
    YG = scratch("YG", [1024, T], F32)
    MG = scratch("MG", [D, T], BF16)
    TBD = scratch("TBD", [8, 128, TABR], F32)
    WIN = [scratch("WIN%d" % l, [D, NIN], BF16) for l in range(depth)]
    WBR = [scratch("WBR%d" % l, [3072, D], BF16) for l in range(depth)]
    WOUT = [scratch("WOUT%d" % l, [D, D], BF16) for l in range(depth)]
    WF1 = [scratch("WF1%d" % l, [D, 2 * DFF], BF16) for l in range(depth)]
    WF2 = [scratch("WF2%d" % l, [DFF, D], BF16) for l in range(depth)]
    WGLU = [scratch("WGLU%d" % l, [1024, 1024], BF16) for l in range(depth)]
    LWA = [scratch("LWA%d" % l, [1024, 128], BF16) for l in range(depth)]
    LWX = [scratch("LWX%d" % l, [1024, 128], BF16) for l in range(depth)]

    cx = Ctx(nc)
    op, dma = cx.op, cx.dma
    uid = [0]

    def sbt(name, shape, dt):
        uid[0] += 1
        return nc.sbuf_tensor("%s_u%d" % (name, uid[0]), shape, dt)

    ident = nc.alloc_sbuf_tensor("ident", [128, 128], F32)
    ones16 = nc.alloc_sbuf_tensor("ones16", [128, 128], BF16)
    onesf = nc.alloc_sbuf_tensor("onesf", [128, 128], F32)
    PS = [nc.alloc_psum_tensor("ps%d" % i, [128, 512], F32) for i in range(8)]

    dma(ident[:], ins["c_ident"], writes=["ident"])
    if dbg:
        junk = nc.dram_tensor("junk", [len(ins), 4], F32).ap()
        for i_, (k_, ap_) in enumerate(ins.items()):
            flat = bass.AP(ap_.tensor, 0, [[4, 1], [1, 4]])
            dma(junk[i_:i_ + 1, :], flat, writes=[("junk", i_)])
    op("dve", lambda e: e.memset(ones16[:], 1.0), writes=["ones16"])
    op("dve", lambda e: e.memset(onesf[:], 1.0), writes=["onesf"])

    CVS = 1024

    def conv_w(dst, src, rows, key):
        for r in range(0, rows, CVS):
            n = min(CVS, rows - r)
            dma(dst[r:r + n, :], src[r:r + n, :], writes=[(key, r)], q="cv")
            cx.persist[(key, r)] = cx.lastw[(key, r)]

    def wkeys(key, rows):
        return [(key, r) for r in range(0, rows, CVS)]

    def conv_group(l, names):
        if "noconv" in dbg or l >= depth:
            return
        if "WIN" in names:
            conv_w(WIN[l], ins["w_in"][l], D, "WIN%d" % l)
        if "MIX" in names:
            conv_w(LWA[l], ins["lru_w_a"][l].rearrange("h i j -> (h i) j"), 1024, "LWA%d" % l)
            conv_w(LWX[l], ins["lru_w_x"][l].rearrange("h i j -> (h i) j"), 1024, "LWX%d" % l)
            conv_w(WGLU[l], ins["s5_w_glu"][l], 1024, "WGLU%d" % l)
            conv_w(WBR[l], ins["w_branch"][l].rearrange("b k m -> (b k) m"), 3072, "WBR%d" % l)
        if "WOUT" in names:
            conv_w(WOUT[l], ins["w_out"][l], D, "WOUT%d" % l)
        if "WF1" in names:
            conv_w(WF1[l], ins["w_ffn_in"][l], D, "WF1%d" % l)
        if "WF2" in names:
            conv_w(WF2[l], ins["w_ffn_out"][l], DFF, "WF2%d" % l)

    conv_group(0, ["WIN"])

    dma(XS[0:16, :], ins["meta"], writes=[("XS", 0)])
    for j in range(1, NTM):
        r0, n = tmt(j)
        dma(XS[r0:r0 + n, :], ins["x"][r0 - 16:r0 - 16 + n, :], writes=[("XS", j)])

    def build_table():
        with (sbt("rb", [33, 8], F32) as rb, sbt("oh", [33, TABR], F32) as oh,
              sbt("lh", [33, 128], F32) as lh, sbt("frow", [128, TABR], F32) as frow):
            op("dve", lambda e: e.memset(rb[32:33, :], -30000.0), writes=["rb32"])
            dma(rb[0:32, :], ins["rel_bias"], writes=["rb"])
            dma(oh[:], ins["c_onehot"], writes=["oh"])
            for h in range(8):
                op("dve", lambda e: e.tensor_copy(out=lh[:], in_=rb[:, h:h + 1].to_broadcast([33, 128])),
                   reads=["rb", "rb32"], writes=["lh"])
                for c0 in range(0, TABR, 512):
                    n = min(512, TABR - c0)
                    pk = "ps%d" % (c0 // 512)
                    op("pe", lambda e: e.matmul(PS[c0 // 512][:, :n], lh[:], oh[:, c0:c0 + n], start=True, stop=True),
                       reads=["lh", "oh"], writes=[pk])
                    op("act", lambda e: e.copy(out=frow[:, c0:c0 + n], in_=PS[c0 // 512][:, :n]),
                       reads=[pk], writes=[("frow", c0)])
                dma(TBD[h], frow[:], reads=[("frow", c0) for c0 in range(0, TABR, 512)], writes=[("TBD", h)])
            cx.barrier()

    if "stop:init" not in dbg:
        build_table()

    def norm_tile_to_hf(P, xt, nr, c0, nwcol, rk):
        sq, ss, xn, hT = P["sq"], P["ss"], P["xn"], P["hT"]
        op("act", lambda e: e.activation(out=sq[:nr, :], in_=xt[:nr, :], func=AF.Square), reads=[rk], writes=["sq"])
        op("dve", lambda e: e.tensor_reduce(out=ss[:nr, 0:1], in_=sq[:nr, :], axis=AX.X, op=ALU.add),
           reads=["sq"], writes=["ss"])
        op("dve", lambda e: e.tensor_scalar(out=ss[:nr, 1:2], in0=ss[:nr, 0:1], scalar1=1.0 / D, scalar2=1e-6,
                                            op0=ALU.mult, op1=ALU.add), reads=["ss"], writes=["ss1"])
        op("act", lambda e: e.sqrt(out=ss[:nr, 2:3], in_=ss[:nr, 1:2]), reads=["ss1"], writes=["ss2"])
        op("dve", lambda e: e.reciprocal(out=ss[:nr, 3:4], in_=ss[:nr, 2:3]), reads=["ss2"], writes=["ss3"])
        op("dve", lambda e: e.tensor_scalar(out=xn[:nr, :], in0=xt[:nr, :], scalar1=ss[:nr, 3:4], scalar2=None,
                                            op0=ALU.mult), reads=[rk, "ss3"], writes=["xn"])
        for f4 in range(4):
            pk = "ps%d" % (4 + f4)
            for i in range(4):
                ft = f4 * 4 + i
                op("pe", lambda e: e.transpose(out=PS[4 + f4][:, i * 128:i * 128 + nr],
                                               in_=xn[:nr, ft * 128:(ft + 1) * 128], identity=ident[:nr, :nr]),
                   reads=["xn", "ident"], writes=[pk])
            src = PS[4 + f4][:].rearrange("p (a b) -> p a b", a=4)[:, :, :nr]
            sc = nwcol[:, f4 * 4:(f4 + 1) * 4].unsqueeze(2).to_broadcast([128, 4, nr])
            op("dve", lambda e: e.tensor_tensor(out=hT[:, f4 * 4:(f4 + 1) * 4, :nr], in0=src, in1=sc, op=ALU.mult),
               reads=[pk, "nwcol"], writes=[("hT", f4)])
        dma(HF.rearrange("(a p) t -> p a t", p=128)[:, :, c0:c0 + nr], hT[:, :, :nr],
            reads=[("hT", f4) for f4 in range(4)], writes=[("HF", c0)], q="pool")

    def norm_bufs(st):
        P = {}
        P["sq"] = st.enter_context(sbt("n_sq", [128, D], F32))
        P["ss"] = st.enter_context(sbt("n_ss", [128, 4], F32))
        P["xn"] = st.enter_context(sbt("n_xn", [128, D], F32))
        P["hT"] = st.enter_context(sbt("n_hT", [128, 16, 128], BF16))
        return P

    from contextlib import ExitStack

    def load_col(dst, src_flat, nft, key):
        dma(dst, src_flat.rearrange("(a p) -> p a", p=128), writes=[key], allow_slow_non_contiguous=True)

    def phase_norm(l, which):
        with ExitStack() as st:
            P = norm_bufs(st)
            nwcol = st.enter_context(sbt("nwcol", [128, 16], F32))
            xt = [st.enter_context(sbt("xt%d" % i, [128, D], F32)) for i in range(2)]
            load_col(nwcol[:], ins["norm_w"][l, which], 16, "nwcol")
            for j in range(NTM):
                r0, n = tmt(j)
                s = j % 2
                dma(xt[s][:n, :], XS[r0:r0 + n, :], reads=[("XS", j)], writes=[("xt", s)])
                norm_tile_to_hf(P, xt[s], n, r0, nwcol, ("xt", s))
            cx.barrier()

    def linear_fm(W, wkey, K, M0, M, xsrc, xkey, sblocks, epilogue, st_bufs=None):
        KC = K // 128
        with ExitStack() as st:
            maxc = max(sum(cblk(b)[1] for b in sb) for sb in sblocks)
            hX = st.enter_context(sbt("l_hX", [128, KC, maxc], BF16))
            wt = [st.enter_context(sbt("l_wt%d" % i, [128, KC, 512], BF16)) for i in range(2)]
            wi = 0
            for sb in sblocks:
                c0 = cblk(sb[0])[0]
                ncol = sum(cblk(b)[1] for b in sb)
                dma(hX[:, :, :ncol], xsrc.rearrange("(a p) t -> p a t", p=128)[:, :, c0:c0 + ncol],
                    reads=[(xkey, cc) for cc in range(c0, c0 + ncol, 16)] if False else [xkey], writes=["hX"])
                for m0 in range(M0, M0 + M, 4):
                    s = wi % 2
                    wi += 1
                    nm = min(4, M0 + M - m0)
                    dma(wt[s][:, :, :nm * 128], W.rearrange("(a p) m -> p a m", p=128)[:, :, m0 * 128:(m0 + nm) * 128],
                        reads=wkeys(wkey, K), writes=[("wt", s)])
                    for i in range(nm):
                        ft = m0 + i
                        half = (ft % 2) * 4
                        off = 0
                        for bi, b in enumerate(sb):
                            n = cblk(b)[1]
                            pk = "ps%d" % (half + bi)
                            for kc in range(KC):
                                op("pe", lambda e: e.matmul(PS[half + bi][:, :n], wt[s][:, kc, i * 128:(i + 1) * 128],
                                                            hX[:, kc, off:off + n], start=(kc == 0), stop=(kc == KC - 1)),
                                   reads=[("wt", s), "hX"], writes=[pk])
                            epilogue(ft, bi, (c0 + off, n), PS[half + bi], pk)
                            off += n
            cx.barrier()

    def superblocks(maxb):
        sbs = [[0]]
        b = 1
        while b <= NB:
            sbs.append(list(range(b, min(b + maxb, NB + 1))))
            b += maxb
        return sbs

    def phase_win(l):
        with ExitStack() as st:
            stg = [st.enter_context(sbt("p2_stg%d" % i, [128, 512], F32)) for i in range(4)]
            stg16 = [st.enter_context(sbt("p2_s16%d" % i, [128, 512], BF16)) for i in range(4)]
            bg = st.enter_context(sbt("p2_bg", [128, 48], F32))
            load_col(bg[:], ins["b_gate"][l].rearrange("a b -> (a b)"), 48, "bg")
            cnt = [0]

            def epi(ft, bi, cr, ps, pk):
                c0, n = cr
                s = cnt[0] % 4
                cnt[0] += 1
                if ft < 8:
                    op("act", lambda e: e.activation(out=stg[s][:, :n], in_=ps[:, :n], func=AF.Gelu),
                       reads=[pk], writes=[("stg", s)])
                    dma(AG[ft * 128:(ft + 1) * 128, c0:c0 + n], stg[s][:, :n], reads=[("stg", s)], writes=[("AG", ft, c0)], q="pool")
                elif ft < 16 or 40 <= ft < 48:
                    dst = AXs if ft < 16 else US
                    r = (ft - 8) if ft < 16 else (ft - 40)
                    op("dve", lambda e: e.tensor_copy(out=stg[s][:, :n], in_=ps[:, :n]), reads=[pk], writes=[("stg", s)])
                    dma(dst[r * 128:(r + 1) * 128, c0:c0 + n], stg[s][:, :n], reads=[("stg", s)],
                        writes=[("AU", ft, c0)], q="pool")
                elif ft < 40:
                    dst, r, sc = (QF, ft - 16, 0.125) if ft < 24 else ((KF, ft - 24, 1.0) if ft < 32 else (VF, ft - 32, 1.0))
                    op("dve", lambda e: e.tensor_scalar(out=stg16[s][:, :n], in0=ps[:, :n], scalar1=sc, scalar2=None,
                                                        op0=ALU.mult), reads=[pk], writes=[("s16", s)])
                    dma(dst[r * 128:(r + 1) * 128, c0:c0 + n], stg16[s][:, :n], reads=[("s16", s)], writes=[("QKV", ft, c0)], q="pool")
                else:
                    r = ft - 48
                    op("act", lambda e: e.activation(out=stg[s][:, :n], in_=ps[:, :n], func=AF.Sigmoid,
                                                     bias=bg[:, r:r + 1]), reads=[pk, "bg"], writes=[("stg", s)])
                    dma(GS[r * 128:(r + 1) * 128, c0:c0 + n], stg[s][:, :n], reads=[("stg", s)], writes=[("GS", ft, c0)], q="pool")

            linear_fm(WIN[l], "WIN%d" % l, D, 0, 48, HF, "HFall", superblocks(4), epi)

    def phase_lru(l):
        with ExitStack() as st:
            def tl(name, dt=F32):
                return st.enter_context(sbt(name, [128, T], dt))
            ax, gg, xc, rr, ig, aa, mm = [tl("l_%d" % i) for i in range(7)]
            xc16 = tl("l_xc16", BF16)
            ya16 = tl("l_ya16", BF16)
            wa = st.enter_context(sbt("l_wa", [128, 8, 128], BF16))
            wx = st.enter_context(sbt("l_wx", [128, 8, 128], BF16))
            cw = st.enter_context(sbt("l_cw", [128, 8, 4], F32))
            cb = st.enter_context(sbt("l_cb", [128, 8], F32))
            ba = st.enter_context(sbt("l_ba", [128, 8], F32))
            bx = st.enter_context(sbt("l_bx", [128, 8], F32))
            lam = st.enter_context(sbt("l_lam", [128, 8, 4], F32))
            dma(wa[:], LWA[l].rearrange("(h i) j -> i h j", i=128), reads=wkeys("LWA%d" % l, 1024), writes=["wa"])
            dma(wx[:], LWX[l].rearrange("(h i) j -> i h j", i=128), reads=wkeys("LWX%d" % l, 1024), writes=["wx"])
            for wi_ in range(4):
                load_col(cw[:, :, wi_], ins["conv_w"][l, wi_], 8, ("cw", wi_))
            load_col(cb[:], ins["conv_b"][l], 8, "cb")
            load_col(ba[:], ins["lru_b_a"][l], 8, "ba")
            load_col(bx[:], ins["lru_b_x"][l], 8, "bx")
            load_col(lam[:, :, 0], ins["lru_lambda"][l], 8, "lam0")
            op("act", lambda e: e.activation(out=lam[:, :, 1], in_=lam[:, :, 0], func=AF.Exp, scale=-1.0),
               reads=["lam0"], writes=["lam1"])
            op("act", lambda e: e.activation(out=lam[:, :, 2], in_=lam[:, :, 1], func=AF.Ln, bias=1.0),
               reads=["lam1"], writes=["lam2"])
            op("dve", lambda e: e.tensor_scalar(out=lam[:, :, 1], in0=lam[:, :, 2], scalar1=-8.0, scalar2=None, op0=ALU.mult),
               reads=["lam2"], writes=["c8"])
            op("dve", lambda e: e.tensor_scalar(out=lam[:, :, 3], in0=lam[:, :, 2], scalar1=-16.0, scalar2=None, op0=ALU.mult),
               reads=["lam2"], writes=["c16"])
            for ct in range(8):
                rs = slice(ct * 128, (ct + 1) * 128)
                dma(ax[:], AXs[rs, :], writes=["ax"])
                dma(gg[:], AG[rs, :], writes=["gg"])
                op("dve", lambda e: e.tensor_scalar(out=xc[:], in0=ax[:], scalar1=cw[:, ct, 3:4], scalar2=cb[:, ct:ct + 1],
                                                    op0=ALU.mult, op1=ALU.add), reads=["ax", "cb"] + [("cw", i_) for i_ in range(4)], writes=["xc"])
                for sft in (1, 2, 3):
                    op("dve", lambda e: e.scalar_tensor_tensor(out=xc[:, sft:], in0=ax[:, :T - sft],
                                                               scalar=cw[:, ct, 3 - sft:4 - sft], in1=xc[:, sft:],
                                                               op0=ALU.mult, op1=ALU.add),
                       reads=["ax", "xc"] + [("cw", i_) for i_ in range(4)], writes=["xc"])
                op("act", lambda e: e.copy(out=xc16[:], in_=xc[:]), reads=["xc"], writes=["xc16"])
                for b in range(NB + 1):
                    c0, n = cblk(b)
                    pa, pb = "ps%d" % ((b % 2) * 2), "ps%d" % ((b % 2) * 2 + 1)
                    op("pe", lambda e: e.matmul(PS[(b % 2) * 2][:, :n], wa[:, ct, :], xc16[:, c0:c0 + n], start=True, stop=True),
                       reads=["wa", "xc16"], writes=[pa])
                    op("pe", lambda e: e.matmul(PS[(b % 2) * 2 + 1][:, :n], wx[:, ct, :], xc16[:, c0:c0 + n], start=True, stop=True),
                       reads=["wx", "xc16"], writes=[pb])
                    op("act", lambda e: e.activation(out=rr[:, c0:c0 + n], in_=PS[(b % 2) * 2][:, :n], func=AF.Sigmoid,
                                                     bias=ba[:, ct:ct + 1]), reads=[pa, "ba"], writes=[("rr", b)])
                    op("act", lambda e: e.activation(out=ig[:, c0:c0 + n], in_=PS[(b % 2) * 2 + 1][:, :n], func=AF.Sigmoid,
                                                     bias=bx[:, ct:ct + 1]), reads=[pb, "bx"], writes=[("ig", b)])
                rrk = [("rr", b) for b in range(NB + 1)]
                igk = [("ig", b) for b in range(NB + 1)]
                op("act", lambda e: e.activation(out=aa[:], in_=rr[:], func=AF.Exp, scale=lam[:, ct, 1:2]),
                   reads=rrk + ["c8"], writes=["aa"])
                op("act", lambda e: e.activation(out=mm[:], in_=rr[:], func=AF.Exp, scale=lam[:, ct, 3:4]),
                   reads=rrk + ["c16"], writes=["mm"])
                op("dve", lambda e: e.tensor_scalar(out=mm[:], in0=mm[:], scalar1=-1.0, scalar2=1.0, op0=ALU.mult, op1=ALU.add),
                   reads=["mm"], writes=["mm"])
                op("act", lambda e: e.sqrt(out=mm[:], in_=mm[:]), reads=["mm"], writes=["mm"])
                op("dve", lambda e: e.tensor_tensor(out=ig[:], in0=ig[:], in1=xc[:], op=ALU.mult),
                   reads=igk + ["xc"], writes=["igx"])
                op("dve", lambda e: e.tensor_tensor(out=mm[:], in0=mm[:], in1=ig[:], op=ALU.mult),
                   reads=["mm", "igx"], writes=["mm"])
                op("dve", lambda e: e.tensor_tensor_scan(out=rr[:], data0=aa[:], data1=mm[:], initial=0.0,
                                                         op0=ALU.mult, op1=ALU.add),
                   reads=["aa", "mm"], writes=rrk + ["hh"])
                op("dve", lambda e: e.tensor_tensor(out=ya16[:], in0=rr[:], in1=gg[:], op=ALU.mult),
                   reads=["hh", "gg"], writes=["ya16"])
                dma(YS[rs, :], ya16[:], reads=["ya16"], writes=[("YS", ct)], q="pool")
            cx.barrier()

    def phase_attn(l):
        lam_init = 0.8 - 0.6 * math.exp(-0.3 * l)
        with ExitStack() as st:
            qz = [st.enter_context(sbt("a_qz%d" % i, [128, T], BF16)) for i in range(2)]
            op("pool", lambda e: e.memset(qz[0][:], 0.0), writes=[("qz", 0)])
            op("pool", lambda e: e.memset(qz[1][:], 0.0), writes=[("qz", 1)])
            k16 = st.enter_context(sbt("a_k", [128, T], BF16))
            v16 = st.enter_context(sbt("a_v", [128, T], BF16))
            vT = st.enter_context(sbt("a_vT", [128, NTM, 128], BF16))
            id16 = st.enter_context(sbt("a_id16", [128, 128], BF16))
            pT = [st.enter_context(sbt("a_pT%d" % i, [128, 512], BF16)) for i in range(4)]
            ssb = [st.enter_context(sbt("a_ss%d" % i, [128, 512], F32)) for i in range(3)]
            o0 = st.enter_context(sbt("a_o0", [128, 512], F32))
            o1 = st.enter_context(sbt("a_o1", [128, 512], F32))
            rc = st.enter_context(sbt("a_rc", [128, 512], F32))
            rc2 = st.enter_context(sbt("a_rc2", [128, 512], F32))
            rc3 = st.enter_context(sbt("a_rc3", [128, 512], F32))
            sqb = st.enter_context(sbt("a_sq", [128, 512], F32))
            yb = [st.enter_context(sbt("a_yb%d" % i, [128, 512], BF16)) for i in range(2)]
            dl = st.enter_context(sbt("a_dl", [128, 4, 64], F32))
            dsc = st.enter_context(sbt("a_dsc", [128, 8], F32))
            sw = st.enter_context(sbt("a_sw", [128, 2], F32))
            tb = st.enter_context(sbt("a_tb", [128, 8, TABW], F32))
            for h in range(8):
                src = bass.AP(TBD.tensor, h * 128 * TABR + 127, [[TABR - 1, 128], [1, TABW]])
                dma(tb[:, h, :], src, writes=["tb"])
            dma(dl[:], bass.AP(ins["da_lambda"].tensor, l * 256, [[0, 128], [64, 4], [1, 64]]), writes=["dl"])
            op("dve", lambda e: e.tensor_tensor(out=dl[:, 0, :], in0=dl[:, 0, :], in1=dl[:, 1, :], op=ALU.mult),
               reads=["dl"], writes=["dl0"])
            op("dve", lambda e: e.tensor_tensor(out=dl[:, 2, :], in0=dl[:, 2, :], in1=dl[:, 3, :], op=ALU.mult),
               reads=["dl"], writes=["dl2"])
            op("dve", lambda e: e.tensor_reduce(out=dsc[:, 0:1], in_=dl[:, 0, :], axis=AX.X, op=ALU.add),
               reads=["dl0"], writes=["d0"])
            op("dve", lambda e: e.tensor_reduce(out=dsc[:, 1:2], in_=dl[:, 2, :], axis=AX.X, op=ALU.add),
               reads=["dl2"], writes=["d1"])
            op("act", lambda e: e.activation(out=dsc[:, 2:4], in_=dsc[:, 0:2], func=AF.Exp), reads=["d0", "d1"], writes=["d2"])
            op("dve", lambda e: e.tensor_tensor(out=dsc[:, 4:5], in0=dsc[:, 3:4], in1=dsc[:, 2:3], op=ALU.subtract),
               reads=["d2"], writes=["d4"])
            op("dve", lambda e: e.tensor_scalar(out=dsc[:, 5:6], in0=dsc[:, 4:5], scalar1=-lam_init, scalar2=None, op0=ALU.add),
               reads=["d4"], writes=["neglam"])
            load_col(sw[:, 0:1], ins["da_subln"][l], 1, "sw0")
            op("dve", lambda e: e.tensor_scalar(out=sw[:, 1:2], in0=sw[:, 0:1], scalar1=1.0 - lam_init, scalar2=None, op0=ALU.mult),
               reads=["sw0"], writes=["sw"])
            op("dve", lambda e: e.tensor_copy(out=id16[:], in_=ident[:]), reads=["ident"], writes=["id16"])
            pi = 0
            si = 0
            yi = 0
            pending = []
            for h in range(8):
                rs = slice(h * 128, (h + 1) * 128)
                dma(qz[0][0:64, :], QF[h * 128:h * 128 + 64, :], writes=[("qz", 0)])
                dma(qz[1][64:128, :], QF[h * 128 + 64:h * 128 + 128, :], writes=[("qz", 1)])
                dma(k16[:], KF[rs, :], writes=["k16"])
                dma(v16[:], VF[rs, :], writes=["v16"])
                for j in range(NTM):
                    r0, n = tmt(j)
                    pk = "ps%d" % (7 * (j % 2))
                    pst = PS[7 * (j % 2)][:].bitcast(BF16)
                    op("pe", lambda e: e.transpose(out=pst[:n, 0:128], in_=v16[:, r0:r0 + n], identity=id16[:]),
                       reads=["v16", "id16"], writes=[pk])
                    op("act", lambda e: e.copy(out=vT[:n, j, :], in_=pst[:n, 0:128]), reads=[pk], writes=[("vT", j)])
                for qb in range(NB + 1):
                    q0, nq = cblk(qb)
                    kts = [j for j in range(NTM) if tmt(j)[0] <= q0 + nq - 1]
                    steps = [(c, ji, j) for c in range(2) for ji, j in enumerate(kts)]
                    LA = 2
                    info = {}
                    for idx in range(len(steps) + LA):
                        if (idx == 4 or idx == len(steps) + LA - 1) and pending:
                            pending.pop(0)()
                        if idx < len(steps):
                            c, ji, j = steps[idx]
                            k0, nk = tmt(j)
                            delta = q0 - k0
                            sslot = si % 3
                            pss = PS[sslot]
                            pks = "ps%d" % sslot
                            op("pe", lambda e: e.matmul(pss[:nk, :nq], k16[:, k0:k0 + nk],
                                                        qz[c][:, q0:q0 + nq], start=True, stop=True),
                               reads=["k16", ("qz", c)], writes=[pks])
                            p = pT[pi % 4]
                            pkk = ("pT", pi % 4)
                            if delta < 240:
                                x0 = delta + TABC
                                sb_ = ssb[si % 3]
                                op("dve", lambda e: e.tensor_tensor(out=sb_[:nk, :nq], in0=pss[:nk, :nq],
                                                                    in1=tb[:nk, h, x0:x0 + nq], op=ALU.add),
                                   reads=[pks, "tb"], writes=[("ssb", si % 3)])
                                op("act", lambda e: e.activation(out=p[:nk, :nq], in_=sb_[:nk, :nq], func=AF.Exp),
                                   reads=[("ssb", si % 3)], writes=[pkk])
                            else:
                                op("act", lambda e: e.activation(out=p[:nk, :nq], in_=pss[:nk, :nq], func=AF.Exp,
                                                                 bias=tb[:nk, h, TABW - 1:TABW]),
                                   reads=[pks, "tb"], writes=[pkk])
                            info[idx] = (p, pkk, nk)
                            si += 1
                            pi += 1
                        if idx - LA >= 0:
                            c, ji, j = steps[idx - LA]
                            p, pkk, nk = info.pop(idx - LA)
                            ps_o, ps_r = PS[3 + c], PS[5 + c]
                            ko, kr = "ps%d" % (3 + c), "ps%d" % (5 + c)
                            first, last = ji == 0, ji == len(kts) - 1
                            op("pe", lambda e: e.matmul(ps_o[:, :nq], vT[:nk, j, :], p[:nk, :nq], start=first, stop=last),
                               reads=[("vT", j), pkk], writes=[ko])
                            op("pe", lambda e: e.matmul(ps_r[:, :nq], ones16[:nk, :], p[:nk, :nq], start=first, stop=last),
                               reads=["ones16", pkk], writes=[kr])
                    op("act", lambda e: e.activation(out=rc[:, :nq], in_=PS[5][:, :nq], func=AF.Ln), reads=["ps5"], writes=["rc"])
                    op("act", lambda e: e.activation(out=rc[:, :nq], in_=rc[:, :nq], func=AF.Exp, scale=-1.0), reads=["rc"], writes=["rc"])
                    op("dve", lambda e: e.tensor_tensor(out=o0[:, :nq], in0=PS[3][:, :nq], in1=rc[:, :nq], op=ALU.mult),
                       reads=["ps3", "rc"], writes=["o0"])
                    op("act", lambda e: e.activation(out=rc2[:, :nq], in_=PS[6][:, :nq], func=AF.Ln), reads=["ps6"], writes=["rc2"])
                    op("act", lambda e: e.activation(out=rc2[:, :nq], in_=rc2[:, :nq], func=AF.Exp, scale=-1.0), reads=["rc2"], writes=["rc2"])
                    op("dve", lambda e: e.tensor_tensor(out=o1[:, :nq], in0=PS[4][:, :nq], in1=rc2[:, :nq], op=ALU.mult),
                       reads=["ps4", "rc2"], writes=["o1"])
                    op("dve", lambda e: e.scalar_tensor_tensor(out=o0[:, :nq], in0=o1[:, :nq], scalar=dsc[:, 5:6],
                                                               in1=o0[:, :nq], op0=ALU.mult, op1=ALU.add),
                       reads=["o1", "o0", "neglam"], writes=["o0"])
                    op("dve", lambda e: e.tensor_tensor(out=sqb[:, :nq], in0=o0[:, :nq], in1=o0[:, :nq], op=ALU.mult),
                       reads=["o0"], writes=["sqb"])

                    def part2(h=h, qb=qb, q0=q0, nq=nq):
                        nonlocal yi
                        op("pe", lambda e: e.matmul(PS[7][:, :nq], onesf[:], sqb[:, :nq], start=True, stop=True),
                           reads=["onesf", "sqb"], writes=["ps7"])
                        op("dve", lambda e: e.tensor_scalar(out=rc3[:, :nq], in0=PS[7][:, :nq], scalar1=1.0 / 128, scalar2=1e-5,
                                                            op0=ALU.mult, op1=ALU.add), reads=["ps7"], writes=["rc3"])
                        op("act", lambda e: e.activation(out=rc3[:, :nq], in_=rc3[:, :nq], func=AF.Ln), reads=["rc3"], writes=["rc3"])
                        op("act", lambda e: e.activation(out=rc3[:, :nq], in_=rc3[:, :nq], func=AF.Exp, scale=-0.5),
                           reads=["rc3"], writes=["rc3"])
                        op("dve", lambda e: e.tensor_tensor(out=o0[:, :nq], in0=o0[:, :nq], in1=rc3[:, :nq], op=ALU.mult),
                           reads=["o0", "rc3"], writes=["o0"])
                        y_ = yb[yi % 2]
                        yk = ("yb", yi % 2)
                        yi += 1
                        op("dve", lambda e: e.tensor_scalar(out=y_[:, :nq], in0=o0[:, :nq], scalar1=sw[:, 1:2], scalar2=None,
                                                            op0=ALU.mult), reads=["o0", "sw"], writes=[yk])
                        dma(YS[1024 + h * 128:1024 + (h + 1) * 128, q0:q0 + nq], y_[:, :nq], reads=[yk], writes=[("YS", h, qb)],
                            q="pool")
                    pending.append(part2)
            while pending:
                pending.pop(0)()
            cx.barrier()

    def phase_s5(l):
        NLV = max(1, int(math.ceil(math.log2(T))))
        with ExitStack() as st, ExitStack() as st1:
            def t2(name, shape, dt=F32):
                return st.enter_context(sbt(name, shape, dt))

            def t1(name, shape, dt=F32):
                return st1.enter_context(sbt(name, shape, dt))
            pwr = t2("s_pwr", [128, 32, NLV]); pwi = t2("s_pwi", [128, 32, NLV]); pwn = t2("s_pwn", [128, 32, NLV])
            dcol = t2("s_dcol", [128, 8])
            mag = t2("s_mag", [128, 32])
            ur = t2("s_ur", [128, 32, 8]); ui = t2("s_ui", [128, 32, 8]); nui = t2("s_nui", [128, 32, 8])
            lpr = t2("s_lpr", [128, 32, 8]); lpi = t2("s_lpi", [128, 32, 8]); nlpi = t2("s_nlpi", [128, 32, 8])
            bbr = t2("s_bbr", [128, 32, 16]); bbi = t2("s_bbi", [128, 32, 16])
            mk = t2("s_mk", [128, 4, 128])
            cld = t2("s_cld", [128, 2, 2, 64])
            lr = t1("s_lr", [128, 32]); li = t1("s_li", [128, 32]); stp = t1("s_stp", [128, 32])
            w = [t1("s_w%d" % i, [128, 32]) for i in range(8)]
            cre = t1("s_cre", [128, 32]); cim = t1("s_cim", [128, 32])
            bre = t1("s_bre", [128, 32, 16]); bim = t1("s_bim", [128, 32, 16])
            btmp = t1("s_btmp", [128, 32, 16])
            dma(mk[:], ins["c_maskk"], writes=["mk"])
            load_col(dcol[:], ins["s5_d"][l], 8, "dcol")
            for g2 in range(2):
                ps_ = slice(g2 * 64, (g2 + 1) * 64)
                dma(lr[ps_, :], ins["s5_lam_re"][l].rearrange("(k g) p -> g p k", g=2)[g2], writes=[("lr", g2)],
                    allow_slow_non_contiguous=True)
                dma(li[ps_, :], ins["s5_lam_im"][l].rearrange("(k g) p -> g p k", g=2)[g2], writes=[("li", g2)],
                    allow_slow_non_contiguous=True)
                dma(stp[ps_, :], bass.AP(ins["s5_log_step"].tensor, l * 64 + g2, [[0, 64], [2, 32]]), writes=[("stp", g2)],
                    allow_slow_non_contiguous=True)
                dma(bre[ps_], ins["s5_b_re"][l].rearrange("(k g) p c -> g p k c", g=2)[g2], writes=[("bre", g2)])
                dma(bim[ps_], ins["s5_b_im"][l].rearrange("(k g) p c -> g p k c", g=2)[g2], writes=[("bim", g2)])
            K2 = [("lr", 0), ("lr", 1), ("li", 0), ("li", 1), ("stp", 0), ("stp", 1)]

            def tt(o, a, b, o_, rd, wr):
                op("dve", lambda e: e.tensor_tensor(out=o, in0=a, in1=b, op=o_), reads=rd, writes=wr)

            def ts(o, a, s1, s2, o0_, o1_, rd, wr):
                if o1_ is None:
                    op("dve", lambda e: e.tensor_scalar(out=o, in0=a, scalar1=s1, scalar2=None, op0=o0_), reads=rd, writes=wr)
                else:
                    op("dve", lambda e: e.tensor_scalar(out=o, in0=a, scalar1=s1, scalar2=s2, op0=o0_, op1=o1_), reads=rd, writes=wr)

            op("act", lambda e: e.activation(out=stp[:], in_=stp[:], func=AF.Exp), reads=K2, writes=["step"])
            tt(w[0][:], lr[:], stp[:], ALU.mult, K2 + ["step"], ["w0"])
            op("act", lambda e: e.activation(out=w[0][:], in_=w[0][:], func=AF.Exp), reads=["w0"], writes=["mag"])
            tt(w[1][:], li[:], stp[:], ALU.mult, K2 + ["step"], ["ang"])
            op("act", lambda e: e.activation(out=w[2][:], in_=w[1][:], func=AF.Sin, scale=1.0 / 16), reads=["ang"], writes=["sn"])
            op("act", lambda e: e.activation(out=w[3][:], in_=w[1][:], func=AF.Sin, scale=1.0 / 16, bias=math.pi / 2),
               reads=["ang"], writes=["cs"])
            for it in range(4):
                tt(w[4][:], w[2][:], w[3][:], ALU.mult, ["sn", "cs"], ["sc"])
                tt(w[5][:], w[3][:], w[3][:], ALU.mult, ["cs"], ["cc"])
                tt(w[6][:], w[2][:], w[2][:], ALU.mult, ["sn"], ["s2"])
                ts(w[2][:], w[4][:], 2.0, None, ALU.mult, None, ["sc", "s2"], ["sn"])
                tt(w[3][:], w[5][:], w[6][:], ALU.subtract, ["cc", "s2", "sc"], ["cs"])
            tt(pwr[:, :, 0], w[0][:], w[3][:], ALU.mult, ["mag", "cs"], ["pw"])
            tt(pwi[:, :, 0], w[0][:], w[2][:], ALU.mult, ["mag", "sn"], ["pw"])
            for lv in range(1, NLV):
                tt(w[4][:], pwr[:, :, lv - 1], pwr[:, :, lv - 1], ALU.mult, ["pw"], ["q0"])
                tt(w[5][:], pwi[:, :, lv - 1], pwi[:, :, lv - 1], ALU.mult, ["pw"], ["q1"])
                tt(w[6][:], pwr[:, :, lv - 1], pwi[:, :, lv - 1], ALU.mult, ["pw"], ["q2"])
                tt(pwr[:, :, lv], w[4][:], w[5][:], ALU.subtract, ["q0", "q1"], ["pw"])
                ts(pwi[:, :, lv], w[6][:], 2.0, None, ALU.mult, None, ["q2"], ["pw"])
            ts(pwn[:], pwi[:], -1.0, None, ALU.mult, None, ["pw"], ["pwn"])
            ts(w[4][:], pwr[:, :, 0], -1.0, None, ALU.add, None, ["pw"], ["am1"])
            tt(w[5][:], lr[:], lr[:], ALU.mult, K2, ["d0"])
            tt(w[6][:], li[:], li[:], ALU.mult, K2, ["d1"])
            tt(w[5][:], w[5][:], w[6][:], ALU.add, ["d0", "d1"], ["den"])
            op("dve", lambda e: e.reciprocal(out=w[5][:], in_=w[5][:]), reads=["den"], writes=["rden"])
            tt(w[6][:], w[4][:], lr[:], ALU.mult, ["am1"] + K2, ["e0"])
            tt(w[7][:], pwi[:, :, 0], li[:], ALU.mult, ["pw"] + K2, ["e1"])
            tt(w[6][:], w[6][:], w[7][:], ALU.add, ["e0", "e1"], ["e2"])
            tt(cre[:], w[6][:], w[5][:], ALU.mult, ["e2", "rden"], ["cre"])
            tt(w[6][:], pwi[:, :, 0], lr[:], ALU.mult, ["pw", "e2", "cre"] + K2, ["f0"])
            tt(w[7][:], w[4][:], li[:], ALU.mult, ["am1", "e1", "e2"] + K2, ["f1"])
            tt(w[6][:], w[6][:], w[7][:], ALU.subtract, ["f0", "f1"], ["f2"])
            tt(cim[:], w[6][:], w[5][:], ALU.mult, ["f2", "rden"], ["cim"])
            BK = [("bre", 0), ("bre", 1), ("bim", 0), ("bim", 1)]
            crb = cre[:].unsqueeze(2).to_broadcast([128, 32, 16])
            cib = cim[:].unsqueeze(2).to_broadcast([128, 32, 16])
            tt(bbr[:], bre[:], crb, ALU.mult, BK + ["cre"], ["bbr"])
            tt(btmp[:], bim[:], cib, ALU.mult, BK + ["cim"], ["btmp"])
            tt(bbr[:], bbr[:], btmp[:], ALU.subtract, ["bbr", "btmp"], ["bbr"])
            tt(bbi[:], bim[:], crb, ALU.mult, BK + ["cre"], ["bbi"])
            tt(btmp[:], bre[:], cib, ALU.mult, BK + ["cim", "bbr"], ["btmp"])
            tt(bbi[:], bbi[:], btmp[:], ALU.add, ["bbi", "btmp"], ["bbi"])
            op("dve", lambda e: e.tensor_copy(out=mag[:], in_=w[0][:]), reads=["mag"], writes=["magk"])
            op("dve", lambda e: e.memset(ur[:, :, 0], 1.0), writes=["tab"])
            op("dve", lambda e: e.memset(ui[:, :, 0], 0.0), writes=["tab"])
            op("dve", lambda e: e.tensor_copy(out=lpr[:, :, 0], in_=pwr[:, :, 0]), reads=["pw"], writes=["tab"])
            op("dve", lambda e: e.tensor_copy(out=lpi[:, :, 0], in_=pwi[:, :, 0]), reads=["pw"], writes=["tab"])
            for j in range(1, 8):
                for (tr, ti, mr, mi) in ((ur, ui, w[3], w[2]), (lpr, lpi, pwr[:, :, 0], pwi[:, :, 0])):
                    mr_ = mr[:] if hasattr(mr, "shape") and len(mr.shape) == 2 and not isinstance(mr, bass.AP) else mr
                    mi_ = mi[:] if hasattr(mi, "shape") and len(mi.shape) == 2 and not isinstance(mi, bass.AP) else mi
                    tt(w[4][:], tr[:, :, j - 1], mr_, ALU.mult, ["tab", "cs", "sn", "pw"], ["t4"])
                    tt(w[5][:], ti[:, :, j - 1], mi_, ALU.mult, ["tab", "cs", "sn", "pw"], ["t5"])
                    tt(w[6][:], tr[:, :, j - 1], mi_, ALU.mult, ["tab", "cs", "sn", "pw"], ["t6"])
                    tt(w[7][:], ti[:, :, j - 1], mr_, ALU.mult, ["tab", "cs", "sn", "pw"], ["t7"])
                    tt(tr[:, :, j], w[4][:], w[5][:], ALU.subtract, ["t4", "t5"], ["tab"])
                    tt(ti[:, :, j], w[6][:], w[7][:], ALU.add, ["t6", "t7"], ["tab"])
            ts(nui[:], ui[:], -1.0, None, ALU.mult, None, ["tab"], ["tab"])
            ts(nlpi[:], lpi[:], -1.0, None, ALU.mult, None, ["tab"], ["tab"])
            cx.barrier()
            st1.close()
            NCH = T // 8
            pieces = []
            c_ = 0
            while c_ < NCH:
                n_ = min(NCH - c_, 257 if NCH > 512 else 512)
                pieces.append((c_, n_))
                c_ += n_
            NLC = 0
            while (1 << NLC) < NCH:
                NLC += 1
            u32 = t2("s_u32", [128, T]); u16 = t2("s_u16", [128, T], BF16); ybuf = t2("s_ybuf", [128, T])
            Bps = [[t2("s_Bp%d_%d" % (b_, i), [128, T]) for i in range(2)] for b_ in range(2)]
            g16 = [t2("s_g%d" % i, [128, T], BF16) for i in range(2)]
            dec = t2("s_dec", [128, T])
            Z = [t2("s_Z%d" % i, [128, NCH]) for i in range(4)]
            H16 = [t2("s_H%d" % i, [128, NCH], BF16) for i in range(2)]
            cstfs = [t2("s_cstf%d" % i, [128, 4, 2, 128]) for i in range(2)]
            rot = t2("s_rot", [128, 2, 8, 16]); rta = t2("s_rta", [128, 8, 16]); rtb = t2("s_rtb", [128, 8, 16])
            bpre8 = t2("s_bpre8", [128, 8, 128])
            bstj = t2("s_bstj", [128, 8, 2, 128], BF16)
            mda = t2("s_mda", [128, 8, 128]); mdb = t2("s_mdb", [128, 8, 128])
            MD = t2("s_MD", [128, 4, 8, 128], BF16)

            def v8(ap):
                return ap.rearrange("p (c j) -> p c j", j=8)
            op("pool", lambda e: e.memset(dec[:], 0.0), writes=["dec"])
            op("pool", lambda e: e.memset(H16[0][:, 0:1], 0.0), writes=[("H16", 0)])
            op("pool", lambda e: e.memset(H16[1][:, 0:1], 0.0), writes=[("H16", 1)])
            ybk = [("ybuf", b) for b in range(len(pieces) * 8)]
            pcnt = 0
            pcn = [0]

            def ct_begin(ct):
                rs = slice(ct * 128, (ct + 1) * 128)
                cstf = cstfs[ct % 2]
                dma(u32[:], US[rs, :], writes=["u32"])
                op("act", lambda e: e.copy(out=u16[:], in_=u32[:]), reads=["u32"], writes=["u16"])
                for ri, cs_ in enumerate((ins["s5_c_re"], ins["s5_c_im"])):
                    srcc = cs_[l].rearrange("(t g) c p -> t (g c) p", g=8)[ct]
                    for dup in range(2):
                        dma(cld[:, ri, dup, :], srcc, writes=[("cld", ri, dup)])
                for ri in range(2):
                    pk = "ps%d" % (6 + ri)
                    op("pe", lambda e: e.transpose(out=PS[6 + ri][:, 0:128], in_=cld[:, ri].rearrange("p a b -> p (a b)"),
                                                   identity=ident[:]),
                       reads=[("cld", ri, 0), ("cld", ri, 1), "ident"], writes=[pk])
                    for q in range(4):
                        if ri == 0:
                            tt(cstf[:, q, 0, :], PS[6][:, 0:128], mk[:, q, :], ALU.mult, [pk, "mk"], [("cstf", ct % 2, q)])
                        else:
                            op("dve", lambda e: e.scalar_tensor_tensor(out=cstf[:, q, 1, :], in0=PS[7][:, 0:128], scalar=-1.0,
                                                                       in1=mk[:, q, :], op0=ALU.mult, op1=ALU.mult),
                               reads=[pk, "mk"], writes=[("cstf", ct % 2, q)])

            def stageA(k):
                ct, q = k // 4, k % 4
                Bpk = Bps[k % 2]
                bk = k % 2
                bbr_b = bbr[:, k, :].unsqueeze(1).to_broadcast([128, 8, 16])
                bbi_b = bbi[:, k, :].unsqueeze(1).to_broadcast([128, 8, 16])
                ur_b = ur[:, k, :].unsqueeze(2).to_broadcast([128, 8, 16])
                ui_b = ui[:, k, :].unsqueeze(2).to_broadcast([128, 8, 16])
                tt(rta[:], bbr_b, ur_b, ALU.mult, ["bbr", "tab"], ["rta"])
                tt(rtb[:], bbi_b, ui_b, ALU.mult, ["bbi", "tab"], ["rtb"])
                tt(rot[:, 0], rta[:], rtb[:], ALU.add, ["rta", "rtb"], [("rot", 0)])
                tt(rta[:], bbi_b, ur_b, ALU.mult, ["bbi", "tab", ("rot", 0)], ["rta"])
                tt(rtb[:], bbr_b, ui_b, ALU.mult, ["bbr", "tab", ("rot", 0)], ["rtb"])
                tt(rot[:, 1], rta[:], rtb[:], ALU.subtract, ["rta", "rtb"], [("rot", 1)])
                for ri in range(2):
                    src = rot[:, ri].unsqueeze(2).to_broadcast([128, 8, 8, 16])
                    msk = mk[:, k % 4, :].rearrange("p (a c) -> p a c", a=8).unsqueeze(1).to_broadcast([128, 8, 8, 16])
                    tt(bpre8[:].rearrange("p j (a c) -> p j a c", a=8), src, msk, ALU.mult, [("rot", ri), "mk"], ["bpre8"])
                    for j4 in range(2):
                        pidx = 6 + j4
                        pk = "ps%d" % pidx
                        for jj in range(4):
                            op("pe", lambda e: e.transpose(out=PS[pidx][:, jj * 128:(jj + 1) * 128], in_=bpre8[:, j4 * 4 + jj, :],
                                                           identity=ident[:]), reads=["bpre8", "ident"], writes=[pk])
                        op("act", lambda e: e.copy(out=bstj[:, j4 * 4:(j4 + 1) * 4, ri, :],
                                                   in_=PS[pidx][:].rearrange("p (a b) -> p a b", a=4)),
                           reads=[pk], writes=[("bstj", ri, j4)])
                bstk = [("bstj", ri, j4) for ri in range(2) for j4 in range(2)]
                for j in range(8):
                    for ri in range(2):
                        for (pc0, pn) in pieces:
                            pidx = pcn[0] % 4
                            pcn[0] += 1
                            pk = "ps%d" % pidx
                            op("pe", lambda e: e.matmul(PS[pidx][:, :pn], bstj[:, j, ri, :], v8(u16[:])[:, pc0:pc0 + pn, j],
                                                        start=True, stop=True), reads=bstk + ["u16"], writes=[pk])
                            op("act", lambda e: e.copy(out=v8(Bpk[ri][:])[:, pc0:pc0 + pn, j], in_=PS[pidx][:, :pn]),
                               reads=[pk], writes=[("Bp", bk, ri)])

            def stageB(k):
                ct, q = k // 4, k % 4
                Bpk = Bps[k % 2]
                bk = k % 2
                cstf = cstfs[ct % 2]
                c0b = cstf[:, q, 0, :].unsqueeze(1).to_broadcast([128, 8, 128])
                c1b = cstf[:, q, 1, :].unsqueeze(1).to_broadcast([128, 8, 128])

                def tb_(tab):
                    return tab[:, k, :].unsqueeze(2).to_broadcast([128, 8, 128])
                for i_, (ta, tb2) in enumerate(((ur, ui), (nui, ur), (lpr, lpi), (nlpi, lpr))):
                    op("pool", lambda e: e.tensor_tensor(out=mda[:], in0=c0b, in1=tb_(ta), op=ALU.mult),
                       reads=[("cstf", ct % 2, q), "tab"], writes=["mda"])
                    op("pool", lambda e: e.tensor_tensor(out=mdb[:], in0=c1b, in1=tb_(tb2), op=ALU.mult),
                       reads=[("cstf", ct % 2, q), "tab"], writes=["mdb"])
                    op("pool", lambda e: e.tensor_tensor(out=MD[:, i_], in0=mda[:], in1=mdb[:], op=ALU.add),
                       reads=["mda", "mdb"], writes=[("MD", i_)])
                op("pool", lambda e: e.tensor_scalar(out=dec[:], in0=dec[:], scalar1=0.0, scalar2=mag[:, k:k + 1],
                                                     op0=ALU.mult, op1=ALU.add), reads=["dec", "magk"], writes=["dec"])
                op("pool", lambda e: e.memset(v8(dec[:])[:, :, 0], 0.0), reads=["dec"], writes=["dec"])
                for ri in range(2):
                    op("dve", lambda e: e.tensor_tensor_scan(out=g16[ri][:], data0=dec[:], data1=Bpk[ri][:], initial=0.0,
                                                             op0=ALU.mult, op1=ALU.add),
                       reads=["dec", ("Bp", bk, ri)], writes=[("g16", ri)])
                g7r, g7i = v8(g16[0][:])[:, :, 7], v8(g16[1][:])[:, :, 7]
                G2 = [("g16", 0), ("g16", 1)]
                ts(Z[0][:], g7r, ur[:, k, 7:8], None, ALU.mult, None, G2 + ["tab"], [("Z", 0)])
                op("dve", lambda e: e.scalar_tensor_tensor(out=Z[0][:], in0=g7i, scalar=nui[:, k, 7:8], in1=Z[0][:],
                                                           op0=ALU.mult, op1=ALU.add), reads=G2 + ["tab", ("Z", 0)], writes=[("Z", 0)])
                ts(Z[1][:], g7r, ui[:, k, 7:8], None, ALU.mult, None, G2 + ["tab"], [("Z", 1)])
                op("dve", lambda e: e.scalar_tensor_tensor(out=Z[1][:], in0=g7i, scalar=ur[:, k, 7:8], in1=Z[1][:],
                                                           op0=ALU.mult, op1=ALU.add), reads=G2 + ["tab", ("Z", 1)], writes=[("Z", 1)])
                cur = 0
                for m in range(NLC):
                    d = 1 << m
                    lv = 3 + m
                    sr, si_ = Z[cur * 2], Z[cur * 2 + 1]
                    dr, di = Z[(1 - cur) * 2], Z[(1 - cur) * 2 + 1]
                    ks = [("Z", cur * 2), ("Z", cur * 2 + 1)]
                    kd0, kd1 = ("Z", (1 - cur) * 2), ("Z", (1 - cur) * 2 + 1)
                    op("dve", lambda e: e.scalar_tensor_tensor(out=dr[:, d:], in0=sr[:, :NCH - d], scalar=pwr[:, k, lv:lv + 1],
                                                               in1=sr[:, d:], op0=ALU.mult, op1=ALU.add),
                       reads=ks + ["pw"], writes=[kd0])
                    op("dve", lambda e: e.scalar_tensor_tensor(out=dr[:, d:], in0=si_[:, :NCH - d], scalar=pwn[:, k, lv:lv + 1],
                                                               in1=dr[:, d:], op0=ALU.mult, op1=ALU.add),
                       reads=ks + ["pwn", kd0], writes=[kd0])
                    op("dve", lambda e: e.scalar_tensor_tensor(out=di[:, d:], in0=si_[:, :NCH - d], scalar=pwr[:, k, lv:lv + 1],
                                                               in1=si_[:, d:], op0=ALU.mult, op1=ALU.add),
                       reads=ks + ["pw"], writes=[kd1])
                    op("dve", lambda e: e.scalar_tensor_tensor(out=di[:, d:], in0=sr[:, :NCH - d], scalar=pwi[:, k, lv:lv + 1],
                                                               in1=di[:, d:], op0=ALU.mult, op1=ALU.add),
                       reads=ks + ["pw", kd1], writes=[kd1])
                    op("dve", lambda e: e.tensor_copy(out=dr[:, :d], in_=sr[:, :d]), reads=ks, writes=[kd0])
                    op("dve", lambda e: e.tensor_copy(out=di[:, :d], in_=si_[:, :d]), reads=ks, writes=[kd1])
                    cur = 1 - cur
                for ri in range(2):
                    op("act", lambda e: e.copy(out=H16[ri][:, 1:NCH], in_=Z[cur * 2 + ri][:, 0:NCH - 1]),
                       reads=[("Z", cur * 2 + ri)], writes=[("H16", ri)])
                for j in range(8):
                    for pi_, (pc0, pn) in enumerate(pieces):
                        pidx = 4 + pcn[0] % 2
                        pcn[0] += 1
                        pk = "ps%d" % pidx
                        rhs = [v8(g16[0][:])[:, pc0:pc0 + pn, j], v8(g16[1][:])[:, pc0:pc0 + pn, j],
                               H16[0][:, pc0:pc0 + pn], H16[1][:, pc0:pc0 + pn]]
                        rk = [("g16", 0), ("g16", 1), ("H16", 0), ("H16", 1)]
                        for i_ in range(4):
                            op("pe", lambda e: e.matmul(PS[pidx][:, :pn], MD[:, i_, j, :], rhs[i_], start=(i_ == 0), stop=(i_ == 3)),
                               reads=[("MD", i_), rk[i_]], writes=[pk])
                        yv = v8(ybuf[:])[:, pc0:pc0 + pn, j]
                        yk = ("ybuf", j * len(pieces) + pi_)
                        if q == 0:
                            op("dve", lambda e: e.scalar_tensor_tensor(out=yv, in0=v8(u32[:])[:, pc0:pc0 + pn, j],
                                                                       scalar=dcol[:, ct:ct + 1], in1=PS[pidx][:, :pn],
                                                                       op0=ALU.mult, op1=ALU.add),
                               reads=[pk, "u32", "dcol"], writes=[yk])
                        else:
                            tt(yv, yv, PS[pidx][:, :pn], ALU.add, [pk, yk], [yk])

            def ct_end(ct):
                rs = slice(ct * 128, (ct + 1) * 128)
                op("act", lambda e: e.activation(out=ybuf[:], in_=ybuf[:], func=AF.Gelu), reads=ybk, writes=ybk)
                dma(YG[rs, :], ybuf[:], reads=ybk, writes=[("YG", ct)], q="pool")

            for k in range(32):
                if k % 4 == 0:
                    ct_begin(k // 4)
                stageA(k)
                if k >= 1:
                    stageB(k - 1)
                    if (k - 1) % 4 == 3:
                        ct_end((k - 1) // 4)
            stageB(31)
            ct_end(7)
            cx.barrier()
        with ExitStack() as st:
            bgl = st.enter_context(sbt("g_b", [128, 8], F32))
            yg32 = st.enter_context(sbt("g_y32", [128, 8, 512], F32))
            yg16 = st.enter_context(sbt("g_y16", [128, 8, 512], BF16))
            wg = st.enter_context(sbt("g_w", [128, 8, 1024], BF16))
            sg = st.enter_context(sbt("g_sg", [128, 512], F32))
            yc = [st.enter_context(sbt("g_yc%d" % i, [128, 512], BF16)) for i in range(2)]
            load_col(bgl[:], ins["s5_b_glu"][l], 8, "bgl")
            dma(wg[:], WGLU[l].rearrange("(a p) m -> p a m", p=128), reads=wkeys("WGLU%d" % l, 1024), writes=["wg"])
            oi = 0
            for b in range(NB + 1):
                c0, n = cblk(b)
                dma(yg32[:, :, :n], YG.rearrange("(a p) t -> p a t", p=128)[:, :, c0:c0 + n], writes=["yg32"])
                op("act", lambda e: e.copy(out=yg16[:, :, :n], in_=yg32[:, :, :n]), reads=["yg32"], writes=["yg16"])
                for ft in range(8):
                    pidx = ft % 2
                    pk = "ps%d" % pidx
                    for kc in range(8):
                        op("pe", lambda e: e.matmul(PS[pidx][:, :n], wg[:, kc, ft * 128:(ft + 1) * 128], yg16[:, kc, :n],
                                                    start=(kc == 0), stop=(kc == 7)), reads=["wg", "yg16"], writes=[pk])
                    op("act", lambda e: e.activation(out=sg[:, :n], in_=PS[pidx][:, :n], func=AF.Sigmoid, bias=bgl[:, ft:ft + 1]),
                       reads=[pk, "bgl"], writes=["sg"])
                    y_ = yc[oi % 2]
                    yk = ("yc", oi % 2)
                    oi += 1
                    op("dve", lambda e: e.tensor_tensor(out=y_[:, :n], in0=sg[:, :n], in1=yg32[:, ft, :n], op=ALU.mult),
                       reads=["sg", "yg32"], writes=[yk])
                    dma(YS[2048 + ft * 128:2048 + (ft + 1) * 128, c0:c0 + n], y_[:, :n], reads=[yk], writes=[("YS", ft, b)], q="pool")
            cx.barrier()

    def phase_branch(l):
        with ExitStack() as st:
            hX = st.enter_context(sbt("b_hX", [128, 16, 1024], BF16))
            yX = st.enter_context(sbt("b_yX", [128, 24, 1024], BF16))
            wt = [st.enter_context(sbt("b_wt%d" % i, [128, 24, 128], BF16)) for i in range(2)]
            wg = [st.enter_context(sbt("b_wg%d" % i, [128, 3, 16, 128], BF16)) for i in range(2)]
            sg = [st.enter_context(sbt("b_sg%d" % i, [128, 512], F32)) for i in range(3)]
            m0 = st.enter_context(sbt("b_m0", [128, 512], F32))
            m1 = st.enter_context(sbt("b_m1", [128, 512], F32))
            mg = [st.enter_context(sbt("b_mg%d" % i, [128, 512], BF16)) for i in range(2)]
            bg = st.enter_context(sbt("b_bg", [128, 48], F32))
            load_col(bg[:], ins["b_gate"][l].rearrange("a b -> (a b)"), 48, "bg")
            Wv = WIN[l].rearrange("(a p) m -> p a m", p=128)
            it = 0
            pc = 0
            mi = 0
            for sb in superblocks(2):
                c0 = cblk(sb[0])[0]
                ncol = sum(cblk(b)[1] for b in sb)
                dma(hX[:, :, :ncol], HF.rearrange("(a p) t -> p a t", p=128)[:, :, c0:c0 + ncol], writes=["hX"])
                dma(yX[:, :, :ncol], YS.rearrange("(a p) t -> p a t", p=128)[:, :, c0:c0 + ncol], writes=["yX"])
                for ft in range(16):
                    s = it % 2
                    it += 1
                    dma(wt[s][:], WBR[l].rearrange("(a p) m -> p a m", p=128)[:, :, ft * 128:(ft + 1) * 128],
                        reads=wkeys("WBR%d" % l, 3072), writes=[("wt", s)])
                    for br in range(3):
                        g0 = 6144 + br * 2048 + ft * 128
                        dma(wg[s][:, br], Wv[:, :, g0:g0 + 128], reads=wkeys("WIN%d" % l, D), writes=[("wg", s, br)])
                    off = 0
                    for b in sb:
                        n = cblk(b)[1]
                        for br in range(3):
                            pg, pb = pc % 8, (pc + 1) % 8
                            pc += 2
                            for kc in range(16):
                                op("pe", lambda e: e.matmul(PS[pg][:, :n], wg[s][:, br, kc, :], hX[:, kc, off:off + n],
                                                            start=(kc == 0), stop=(kc == 15)),
                                   reads=[("wg", s, br), "hX"], writes=["ps%d" % pg])
                            for kc in range(8):
                                op("pe", lambda e: e.matmul(PS[pb][:, :n], wt[s][:, br * 8 + kc, :], yX[:, br * 8 + kc, off:off + n],
                                                            start=(kc == 0), stop=(kc == 7)),
                                   reads=[("wt", s), "yX"], writes=["ps%d" % pb])
                            r = br * 16 + ft
                            if "noepi" in dbg:
                                continue
                            op("act", lambda e: e.activation(out=sg[br][:, :n], in_=PS[pg][:, :n], func=AF.Sigmoid,
                                                             bias=bg[:, r:r + 1]), reads=["ps%d" % pg, "bg"], writes=[("sg", br)])
                            if "nodve" in dbg:
                                continue
                            if br == 0:
                                op("dve", lambda e: e.tensor_tensor(out=m0[:, :n], in0=PS[pb][:, :n], in1=sg[br][:, :n], op=ALU.mult),
                                   reads=["ps%d" % pb, ("sg", br)], writes=["m0"])
                            else:
                                op("dve", lambda e: e.tensor_tensor(out=m1[:, :n], in0=PS[pb][:, :n], in1=sg[br][:, :n], op=ALU.mult),
                                   reads=["ps%d" % pb, ("sg", br)], writes=["m1"])
                                if br == 1:
                                    op("dve", lambda e: e.tensor_tensor(out=m0[:, :n], in0=m0[:, :n], in1=m1[:, :n], op=ALU.add),
                                       reads=["m0", "m1"], writes=["m0"])
                        ms = mi % 2
                        mi += 1
                        if "noepi" in dbg or "nodve" in dbg:
                            off += n
                            continue
                        op("dve", lambda e: e.tensor_tensor(out=mg[ms][:, :n], in0=m0[:, :n], in1=m1[:, :n], op=ALU.add),
                           reads=["m0", "m1"], writes=[("mg", ms)])
                        if "nomgdma" not in dbg:
                            dma(MG[ft * 128:(ft + 1) * 128, c0 + off:c0 + off + n], mg[ms][:, :n], reads=[("mg", ms)],
                                writes=[("MG", ft, b)], q="pool")
                        off += n
            cx.barrier()

    def tm_epilogue(P, mixsrc, mixkeys, j, nwb, nwcol_next, xt, xk, last_out):
        r0, n = tmt(j)
        sq, ss, tmp = P["esq"], P["ss2"], P["esq"]
        dma(xt[:n, :], XS[r0:r0 + n, :], reads=[("XS", j)], writes=[xk], q="pool")
        op("act", lambda e: e.activation(out=sq[:n, :], in_=mixsrc[:n, :], func=AF.Square), reads=mixkeys, writes=["esq"])
        op("dve", lambda e: e.tensor_reduce(out=ss[:n, 0:1], in_=sq[:n, :], axis=AX.X, op=ALU.add), reads=["esq"], writes=["e0"])
        op("dve", lambda e: e.tensor_scalar(out=ss[:n, 1:2], in0=ss[:n, 0:1], scalar1=1.0 / D, scalar2=1e-6,
                                            op0=ALU.mult, op1=ALU.add), reads=["e0"], writes=["e1"])
        op("act", lambda e: e.sqrt(out=ss[:n, 2:3], in_=ss[:n, 1:2]), reads=["e1"], writes=["e2"])
        op("dve", lambda e: e.reciprocal(out=ss[:n, 3:4], in_=ss[:n, 2:3]), reads=["e2"], writes=["e3"])
        op("dve", lambda e: e.tensor_scalar(out=tmp[:n, :], in0=mixsrc[:n, :], scalar1=ss[:n, 3:4], scalar2=None, op0=ALU.mult),
           reads=mixkeys + ["e3", "esq"], writes=["esq"])
        op("pool", lambda e: e.tensor_tensor(out=tmp[:n, :], in0=tmp[:n, :], in1=nwb[:n, :], op=ALU.mult),
           reads=["esq", "nwb"], writes=["esq"])
        op("pool", lambda e: e.tensor_tensor(out=xt[:n, :], in0=xt[:n, :], in1=tmp[:n, :], op=ALU.add),
           reads=["esq", xk], writes=[xk])
        if last_out:
            if j >= 1:
                dma(out[r0 - 16:r0 - 16 + n, :], xt[:n, :], reads=[xk], writes=[("OUT", j)], q="pool")
        else:
            dma(XS[r0:r0 + n, :], xt[:n, :], reads=[xk], writes=[("XS", j)], q="pool")
            if nwcol_next is not None:
                return lambda: norm_tile_to_hf(P, xt, n, r0, nwcol_next, xk)
        return None

    def epi_bufs(st):
        P = norm_bufs(st)
        P["ss2"] = st.enter_context(sbt("e_ss", [128, 4], F32))
        P["esq"] = st.enter_context(sbt("e_sq", [128, D], F32))
        return P

    def load_bcast_row(dst, src_row, key):
        dma(dst, bass.AP(src_row.tensor, src_row.offset, [[0, 128], [1, D]]), writes=[key])

    def phase_wout(l):
        with ExitStack() as st:
            P = epi_bufs(st)
            wo = st.enter_context(sbt("o_w", [128, 16, 1024], BF16))
            mT = [st.enter_context(sbt("o_m%d" % i, [128, 16, 128], BF16)) for i in range(2)]
            mixs = [st.enter_context(sbt("o_mix%d" % i, [128, D], F32)) for i in range(2)]
            nwb = st.enter_context(sbt("o_nwb", [128, D], F32))
            nwc = st.enter_context(sbt("o_nwc", [128, 16], F32))
            xt = [st.enter_context(sbt("o_xt%d" % i, [128, D], F32)) for i in range(2)]
            load_bcast_row(nwb[:], ins["norm_w"][l, 1], "nwb")
            load_col(nwc[:], ins["norm_w"][l, 2], 16, "nwcol")
            for half in range(2):
                dma(wo[:], WOUT[l].rearrange("(a p) m -> p a m", p=128)[:, :, half * 1024:(half + 1) * 1024],
                    reads=wkeys("WOUT%d" % l, D), writes=["wo"])
                break
            wo2 = st.enter_context(sbt("o_w2", [128, 16, 1024], BF16))
            dma(wo2[:], WOUT[l].rearrange("(a p) m -> p a m", p=128)[:, :, 1024:2048],
                reads=wkeys("WOUT%d" % l, D), writes=["wo2"])
            pend = None
            for j in range(NTM):
                r0, n = tmt(j)
                s = j % 2
                dma(mT[s][:, :, :n], MG.rearrange("(a p) t -> p a t", p=128)[:, :, r0:r0 + n], writes=[("mT", s)])
                for fb in range(4):
                    wsrc, wk = (wo, "wo") if fb < 2 else (wo2, "wo2")
                    pidx = fb
                    pk = "ps%d" % pidx
                    for kc in range(16):
                        op("pe", lambda e: e.matmul(PS[pidx][:n, :], mT[s][:, kc, :n], wsrc[:, kc, (fb % 2) * 512:(fb % 2 + 1) * 512],
                                                    start=(kc == 0), stop=(kc == 15)), reads=[("mT", s), wk], writes=[pk])
                    op("act", lambda e: e.copy(out=mixs[s][:n, fb * 512:(fb + 1) * 512], in_=PS[pidx][:n, :]),
                       reads=[pk], writes=[("mix", s, fb)])
                nxt = tm_epilogue(P, mixs[s], [("mix", s, fb) for fb in range(4)], j, nwb, nwc, xt[s], ("xt", s), False)
                if pend is not None:
                    pend()
                pend = nxt
            if pend is not None:
                pend()
            cx.barrier()

    def phase_ffn(l, last):
        with ExitStack() as st:
            P = epi_bufs(st)
            hX = st.enter_context(sbt("f_hX", [128, 16, 512], BF16))
            act16 = st.enter_context(sbt("f_act", [128, 44, 512], BF16))
            w1 = [st.enter_context(sbt("f_w1%d" % i, [128, 2, 16, 128], BF16)) for i in range(2)]
            w2 = [st.enter_context(sbt("f_w2%d" % i, [128, 44, 256], BF16)) for i in range(2)]
            sl = st.enter_context(sbt("f_sl", [128, 512], F32))
            mix = [st.enter_context(sbt("f_mix%d" % i, [128, D], F32)) for i in range(4)]
            nwb = st.enter_context(sbt("f_nwb", [128, D], F32))
            nwc = st.enter_context(sbt("f_nwc", [128, 16], F32))
            xt = st.enter_context(sbt("f_xt", [128, D], F32))
            load_bcast_row(nwb[:], ins["norm_w"][l, 3], "nwb")
            if not last:
                load_col(nwc[:], ins["norm_w"][l + 1, 0], 16, "nwcol")
            W1v = WF1[l].rearrange("(a p) m -> p a m", p=128)
            W2v = WF2[l].rearrange("(a p) m -> p a m", p=128)
            i1 = 0
            i2 = 0
            for b in range(NB + 1):
                c0, n = cblk(b)
                dma(hX[:, :, :n], HF.rearrange("(a p) t -> p a t", p=128)[:, :, c0:c0 + n], reads=["HFall"], writes=["hX"])
                for ft in range(44):
                    s = i1 % 2
                    i1 += 1
                    dma(w1[s][:, 0], W1v[:, :, ft * 128:(ft + 1) * 128], reads=wkeys("WF1%d" % l, D), writes=[("w1", s, 0)])
                    dma(w1[s][:, 1], W1v[:, :, DFF + ft * 128:DFF + (ft + 1) * 128], reads=wkeys("WF1%d" % l, D),
                        writes=[("w1", s, 1)])
                    pg, pu = (ft % 2) * 2, (ft % 2) * 2 + 1
                    for gu, pidx in ((0, pg), (1, pu)):
                        for kc in range(16):
                            op("pe", lambda e: e.matmul(PS[pidx][:, :n], w1[s][:, gu, kc, :], hX[:, kc, :n],
                                                        start=(kc == 0), stop=(kc == 15)),
                               reads=[("w1", s, gu), "hX"], writes=["ps%d" % pidx])
                    op("act", lambda e: e.activation(out=sl[:, :n], in_=PS[pg][:, :n], func=AF.Silu),
                       reads=["ps%d" % pg], writes=["sl"])
                    op("dve", lambda e: e.tensor_tensor(out=act16[:, ft, :n], in0=sl[:, :n], in1=PS[pu][:, :n], op=ALU.mult),
                       reads=["sl", "ps%d" % pu], writes=[("act", ft)])
                ntile = 1 if b == 0 else 4
                for fq in range(8):
                    s = i2 % 2
                    i2 += 1
                    dma(w2[s][:], W2v[:, :, fq * 256:(fq + 1) * 256], reads=wkeys("WF2%d" % l, DFF), writes=[("w2", s)])
                    for tt_ in range(ntile):
                        nt = 16 if b == 0 else 128
                        pidx = 4 + (fq * ntile + tt_) % 4
                        pk = "ps%d" % pidx
                        for kc in range(44):
                            op("pe", lambda e: e.matmul(PS[pidx][:nt, :256], act16[:, kc, tt_ * 128:tt_ * 128 + nt], w2[s][:, kc, :],
                                                        start=(kc == 0), stop=(kc == 43)),
                               reads=[("act", kc), ("w2", s)], writes=[pk])
                        op("act", lambda e: e.copy(out=mix[tt_][:nt, fq * 256:(fq + 1) * 256], in_=PS[pidx][:nt, :256]),
                           reads=[pk], writes=[("mix", tt_, fq)])
                for tt_ in range(ntile):
                    j = 0 if b == 0 else 1 + 4 * (b - 1) + tt_
                    nxt = tm_epilogue(P, mix[tt_], [("mix", tt_, fq) for fq in range(8)], j, nwb, None if last else nwc,
                                      xt, "xt", last)
                    if nxt is not None:
                        nxt()
            cx.barrier()

    stop_after = None
    for d_ in dbg:
        if d_.startswith("stop:"):
            stop_after = d_[5:]
    for l in range(depth):
        last = l == depth - 1
        if stop_after in ("init", "table"):
            break
        if l == 0:
            phase_norm(l, 0)
            conv_group(0, ["MIX"])
        if stop_after == "norm":
            break
        phase_win(l)
        if l == 0:
            conv_group(0, ["WOUT", "WF1", "WF2"])
        if stop_after == "win":
            break
        phase_lru(l)
        conv_group(l + 1, ["WIN"])
        if stop_after == "lru":
            break
        phase_attn(l)
        conv_group(l + 1, ["MIX"])
        if stop_after == "attn":
            break
        phase_s5(l)
        conv_group(l + 1, ["WOUT", "WF1"])
        if stop_after == "s5":
            break
        phase_branch(l)
        conv_group(l + 1, ["WF2"])
        if stop_after == "branch":
            break
        phase_wout(l)
        if stop_after == "wout":
            break
        phase_ffn(l, last)
    cx.barrier(final=True)
    return nc, hc


_CACHE = {}


def kernel(**inputs):
    x = np.ascontiguousarray(np.asarray(inputs["x"], dtype=np.float32))
    B = x.shape[0]
    if "nc" not in _CACHE:
        _CACHE["nc"] = build(NB=x.shape[1] // 512)
    nc, hc = _CACHE["nc"]
    base = {k: np.ascontiguousarray(np.asarray(inputs[k], dtype=np.float32)) for k in PARAM_SHAPES}
    for k, v in hc.items():
        base["c_" + k] = v
    in_maps = []
    for c in range(8):
        m = dict(base)
        m["x"] = x[c % B]
        in_maps.append(m)
    res = run_bass_kernel_spmd(nc, in_maps, core_ids=list(range(8)))
    return np.stack([res.results[b]["out"] for b in range(B)], axis=0).astype(np.float32)
```
